# Optimizing a Trainium2 kernel written in Bass

```python
import math
import jax, jax.numpy as jnp
from jax import lax
import numpy as np

D_MODEL = 2048
BATCH = 4
SEQ = 2048
DEPTH = 4

CHUNK = 64
D_MIX = D_MODEL
D_SSM = D_MIX // 2
SSM_GROUP = 16
N_SSM_GROUPS = D_SSM // SSM_GROUP
SSM_STATE = 64
N_DN_HEADS = 8
DN_HEAD_DIM = (D_MIX - D_SSM) // N_DN_HEADS
D_DN = N_DN_HEADS * DN_HEAD_DIM
CONV_WIDTH = 4
N_IN = D_SSM + 4 * D_DN + 2 * N_DN_HEADS
PEER_HEADS = 8
PEER_NKEYS = 128
PEER_EXPERTS = PEER_NKEYS * PEER_NKEYS
PEER_QDIM = 256
PEER_HALF = PEER_QDIM // 2
PEER_TOPK = 16
PEER_TOKEN_BLOCK = 128
DEEPNORM_ALPHA = (2.0 * DEPTH) ** 0.25
DEEPNORM_BETA = (8.0 * DEPTH) ** -0.25
LN_EPS = 1e-5
NORM_EPS = 1e-6

kernel_name = "hymba_s5_gdn_peer_deepnorm_trunk"


def layer_norm(x):
    xf = x.astype(jnp.float32)
    mu = xf.mean(-1, keepdims=True)
    var = jnp.square(xf - mu).mean(-1, keepdims=True)
    return ((xf - mu) * lax.rsqrt(var + LN_EPS)).astype(x.dtype)


def layer_norm_affine(x, gain, bias):
    xf = x.astype(jnp.float32)
    mu = xf.mean(-1, keepdims=True)
    var = jnp.square(xf - mu).mean(-1, keepdims=True)
    y = (xf - mu) * lax.rsqrt(var + LN_EPS) * gain.astype(jnp.float32) + bias.astype(jnp.float32)
    return y.astype(x.dtype)


def ada_modulate(x, shift, scale):
    return layer_norm(x) * (1.0 + scale) + shift


def l2norm(t):
    return t * lax.rsqrt(jnp.sum(t * t, axis=-1, keepdims=True) + NORM_EPS)


def s5_mixer(u, lam_re, lam_im, log_step, b_re, b_im, c_re, c_im, d_skip, w_glu):
    f32 = jnp.float32
    bsz, seq, _ = u.shape
    uf = u.astype(f32).reshape(bsz, seq, N_SSM_GROUPS, SSM_GROUP)
    lam = lax.complex(lam_re.astype(f32), lam_im.astype(f32))
    step = jnp.exp(log_step.astype(f32))[:, None]
    lam_bar = jnp.exp(lam * step)
    b = lax.complex(b_re.astype(f32), b_im.astype(f32))
    b_bar = ((lam_bar - 1.0) / lam)[..., None] * b
    bu = jnp.einsum('gpc,bsgc->bsgp', b_bar, uf.astype(jnp.complex64))
    a = jnp.broadcast_to(lam_bar, (1, seq, N_SSM_GROUPS, SSM_STATE))

    def combine(left, right):
        a_l, b_l = left
        a_r, b_r = right
        return a_l * a_r, a_r * b_l + b_r

    _, states = lax.associative_scan(combine, (a, bu), axis=1)
    cmat = lax.complex(c_re.astype(f32), c_im.astype(f32))
    y = jnp.einsum('gcp,bsgp->bsgc', cmat, states).real
    y = y + d_skip.astype(f32).reshape(N_SSM_GROUPS, SSM_GROUP) * uf
    y = jax.nn.gelu(y.reshape(bsz, seq, D_SSM))
    ab = y @ w_glu.astype(f32)
    out = ab[..., :D_SSM] * jax.nn.sigmoid(ab[..., D_SSM:])
    return out.astype(u.dtype)


def causal_depthwise_conv(x, w):
    ch = x.shape[-1]
    return lax.conv_general_dilated(
        x, w[:, None, :].astype(x.dtype), window_strides=(1,),
        padding=[(CONV_WIDTH - 1, 0)], dimension_numbers=('NWC', 'WIO', 'NWC'),
        feature_group_count=ch)


def gated_delta_chunked(q, k, v, g, beta):
    f32 = jnp.float32
    bsz, seq, nh, dk = q.shape
    dv = v.shape[-1]
    nc = seq // CHUNK
    q = q * (dk ** -0.5)

    def to_chunks(t):
        t = t.reshape(bsz, nc, CHUNK, nh, *t.shape[3:])
        return jnp.moveaxis(t, 3, 1)

    qc, kc, vc = to_chunks(q), to_chunks(k), to_chunks(v)
    gc, bc = to_chunks(g), to_chunks(beta)
    gcum = jnp.cumsum(gc, axis=-1)
    tril = jnp.tril(jnp.ones((CHUNK, CHUNK), dtype=bool))
    strict = jnp.tril(jnp.ones((CHUNK, CHUNK), dtype=bool), -1)
    diff = gcum[..., :, None] - gcum[..., None, :]
    decay = jnp.where(tril, jnp.exp(jnp.where(tril, diff, 0.0)), 0.0)
    kb = kc * bc[..., None]
    vb = vc * bc[..., None]
    lmat = jnp.where(strict, jnp.einsum('bhncd,bhnsd->bhncs', kb, kc) * decay, 0.0)
    eye = jnp.eye(CHUNK, dtype=f32)
    tmat = lax.linalg.triangular_solve(eye + lmat, jnp.broadcast_to(eye, lmat.shape),
                                       left_side=True, lower=True)
    w_val = tmat @ vb
    k_cum = tmat @ (kb * jnp.exp(gcum)[..., None])
    attn_intra = jnp.where(tril, jnp.einsum('bhncd,bhnsd->bhncs', qc, kc) * decay, 0.0)
    g_last = gcum[..., -1]
    k_tail = kc * jnp.exp(g_last[..., None] - gcum)[..., None]
    q_dec = qc * jnp.exp(gcum)[..., None]

    def step(state, inp):
        q_d, k_cd, w_v, a_in, k_t, gl = inp
        v_new = w_v - k_cd @ state
        out = q_d @ state + a_in @ v_new
        state = state * jnp.exp(gl)[..., None, None] + jnp.swapaxes(k_t, -1, -2) @ v_new
        return state, out

    xs = tuple(jnp.moveaxis(t, 2, 0) for t in (q_dec, k_cum, w_val, attn_intra, k_tail, g_last))
    s0 = jnp.zeros((bsz, nh, dk, dv), f32)
    _, out = lax.scan(step, s0, xs)
    return jnp.transpose(out, (1, 0, 3, 2, 4)).reshape(bsz, seq, nh, dv)


def deltanet_mixer(qkv, z, b_in, a_in, conv_w, a_log, dt_bias, norm_w):
    f32 = jnp.float32
    bsz, seq, _ = qkv.shape
    qkv = jax.nn.silu(causal_depthwise_conv(qkv, conv_w)).astype(f32)
    q, k, v = jnp.split(qkv, 3, axis=-1)
    q = l2norm(q.reshape(bsz, seq, N_DN_HEADS, DN_HEAD_DIM))
    k = l2norm(k.reshape(bsz, seq, N_DN_HEADS, DN_HEAD_DIM))
    v = v.reshape(bsz, seq, N_DN_HEADS, DN_HEAD_DIM)
    beta = jax.nn.sigmoid(b_in.astype(f32))
    g = -jnp.exp(a_log.astype(f32)) * jax.nn.softplus(a_in.astype(f32) + dt_bias.astype(f32))
    o = gated_delta_chunked(q, k, v, g, beta)
    o = o * lax.rsqrt(jnp.mean(o * o, axis=-1, keepdims=True) + NORM_EPS) * norm_w.astype(f32)
    o = o * jax.nn.silu(z.astype(f32).reshape(bsz, seq, N_DN_HEADS, DN_HEAD_DIM))
    return o.reshape(bsz, seq, D_DN).astype(z.dtype)


def peer_ffn(h, w_query, sub_keys, u_table, v_table):
    f32 = jnp.float32
    bsz, seq, d = h.shape
    q = (h @ w_query).astype(f32).reshape(bsz, seq, PEER_HEADS, 2, PEER_HALF)
    scores = jnp.einsum('bshid,hind->bshin', q, sub_keys.astype(f32))
    top_s, top_i = lax.top_k(scores, PEER_TOPK)
    cand_s = top_s[..., 0, :, None] + top_s[..., 1, None, :]
    cand_i = top_i[..., 0, :, None] * PEER_NKEYS + top_i[..., 1, None, :]
    cand_s = cand_s.reshape(bsz, seq, PEER_HEADS, PEER_TOPK * PEER_TOPK)
    cand_i = cand_i.reshape(bsz, seq, PEER_HEADS, PEER_TOPK * PEER_TOPK)
    best_s, best_pos = lax.top_k(cand_s, PEER_TOPK)
    expert_idx = jnp.take_along_axis(cand_i, best_pos, axis=-1)
    gates = jax.nn.softmax(best_s, axis=-1)
    nblk = (bsz * seq) // PEER_TOKEN_BLOCK
    hb = h.reshape(nblk, PEER_TOKEN_BLOCK, d)
    ib = expert_idx.reshape(nblk, PEER_TOKEN_BLOCK, PEER_HEADS * PEER_TOPK)
    gb = gates.reshape(nblk, PEER_TOKEN_BLOCK, PEER_HEADS * PEER_TOPK)

    def block(args):
        hx, idx, gt = args
        u_sel = u_table[idx]
        v_sel = v_table[idx]
        act = jax.nn.gelu(jnp.einsum('tkd,td->tk', u_sel, hx).astype(f32)) * gt
        return jnp.einsum('tk,tkd->td', act.astype(v_sel.dtype), v_sel)

    out = lax.map(block, (hb, ib, gb))
    return out.reshape(bsz, seq, d).astype(h.dtype)


def setup_inputs(seed: int = 0) -> dict:
    key = jax.random.key(seed)
    ks = jax.random.split(key, 32)
    f32 = jnp.float32
    L, G, P = DEPTH, N_SSM_GROUPS, SSM_STATE

    def nrm(k, shape, scale):
        return jax.random.normal(k, shape, f32) * scale

    dt = jnp.exp(jax.random.uniform(ks[16], (L, N_DN_HEADS), f32, math.log(1e-3), math.log(1e-1)))
    return {
        "x": nrm(ks[0], (BATCH, SEQ, D_MODEL), 1.0),
        "c": nrm(ks[1], (BATCH, D_MODEL), 1.0),
        "w_ada": nrm(ks[2], (L, D_MODEL, 6 * D_MODEL), 0.5 * D_MODEL ** -0.5),
        "b_ada": nrm(ks[3], (L, 6 * D_MODEL), 0.02),
        "w_in": nrm(ks[4], (L, D_MODEL, N_IN), D_MODEL ** -0.5),
        "ssm_lam_re": -0.5 + nrm(ks[5], (L, G, P), 0.01),
        "ssm_lam_im": math.pi * jnp.arange(P, dtype=f32) + nrm(ks[6], (L, G, P), 0.01),
        "ssm_log_step": jax.random.uniform(ks[7], (L, G), f32, math.log(1e-3), math.log(1e-1)),
        "ssm_b_re": nrm(ks[8], (L, G, P, SSM_GROUP), (2 * SSM_GROUP) ** -0.5),
        "ssm_b_im": nrm(ks[9], (L, G, P, SSM_GROUP), (2 * SSM_GROUP) ** -0.5),
        "ssm_c_re": nrm(ks[10], (L, G, SSM_GROUP, P), (2 * P) ** -0.5),
        "ssm_c_im": nrm(ks[11], (L, G, SSM_GROUP, P), (2 * P) ** -0.5),
        "ssm_d": nrm(ks[12], (L, D_SSM), 1.0),
        "ssm_w_glu": nrm(ks[13], (L, D_SSM, 2 * D_SSM), D_SSM ** -0.5),
        "dn_conv_w": nrm(ks[14], (L, CONV_WIDTH, 3 * D_DN), CONV_WIDTH ** -0.5),
        "dn_a_log": jnp.log(jax.random.uniform(ks[15], (L, N_DN_HEADS), f32, 1.0, 16.0)),
        "dn_dt_bias": dt + jnp.log(-jnp.expm1(-dt)),
        "dn_norm_w": 1.0 + nrm(ks[17], (L, DN_HEAD_DIM), 0.02),
        "w_out": nrm(ks[18], (L, D_MIX, D_MODEL), D_MIX ** -0.5 * DEEPNORM_BETA),
        "ln1_g": 1.0 + nrm(ks[19], (L, D_MODEL), 0.02),
        "ln1_b": nrm(ks[20], (L, D_MODEL), 0.02),
        "peer_w_query": nrm(ks[21], (L, D_MODEL, PEER_HEADS * PEER_QDIM), D_MODEL ** -0.5),
        "peer_sub_keys": nrm(ks[22], (L, PEER_HEADS, 2, PEER_NKEYS, PEER_HALF), PEER_HALF ** -0.5),
        "peer_u": nrm(ks[23], (L, PEER_EXPERTS, D_MODEL), D_MODEL ** -0.5),
        "peer_v": nrm(ks[24], (L, PEER_EXPERTS, D_MODEL), PEER_HEADS ** -0.5 * DEEPNORM_BETA),
        "ln2_g": 1.0 + nrm(ks[25], (L, D_MODEL), 0.02),
        "ln2_b": nrm(ks[26], (L, D_MODEL), 0.02),
    }


def reference(x, c, w_ada, b_ada, w_in, ssm_lam_re, ssm_lam_im, ssm_log_step, ssm_b_re,
              ssm_b_im, ssm_c_re, ssm_c_im, ssm_d, ssm_w_glu, dn_conv_w, dn_a_log,
              dn_dt_bias, dn_norm_w, w_out, ln1_g, ln1_b, peer_w_query, peer_sub_keys,
              peer_u, peer_v, ln2_g, ln2_b):
    split_at = [D_SSM, D_SSM + 3 * D_DN, D_SSM + 4 * D_DN, D_SSM + 4 * D_DN + N_DN_HEADS]
    c_act = jax.nn.silu(c)
    for l in range(DEPTH):
        mod = c_act @ w_ada[l] + b_ada[l]
        sh1, sc1, gt1, sh2, sc2, gt2 = jnp.split(mod[:, None, :], 6, axis=-1)

        hmix = ada_modulate(x, sh1, sc1)
        proj = hmix @ w_in[l]
        u_ssm, qkv, z, b_in, a_in = jnp.split(proj, split_at, axis=-1)
        y_ssm = s5_mixer(u_ssm, ssm_lam_re[l], ssm_lam_im[l], ssm_log_step[l], ssm_b_re[l],
                         ssm_b_im[l], ssm_c_re[l], ssm_c_im[l], ssm_d[l], ssm_w_glu[l])
        y_dn = deltanet_mixer(qkv, z, b_in, a_in, dn_conv_w[l], dn_a_log[l], dn_dt_bias[l],
                              dn_norm_w[l])
        y = jnp.concatenate([y_ssm.astype(x.dtype), y_dn.astype(x.dtype)], axis=-1) @ w_out[l]
        x = layer_norm_affine(DEEPNORM_ALPHA * x + gt1 * y, ln1_g[l], ln1_b[l])

        hffn = ada_modulate(x, sh2, sc2)
        y = peer_ffn(hffn, peer_w_query[l], peer_sub_keys[l], peer_u[l], peer_v[l])
        x = layer_norm_affine(DEEPNORM_ALPHA * x + gt2 * y, ln2_g[l], ln2_b[l])
    return x
```

```python
import math
import os as _os
import numpy as np
from contextlib import ExitStack
import concourse.bass as bass
import concourse.mybir as mybir
from concourse.bass_utils import run_bass_kernel_spmd

F32 = mybir.dt.float32
BF16 = mybir.dt.bfloat16
AF = mybir.ActivationFunctionType
ALU = mybir.AluOpType

D = 2048
S = 2048
DEPTH = 4
NCH = 16
N_IN = 5136
ALPHA = (2.0 * DEPTH) ** 0.25
LN_EPS = 1e-5

ENGS = ['pe', 'act', 'dve', 'pool', 'sp']
NDS = 8


class Prog:
    def __init__(self, nc, es):
        self.nc = nc
        self.es = es
        self.ops = {e: [] for e in ENGS}
        self.sem = {e: es.enter_context(nc.semaphore('s_' + e)) for e in ENGS}
        self.cnt = {e: 0 for e in ENGS}
        self.seen = {e: {} for e in ENGS}
        self.res = {}
        self.dsem = {e: [es.enter_context(nc.semaphore('d_%s%d' % (e, i))) for i in range(NDS)]
                     for e in ('sp', 'pool', 'act')}
        self.dcnt = {e: [0] * NDS for e in ('sp', 'pool', 'act')}
        self.dnext = {e: 0 for e in ('sp', 'pool', 'act')}
        self.n_ins = 0

    def _deps(self, eng, reads, writes):
        toks = []
        for r in reads:
            st = self.res.get(r)
            if st is not None and st['w'] is not None:
                toks.append(st['w'])
        for w in writes:
            st = self.res.get(w)
            if st is not None:
                if st['w'] is not None:
                    toks.append(st['w'])
                toks.extend(st['r'].values())
        waits = []
        for (key, s, v, e) in toks:
            if e == 'pe' and eng == 'pe':
                continue
            if self.seen[eng].get(key, 0) >= v:
                continue
            self.seen[eng][key] = v
            waits.append((s, v))
        return waits

    def _update(self, tok, reads, writes):
        for r in reads:
            st = self.res.setdefault(r, {'w': None, 'r': {}})
            st['r'][tok[0]] = tok
        for w in writes:
            self.res[w] = {'w': tok, 'r': {}}

    def op(self, eng, fn, reads=(), writes=()):
        waits = self._deps(eng, reads, writes)
        self.cnt[eng] += 1
        mysem = self.sem[eng]
        tok = ('e_' + eng, mysem, self.cnt[eng], eng)

        def run(eo, waits=waits, fn=fn, mysem=mysem):
            for (s, v) in waits:
                eo.wait_ge(s, v)
            ins = fn(eo)
            ins.then_inc(mysem, 1)
        self.ops[eng].append(run)
        self._update(tok, reads, writes)
        self.n_ins += 1
        return tok

    def dma(self, q, out, in_, reads=(), writes=(), **kw):
        waits = self._deps(q, reads, writes)
        j = self.dnext[q]
        self.dnext[q] = (j + 1) % NDS
        s = self.dsem[q][j]
        prev = self.dcnt[q][j]
        key = 'd_%s%d' % (q, j)
        if prev > 0 and self.seen[q].get(key, 0) < prev:
            waits.append((s, prev))
            self.seen[q][key] = prev
        self.dcnt[q][j] = prev + 16
        tok = (key, s, prev + 16, None)

        def run(eo, waits=waits, s=s, out=out, in_=in_, kw=kw):
            for (ws, v) in waits:
                eo.wait_ge(ws, v)
            eo.dma_start(out=out, in_=in_, **kw).then_inc(s, 16)
        self.ops[q].append(run)
        self._update(tok, reads, writes)
        self.n_ins += 1
        return tok

    def barrier(self):
        allw = []
        for x in ENGS:
            if self.cnt[x] > 0:
                allw.append(('e_' + x, self.sem[x], self.cnt[x], x))
        for q in self.dsem:
            for j in range(NDS):
                if self.dcnt[q][j] > 0:
                    allw.append(('d_%s%d' % (q, j), self.dsem[q][j], self.dcnt[q][j], None))
        for eng in ENGS:
            waits = []
            for (key, s, v, x) in allw:
                if x == eng and eng in ('pe', 'sp'):
                    continue
                if self.seen[eng].get(key, 0) >= v:
                    continue
                self.seen[eng][key] = v
                waits.append((s, v))

            def run(eo, waits=waits):
                for (s, v) in waits:
                    eo.wait_ge(s, v)
            self.ops[eng].append(run)
        self.res = {}

    def wait_all(self, eng, keys):
        waits = self._deps(eng, keys, ())

        def run(eo, waits=waits):
            for (s, v) in waits:
                eo.wait_ge(s, v)
        self.ops[eng].append(run)

    def finish(self):
        nc = self.nc
        with nc.Block() as block:
            @block.tensor
            def _(e):
                for f in self.ops['pe']:
                    f(e)

            @block.scalar
            def _(e):
                for f in self.ops['act']:
                    f(e)

            @block.vector
            def _(e):
                for f in self.ops['dve']:
                    f(e)

            @block.gpsimd
            def _(e):
                for f in self.ops['pool']:
                    f(e)

            @block.sync
            def _(e):
                for f in self.ops['sp']:
                    f(e)


class Ctx:
    pass


class Arena:
    def __init__(self, nc, es, words):
        self.t = es.enter_context(nc.sbuf_tensor('arena', [128, words], F32))
        self.tb = self.t.bitcast(BF16)
        self.words = words
        self.off = 0

    def reset(self):
        self.off = 0

    def mark(self):
        return self.off

    def release(self, m):
        self.off = m

    def alloc(self, shape, dt=F32):
        n = 1
        for s_ in shape[1:]:
            n *= s_
        esz = 4 if dt == F32 else 2
        nbytes = (n * esz + 63) // 64 * 64
        o = self.off
        self.off += nbytes
        assert self.off <= self.words * 4, "arena overflow %d" % self.off
        base = self.t if dt == F32 else self.tb
        v = base[:shape[0], o // esz:o // esz + n]
        if len(shape) == 3:
            v = v.rearrange("p (a b) -> p a b", a=shape[1])
        elif len(shape) == 4:
            v = v.rearrange("p (a b c) -> p a b c", a=shape[1], b=shape[2])
        return v


GELU_MODE = ['native']


def gelu_tanh(C, dst, src, src_key, dst_key):
    P = C.P
    if GELU_MODE[0] == 'native':
        P.op('act', lambda e: e.activation(dst, src, AF.Gelu_apprx_tanh), reads=[src_key], writes=[dst_key])
        return
    t = C.gelu_tmp[C.gelu_i % 2]
    tk = 'gelu_tmp%d' % (C.gelu_i % 2)
    C.gelu_i += 1
    P.op('dve', lambda e: e.tensor_tensor(t, src, src, ALU.mult), reads=[src_key], writes=[tk])
    P.op('dve', lambda e: e.tensor_scalar(t, t, 0.044715, 1.0, ALU.mult, ALU.add), reads=[tk], writes=[tk])
    P.op('dve', lambda e: e.tensor_tensor(t, t, src, ALU.mult), reads=[tk, src_key], writes=[tk])
    P.op('act', lambda e: e.activation(t, t, AF.Sigmoid, scale=1.5957691216057308), reads=[tk], writes=[tk])
    P.op('dve', lambda e: e.tensor_tensor(dst, t, src, ALU.mult), reads=[tk, src_key], writes=[dst_key])


def build(n_layers=DEPTH, stop_after=None, debug=False):
    nc = bass.Bass("TRN2", target_bir_lowering=False)
    es = ExitStack()
    P = Prog(nc, es)
    C = Ctx()
    C.nc, C.P, C.es = nc, P, es
    C.debug = debug
    C.dram_written = ['out']

    def din(name, shape, dt=F32):
        return nc.dram_tensor(name, list(shape), dt, kind="ExternalInput").ap()

    def dscr(name, shape, dt=F32):
        kind = "ExternalOutput" if (debug and name in debug) else "Internal"
        C.dram_written.append(name)
        return nc.dram_tensor(name, list(shape), dt, kind=kind).ap()

    def sb(name, shape, dt=F32):
        return es.enter_context(nc.sbuf_tensor(name, list(shape), dt))

    x_in = din("x", [S, D])
    c_col = din("c_col", [128, NCH])
    w_ada = din("w_ada", [DEPTH, D, 6 * D])
    b_ada = din("b_ada_col", [128, DEPTH * 96])
    w_in = din("w_in", [DEPTH, D, N_IN])
    ident_d = din("ident", [128, 128])
    s5A_d = din("s5A", [DEPTH, 128, 3 * 64])
    braw_d = din("braw", [DEPTH, 64, 128, 128])
    crci_d = din("crci", [DEPTH, 128, 2 * 1024])
    dskip_d = din("dskip_col", [128, DEPTH * 8])
    w_glu = din("ssm_w_glu", [DEPTH, 1024, 2048])
    convw_d = din("convw_col", [DEPTH, 128, 96])
    w_out = din("w_out", [DEPTH, D, D])
    wq_d = din("peer_w_query", [DEPTH, D, D])
    keysT_d = din("keysT", [DEPTH, 128, 16 * 128])
    uT_d = din("peer_uT", [DEPTH, D, 16384])
    v_d = din("peer_v", [DEPTH, 16384, D])
    iota_d = din("iota128", [128, 128])
    lnp_d = din("lnp_col", [128, DEPTH * 64])
    dnhp_d = din("dnhp", [8, DEPTH * 2])
    normw_d = din("normw_col", [128, DEPTH])
    maskL_d = din("maskL", [64, 64])
    maskAT_d = din("maskAT", [64, 64])
    selrow_d = din("selrow", [8, 8 * 64])
    sel63_d = din("sel63", [64, 128])
    cmask_d = din("cmask", [8, S])
    out_d = nc.dram_tensor("out", [S, D], F32, kind="ExternalOutput").ap()

    xT_d = dscr("xT_d", [D, S])
    projT_d = dscr("projT_d", [N_IN, S])
    ycatT_d = dscr("ycatT_d", [D, S], BF16)
    gyT_d = dscr("gyT_d", [1024, S], BF16)
    rT_d = dscr("rT_d", [D, S])
    hfT_d = dscr("hfT_d", [D, S], BF16)
    qT_d = dscr("qT_d", [D, S])
    Gd = dscr("Gd", [128, 128, S], BF16)

    ident = sb("ident_sb", [128, 128])
    ones_m = sb("ones_m", [128, 128])
    modT = sb("modT", [128, DEPTH * 96])
    cact = sb("cact", [128, NCH])
    eps_c = sb("eps_c", [128, 1])
    lnp = sb("lnp", [128, DEPTH * 64])
    psb = [es.enter_context(nc.psum_tensor("psb%d" % i, [128, 512], F32)) for i in range(8)]
    PS = lambda i: 'ps%d' % i
    A = Arena(nc, es, 51 * 1024)
    C.ident, C.ones_m, C.modT, C.psb, C.A = ident, ones_m, modT, psb, A

    P.dma('sp', ident[:], ident_d, writes=['ident'])
    P.dma('sp', lnp[:], lnp_d, writes=['lnp'])
    P.op('dve', lambda e: e.memset(ones_m[:], 1.0 / D), writes=['ones_m'])
    P.op('dve', lambda e: e.memset(eps_c[:], LN_EPS), writes=['eps_c'])

    xtm = [A.alloc([128, D]) for i in range(2)]
    xst = [A.alloc([128, NCH, 128]) for i in range(2)]
    ev = 0
    for tt in range(S // 128):
        b = tt % 2
        P.dma('sp', xtm[b], x_in[tt * 128:(tt + 1) * 128, :], writes=['xtm%d' % b])
        for g in range(4):
            pb = g % 2

            def tr(e, b=b, g=g, pb=pb):
                ins = None
                for jj in range(4):
                    j = g * 4 + jj
                    ins = e.transpose(psb[pb][:, jj * 128:(jj + 1) * 128],
                                      xtm[b][:, j * 128:(j + 1) * 128], ident[:])
                return ins
            P.op('pe', tr, reads=['xtm%d' % b, 'ident'], writes=[PS(pb)])
            dst = xst[b][:, g * 4:(g + 1) * 4, :]
            src = psb[pb][:].rearrange("p (j t) -> p j t", j=4)
            if ev % 2 == 0:
                P.op('act', lambda e, dst=dst, src=src: e.copy(dst, src),
                     reads=[PS(pb)], writes=[('xst', b, g)])
            else:
                P.op('dve', lambda e, dst=dst, src=src: e.tensor_copy(dst, src),
                     reads=[PS(pb)], writes=[('xst', b, g)])
            ev += 1
        P.dma('sp', xT_d.rearrange("(j p) t -> p j t", p=128)[:, :, tt * 128:(tt + 1) * 128],
              xst[b], reads=[('xst', b, g) for g in range(4)], writes=['xT_d'])

    P.barrier()
    A.reset()
    P.dma('sp', cact[:], c_col, writes=['cact'])
    P.op('act', lambda e: e.activation(cact[:], cact[:], AF.Silu), reads=['cact'], writes=['cact'])
    bada = A.alloc([128, DEPTH * 96])
    P.dma('sp', bada, b_ada, writes=['bada'])
    WA_COLS = 3072
    wa = [A.alloc([128, WA_COLS]) for i in range(3)]
    wi = 0
    for l in range(n_layers):
        for cb in range(6 * D // WA_COLS):
            for k in range(NCH):
                b = wi % 3
                wi += 1
                P.dma('sp' if (wi % 2) else 'act', wa[b],
                      w_ada[l, k * 128:(k + 1) * 128, cb * WA_COLS:(cb + 1) * WA_COLS],
                      writes=['wa%d' % b])

                def mm(e, b=b, k=k, cb=cb):
                    ins = None
                    for n in range(WA_COLS // 128):
                        col = cb * (WA_COLS // 128) + n
                        ins = e.matmul(psb[7][:, col:col + 1], wa[b][:, n * 128:(n + 1) * 128],
                                       cact[:, k:k + 1], start=(k == 0 and n == 0 and cb == 0),
                                       stop=(k == NCH - 1), skip_group_check=True)
                    return ins
                P.op('pe', mm, reads=['wa%d' % b, 'cact'], writes=[PS(7)])
        P.op('dve', lambda e, l=l: e.tensor_tensor(modT[:, l * 96:(l + 1) * 96], psb[7][:, 0:96],
                                                  bada[:, l * 96:(l + 1) * 96], ALU.add),
             reads=[PS(7), 'bada'], writes=['modT'])
    if stop_after == 'A':
        return finish_debug(C)

    for l in range(n_layers):
        for c0 in (16, 64):
            sl = modT[:, l * 96 + c0:l * 96 + c0 + 16]
            P.op('dve', lambda e, sl=sl: e.tensor_scalar(sl, sl, 1.0, None, ALU.add),
                 reads=['modT'], writes=['modT'])

    TB = 256
    C.TB = TB

    def ln_alloc():
        L = Ctx()
        L.x = [A.alloc([128, NCH, TB]) for i in range(2)]
        L.sq = A.alloc([128, NCH, TB])
        L.m = A.alloc([128, TB])
        L.v = A.alloc([128, TB])
        L.r = A.alloc([128, TB])
        L.nmr = A.alloc([128, TB])
        L.t = [A.alloc([128, TB]) for i in range(2)]
        return L

    def ln_block(L, xt, xt_key, a_of, b_of, out_of, out_key_of):
        mean_ps = psb[6][:, 0:TB]
        msq_ps = psb[7][:, 0:TB]

        def mm1(e):
            ins = None
            for j in range(NCH):
                ins = e.matmul(mean_ps, ones_m[:], xt[:, j, :], start=(j == 0), stop=(j == NCH - 1))
            return ins
        P.op('pe', mm1, reads=[xt_key, 'ones_m'], writes=[PS(6)])
        P.op('act', lambda e: e.activation(L.sq, xt, AF.Square), reads=[xt_key], writes=['ln_sq'])

        def mm2(e):
            ins = None
            for j in range(NCH):
                ins = e.matmul(msq_ps, ones_m[:], L.sq[:, j, :], start=(j == 0), stop=(j == NCH - 1))
            return ins
        P.op('pe', mm2, reads=['ln_sq', 'ones_m'], writes=[PS(7)])
        P.op('act', lambda e: e.copy(L.m, mean_ps), reads=[PS(6)], writes=['ln_m'])
        P.op('dve', lambda e: e.scalar_tensor_tensor(L.v, L.m, -1.0, L.m, ALU.mult, ALU.mult),
             reads=['ln_m'], writes=['ln_v'])
        P.op('dve', lambda e: e.tensor_tensor(L.v, L.v, msq_ps, ALU.add),
             reads=['ln_v', PS(7)], writes=['ln_v'])
        P.op('act', lambda e: e.activation(L.v, L.v, AF.Sqrt, bias=eps_c[:], scale=1.0),
             reads=['ln_v', 'eps_c'], writes=['ln_v'])
        P.op('dve', lambda e: e.reciprocal(L.r, L.v), reads=['ln_v'], writes=['ln_r'])
        P.op('dve', lambda e: e.scalar_tensor_tensor(L.nmr, L.m, -1.0, L.r, ALU.mult, ALU.mult),
             reads=['ln_m', 'ln_r'], writes=['ln_nmr'])
        for j in range(NCH):
            tb = j % 2
            t = L.t[tb]
            a = a_of(j)
            P.op('dve', lambda e, t=t, j=j, a=a: e.scalar_tensor_tensor(t, xt[:, j, :], a, L.r,
                                                                      ALU.mult, ALU.mult),
                 reads=[xt_key, 'ln_r', 'modT'], writes=['ln_t%d' % tb])
            P.op('dve', lambda e, t=t, a=a: e.scalar_tensor_tensor(t, L.nmr, a, t, ALU.mult, ALU.add),
                 reads=['ln_nmr', 'ln_t%d' % tb, 'modT'], writes=['ln_t%d' % tb])
            o = out_of(j)
            bcol = b_of(j)
            P.op('act', lambda e, o=o, t=t, bcol=bcol: e.activation(o, t, AF.Identity, bias=bcol, scale=1.0),
                 reads=['ln_t%d' % tb, 'modT'], writes=[out_key_of(j)])

    xT_v = xT_d.rearrange("(j p) t -> p j t", p=128)

    for l in range(n_layers):
        mo = l * 96
        def phase_0(l=l, mo=mo):
            P.barrier()
            A.reset()
            hT = A.alloc([128, NCH, S], BF16)
            L = ln_alloc()
            for tb in range(S // TB):
                b = tb % 2
                P.dma('sp', L.x[b], xT_v[:, :, tb * TB:(tb + 1) * TB], reads=['xT_d'], writes=['ln_x%d' % b])
                ln_block(L, L.x[b], 'ln_x%d' % b,
                         lambda j: modT[:, mo + 16 + j:mo + 16 + j + 1],
                         lambda j: modT[:, mo + j:mo + j + 1],
                         lambda j, tb=tb: hT[:, j, tb * TB:(tb + 1) * TB],
                         lambda j, tb=tb: ('hT', tb))
            wt = [A.alloc([128, NCH, 512], BF16) for i in range(2)]
            stg = [A.alloc([128, 512]) for i in range(4)]
            w_in_v = w_in[l].rearrange("(k p) n -> p k n", p=128)
            hkeys = [('hT', tb) for tb in range(S // TB)]
            cnt = 0
            for ng in range((N_IN + 511) // 512):
                n0 = ng * 512
                nw = min(512, N_IN - n0)
                b = ng % 2
                P.dma('pool', wt[b][:, :, :nw], w_in_v[:, :, n0:n0 + nw], writes=['wt%d' % b])
                for nc_ in range((nw + 127) // 128):
                    m = min(128, nw - nc_ * 128)
                    for t4 in range(S // 512):
                        pb = cnt % 4
                        cnt += 1

                        def mm(e, b=b, nc_=nc_, m=m, t4=t4, pb=pb):
                            ins = None
                            for k in range(NCH):
                                ins = e.matmul(psb[pb][:m, :], wt[b][:, k, nc_ * 128:nc_ * 128 + m],
                                               hT[:, k, t4 * 512:(t4 + 1) * 512],
                                               start=(k == 0), stop=(k == NCH - 1))
                            return ins
                        P.op('pe', mm, reads=['wt%d' % b] + hkeys, writes=[PS(pb)])
                        if pb % 2 == 0:
                            P.op('act', lambda e, pb=pb, m=m: e.copy(stg[pb][:m, :], psb[pb][:m, :]),
                                 reads=[PS(pb)], writes=['stg%d' % pb])
                        else:
                            P.op('dve', lambda e, pb=pb, m=m: e.tensor_copy(stg[pb][:m, :], psb[pb][:m, :]),
                                 reads=[PS(pb)], writes=['stg%d' % pb])
                        r0 = n0 + nc_ * 128
                        P.dma('sp', projT_d[r0:r0 + m, t4 * 512:(t4 + 1) * 512], stg[pb][:m, :],
                              reads=['stg%d' % pb], writes=['projT_d'])
            if stop_after == 'B':
                return 'STOP'

            return None
        if phase_0() == 'STOP':
            return finish_debug(C)
        def phase_1(l=l, mo=mo):
            P.barrier()
            A.reset()
            NL = list(range(8)) + [8 * k for k in range(1, 8)] + [64 * k for k in range(1, 8)] + [0, 512, 1024, 1536]
            NP_ = len(NL)
            TWO_PI = 2.0 * math.pi
            MAGIC = 12582912.0
            COL1 = A.alloc([128, 22, 64])
            COL2 = A.alloc([128, 22, 64])
            WcF = A.alloc([128, 4, 64, 16])
            PIp = A.alloc([128, 64])
            identb = A.alloc([128, 128], BF16)
            dsk = A.alloc([128, 8])
            s5_mark = A.mark()
            s5a = A.alloc([128, 3, 64])
            P.dma('sp', s5a, s5A_d[l].rearrange("p (a g) -> p a g", a=3), writes=['s5a'])
            crci = A.alloc([128, 2, 64, 16])
            P.dma('sp', crci, crci_d[l].rearrange("p (a g c) -> p a g c", a=2, g=64), writes=['crci'])
            P.dma('sp', dsk, dskip_d[:, l * 8:(l + 1) * 8], writes=['dsk'])
            NV = A.alloc([128, NP_, 64])
            for i, n in enumerate(NL):
                P.op('pool', lambda e, i=i, n=n: e.memset(NV[:, i, :], float(n)), writes=['NV'])
            stp = A.alloc([128, 64])
            thd = A.alloc([128, 2, 64])
            P.op('act', lambda e: e.activation(stp, s5a[:, 2, :], AF.Exp), reads=['s5a'], writes=['stp'])
            P.op('dve', lambda e: e.tensor_tensor(thd[:, 0, :], s5a[:, 1, :], stp, ALU.mult), reads=['s5a', 'stp'], writes=['thd'])
            P.op('dve', lambda e: e.tensor_tensor(thd[:, 1, :], s5a[:, 0, :], stp, ALU.mult), reads=['s5a', 'stp', 'thd'], writes=['thd'])
            ang = A.alloc([128, NP_, 64])
            tq = A.alloc([128, NP_, 64])
            sinv = A.alloc([128, NP_, 64])
            cosv = A.alloc([128, NP_, 64])
            mag = A.alloc([128, NP_, 64])
            bc = lambda ap2: ap2.unsqueeze(1).to_broadcast([128, NP_, 64])
            P.op('dve', lambda e: e.tensor_tensor(ang, NV, bc(thd[:, 0, :]), ALU.mult), reads=['NV', 'thd'], writes=['ang'])

            def sin_of(dst, key, shift):
                if shift != 0.0:
                    P.op('dve', lambda e: e.tensor_scalar(dst, ang, shift, None, ALU.add), reads=['ang'], writes=[key])
                    srcx = dst
                else:
                    srcx = ang
                P.op('dve', lambda e: e.tensor_scalar(tq, srcx, 1.0 / TWO_PI, MAGIC, ALU.mult, ALU.add),
                     reads=['ang', key], writes=['tq'])
                P.op('dve', lambda e: e.tensor_scalar(tq, tq, MAGIC, None, ALU.subtract), reads=['tq'], writes=['tq'])
                C1 = 6.28125
                C2 = TWO_PI - C1
                P.op('dve', lambda e: e.scalar_tensor_tensor(dst, tq, -C1, srcx, ALU.mult, ALU.add),
                     reads=['tq', 'ang', key], writes=[key])
                P.op('dve', lambda e: e.scalar_tensor_tensor(dst, tq, -C2, dst, ALU.mult, ALU.add),
                     reads=['tq', key], writes=[key])
                P.op('dve', lambda e: e.tensor_scalar(dst, dst, 3.14159, -3.14159, ALU.min, ALU.max), reads=[key], writes=[key])
                P.op('act', lambda e: e.activation(dst, dst, AF.Sin), reads=[key], writes=[key])
            sin_of(sinv, 'sinv', 0.0)
            sin_of(cosv, 'cosv', 0.5 * math.pi)
            P.op('dve', lambda e: e.tensor_tensor(mag, NV, bc(thd[:, 1, :]), ALU.mult), reads=['NV', 'thd'], writes=['mag'])
            P.op('act', lambda e: e.activation(mag, mag, AF.Exp), reads=['mag'], writes=['mag'])
            P.op('dve', lambda e: e.tensor_tensor(cosv, cosv, mag, ALU.mult), reads=['cosv', 'mag'], writes=['cosv'])
            P.op('dve', lambda e: e.tensor_tensor(sinv, sinv, mag, ALU.mult), reads=['sinv', 'mag'], writes=['sinv'])
            ar, ai = cosv, sinv
            cf = A.alloc([128, 6, 64])
            P.op('dve', lambda e: e.tensor_scalar(cf[:, 0, :], ar[:, 1, :], -1.0, None, ALU.add), reads=['cosv'], writes=['cf0'])
            P.op('dve', lambda e: e.tensor_tensor(cf[:, 1, :], s5a[:, 0, :], s5a[:, 0, :], ALU.mult), reads=['s5a'], writes=['cf1'])
            P.op('dve', lambda e: e.tensor_tensor(cf[:, 4, :], s5a[:, 1, :], s5a[:, 1, :], ALU.mult), reads=['s5a'], writes=['cf4'])
            P.op('dve', lambda e: e.tensor_tensor(cf[:, 1, :], cf[:, 1, :], cf[:, 4, :], ALU.add), reads=['cf1', 'cf4'], writes=['cf1'])
            P.op('dve', lambda e: e.reciprocal(cf[:, 5, :], cf[:, 1, :]), reads=['cf1'], writes=['cf5'])
            P.op('dve', lambda e: e.tensor_tensor(cf[:, 2, :], cf[:, 0, :], s5a[:, 0, :], ALU.mult), reads=['cf0', 's5a'], writes=['cf2'])
            P.op('dve', lambda e: e.tensor_tensor(cf[:, 4, :], ai[:, 1, :], s5a[:, 1, :], ALU.mult), reads=['sinv', 's5a', 'cf4'], writes=['cf4'])
            P.op('dve', lambda e: e.tensor_tensor(cf[:, 2, :], cf[:, 2, :], cf[:, 4, :], ALU.add), reads=['cf2', 'cf4'], writes=['cf2'])
            P.op('dve', lambda e: e.tensor_tensor(cf[:, 2, :], cf[:, 2, :], cf[:, 5, :], ALU.mult), reads=['cf2', 'cf5'], writes=['cf2'])
            P.op('dve', lambda e: e.tensor_tensor(cf[:, 3, :], ai[:, 1, :], s5a[:, 0, :], ALU.mult), reads=['sinv', 's5a'], writes=['cf3'])
            P.op('dve', lambda e: e.tensor_tensor(cf[:, 4, :], cf[:, 0, :], s5a[:, 1, :], ALU.mult), reads=['cf0', 's5a', 'cf2'], writes=['cf4'])
            P.op('dve', lambda e: e.tensor_tensor(cf[:, 3, :], cf[:, 3, :], cf[:, 4, :], ALU.subtract), reads=['cf3', 'cf4'], writes=['cf3'])
            P.op('dve', lambda e: e.tensor_tensor(cf[:, 3, :], cf[:, 3, :], cf[:, 5, :], ALU.mult), reads=['cf3', 'cf5'], writes=['cf3'])
            FR = A.alloc([128, 22, 64])
            FI = A.alloc([128, 22, 64])
            ftmp = A.alloc([128, 8, 64])
            bc8 = lambda ap2: ap2.unsqueeze(1).to_broadcast([128, 8, 64])
            P.op('dve', lambda e: e.tensor_tensor(FR[:, 0:8, :], ar[:, 0:8, :], bc8(cf[:, 2, :]), ALU.mult), reads=['cosv', 'cf2'], writes=['FR'])
            P.op('dve', lambda e: e.tensor_tensor(ftmp, ai[:, 0:8, :], bc8(cf[:, 3, :]), ALU.mult), reads=['sinv', 'cf3'], writes=['ftmp'])
            P.op('dve', lambda e: e.tensor_tensor(FR[:, 0:8, :], FR[:, 0:8, :], ftmp, ALU.subtract), reads=['FR', 'ftmp'], writes=['FR'])
            P.op('dve', lambda e: e.tensor_tensor(FI[:, 0:8, :], ar[:, 0:8, :], bc8(cf[:, 3, :]), ALU.mult), reads=['cosv', 'cf3'], writes=['FI'])
            P.op('dve', lambda e: e.tensor_tensor(ftmp, ai[:, 0:8, :], bc8(cf[:, 2, :]), ALU.mult), reads=['sinv', 'cf2', 'FR'], writes=['ftmp'])
            P.op('dve', lambda e: e.tensor_tensor(FI[:, 0:8, :], FI[:, 0:8, :], ftmp, ALU.add), reads=['FI', 'ftmp'], writes=['FI'])
            P.op('dve', lambda e: e.tensor_copy(FR[:, 8:22, :], ar[:, 8:22, :]), reads=['cosv', 'FR'], writes=['FR'])
            P.op('dve', lambda e: e.tensor_copy(FI[:, 8:22, :], ai[:, 8:22, :]), reads=['sinv', 'FI'], writes=['FI'])
            P.op('dve', lambda e: e.tensor_copy(COL1[0:64], FR[0:64]), reads=['FR'], writes=['COL1a'])
            P.op('dve', lambda e: e.tensor_scalar(COL1[64:128], FI[64:128], -1.0, None, ALU.mult), reads=['FI'], writes=['COL1b'])
            P.op('dve', lambda e: e.tensor_copy(COL2[0:64], FI[0:64]), reads=['FI'], writes=['COL2a'])
            P.op('dve', lambda e: e.tensor_copy(COL2[64:128], FR[64:128]), reads=['FR'], writes=['COL2b'])
            colkeys = ['COL1a', 'COL1b', 'COL2a', 'COL2b']
            X1 = A.alloc([128, 4, 64])
            X2 = A.alloc([128, 4, 64])
            P.op('dve', lambda e: e.tensor_copy(X1[0:64], ar[0:64, 22:26, :]), reads=['cosv'], writes=['X1a'])
            P.op('dve', lambda e: e.tensor_scalar(X1[64:128], ai[64:128, 22:26, :], -1.0, None, ALU.mult), reads=['sinv'], writes=['X1b'])
            P.op('dve', lambda e: e.tensor_scalar(X2[0:64], ai[0:64, 22:26, :], -1.0, None, ALU.mult), reads=['sinv'], writes=['X2a'])
            P.op('dve', lambda e: e.tensor_scalar(X2[64:128], ar[64:128, 22:26, :], -1.0, None, ALU.mult), reads=['cosv'], writes=['X2b'])
            wct = A.alloc([128, 64, 16])
            for k in range(4):
                x1b = X1[:, k, :].unsqueeze(2).to_broadcast([128, 64, 16])
                x2b = X2[:, k, :].unsqueeze(2).to_broadcast([128, 64, 16])
                P.op('dve', lambda e, k=k, x1b=x1b: e.tensor_tensor(WcF[:, k], crci[:, 0], x1b, ALU.mult),
                     reads=['crci', 'X1a', 'X1b'], writes=[('WcF', k)])
                P.op('dve', lambda e, k=k, x2b=x2b: e.tensor_tensor(wct, crci[:, 1], x2b, ALU.mult),
                     reads=['crci', 'X2a', 'X2b'], writes=['wct'])
                P.op('dve', lambda e, k=k: e.tensor_tensor(WcF[:, k], WcF[:, k], wct, ALU.add),
                     reads=[('WcF', k), 'wct'], writes=[('WcF', k)])
            P.op('dve', lambda e: e.tensor_copy(PIp[0:64], ident[0:64, 0:64]), reads=['ident'], writes=['PIa'])
            P.op('dve', lambda e: e.tensor_copy(PIp[64:128], ident[64:128, 64:128]), reads=['ident'], writes=['PIb'])
            P.op('dve', lambda e: e.tensor_copy(identb, ident[:]), reads=['ident'], writes=['identb'])

            if debug and 'dbgS5' in debug:
                for nm, t_, shp in (('dbg_col1', COL1, [128, 22 * 64]), ('dbg_col2', COL2, [128, 22 * 64]),
                                    ('dbg_wcf', WcF, [128, 4 * 64 * 16])):
                    dd = nc.dram_tensor(nm, shp, F32, kind="ExternalOutput").ap()
                    flat = t_.rearrange("p a b -> p (a b)") if len(t_.shape) == 3 else t_.rearrange("p a b c -> p (a b c)")
                    P.dma('sp', dd, flat, reads=['COL1a', 'COL1b', 'COL2a', 'COL2b'] + [('WcF', k) for k in range(4)], writes=[nm])
                    C.dram_written.append(nm)
            P.barrier()
            A.release(s5_mark)
            colkeys = []
            PAD0, PAD1, PAD2, PAD3 = 8, 56, 448, 1536
            Wd = A.alloc([128, 22, 8, 128], BF16)
            WcZ = A.alloc([128, 4, 8, 128], BF16)
            Bw = A.alloc([128, 8, 128], BF16)
            u32 = A.alloc([128, S])
            ubf = A.alloc([128, S], BF16)
            z0 = [A.alloc([128, PAD0 + S], BF16) for i in range(2)]
            w1 = [A.alloc([128, PAD1 + S], BF16) for i in range(2)]
            w2 = [A.alloc([128, PAD2 + S], BF16) for i in range(2)]
            w3 = [A.alloc([128, PAD3 + S], BF16) for i in range(8)]
            gyo = [A.alloc([128, 512], BF16) for i in range(2)]
            ytmp = [A.alloc([128, 512]) for i in range(2)]
            C.gelu_tmp = [A.alloc([128, 512]) for i in range(2)]
            C.gelu_i = 0
            P.op('pool', lambda e: e.memset(WcZ, 0.0), writes=['WcZ'])
            for i in range(2):
                P.op('pool', lambda e, i=i: e.memset(z0[i][:, 0:PAD0], 0.0), writes=['z0_%d' % i])
                P.op('pool', lambda e, i=i: e.memset(w1[i][:, 0:PAD1], 0.0), writes=['w1_%d' % i])
                P.op('pool', lambda e, i=i: e.memset(w2[i][:, 0:PAD2], 0.0), writes=['w2_%d' % i])
            for i in range(8):
                P.op('pool', lambda e, i=i: e.memset(w3[i][:, 0:PAD3], 0.0), writes=['w3_%d' % i])
            evc = [0]

            def evac(dst, pb, key, reads_extra=()):
                if evc[0] % 2 == 0:
                    P.op('act', lambda e: e.copy(dst, psb[pb][:]), reads=[PS(pb)], writes=[key])
                else:
                    P.op('dve', lambda e: e.tensor_copy(dst, psb[pb][:]), reads=[PS(pb)], writes=[key])
                evc[0] += 1
            pbc = [0]

            def nextpb():
                pb = pbc[0] % 6
                pbc[0] += 1
                return pb

            import os as _os
            for ch in ([7, 6, 5, 4, 3, 2, 1, 0] if _os.environ.get('S5REV') else range(8)):
                g0 = ch * 8
                P.dma('sp', u32, projT_d[ch * 128:(ch + 1) * 128, :], reads=['projT_d'], writes=['u32'])
                P.op('pool', lambda e: e.tensor_copy(ubf, u32), reads=['u32'], writes=['ubf'])
                P.dma('pool', Bw, braw_d[l, g0:g0 + 8].rearrange("g k m -> k g m"), writes=['Bw'])
                for n in range(22):
                    c1 = COL1[:, n, g0:g0 + 8].unsqueeze(2).to_broadcast([128, 8, 64])
                    c2 = COL2[:, n, g0:g0 + 8].unsqueeze(2).to_broadcast([128, 8, 64])
                    pib = PIp.unsqueeze(1).to_broadcast([128, 8, 64])
                    eng = 'dve' if n % 2 == 0 else 'pool'
                    P.op(eng, lambda e, n=n, c1=c1, pib=pib: e.tensor_tensor(Wd[:, n, :, 0:64], pib, c1, ALU.mult),
                         reads=colkeys + ['PIa', 'PIb'], writes=[('Wd', n, 0)])
                    P.op(eng, lambda e, n=n, c2=c2, pib=pib: e.tensor_tensor(Wd[:, n, :, 64:128], pib, c2, ALU.mult),
                         reads=colkeys + ['PIa', 'PIb'], writes=[('Wd', n, 1)])
                wdkeys = [('Wd', n, h) for n in range(22) for h in range(2)]
                for k in range(4):
                    for i in range(8):
                        P.op("pool", lambda e, k=k, i=i, g0=g0: e.tensor_copy(WcZ[:, k, i, 16 * i:16 * i + 16], WcF[:, k, g0 + i, :]),
                             reads=[('WcF', k)], writes=['WcZ'])
                for gi in range(8):
                    zb = gi % 2
                    for t4 in range(4):
                        pb = nextpb()
                        P.op('pe', lambda e, pb=pb, gi=gi, t4=t4: e.matmul(psb[pb][:], Bw[:, gi, :], ubf[:, t4 * 512:(t4 + 1) * 512],
                                                                          start=True, stop=True),
                             reads=['Bw', 'ubf'], writes=[PS(pb)])
                        evac(z0[zb][:, PAD0 + t4 * 512:PAD0 + (t4 + 1) * 512], pb, 'z0_%d' % zb)
                    for (lev, src, skey, pad_s, dst, dkey, pad_d, stride, nbase) in (
                            (1, z0[zb], 'z0_%d' % zb, PAD0, w1[zb], 'w1_%d' % zb, PAD1, 1, 0),
                            (2, w1[zb], 'w1_%d' % zb, PAD1, w2[zb], 'w2_%d' % zb, PAD2, 8, 7),
                            (3, w2[zb], 'w2_%d' % zb, PAD2, w3[gi], 'w3_%d' % gi, PAD3, 64, 14)):
                        for t4 in range(4):
                            pb = nextpb()

                            def mm(e, pb=pb, lev=lev, src=src, pad_s=pad_s, stride=stride, nbase=nbase, t4=t4, gi=gi):
                                ins = None
                                for k in range(8):
                                    if lev == 1:
                                        w = Wd[:, k, gi, :]
                                    elif k == 0:
                                        w = identb
                                    else:
                                        w = Wd[:, nbase + k, gi, :]
                                    o = pad_s + t4 * 512 - stride * k
                                    ins = e.matmul(psb[pb][:], w, src[:, o:o + 512], start=(k == 0), stop=(k == 7))
                                return ins
                            P.op('pe', mm, reads=wdkeys + ['identb', skey], writes=[PS(pb)])
                            evac(dst[:, pad_d + t4 * 512:pad_d + (t4 + 1) * 512], pb, dkey)
                for t4 in range(4):
                    pb = nextpb()

                    def mm4(e, pb=pb, t4=t4):
                        ins = None
                        for gi in range(8):
                            for k in range(4):
                                o = PAD3 + t4 * 512 - 512 * k
                                ins = e.matmul(psb[pb][:], WcZ[:, k, gi, :], w3[gi][:, o:o + 512],
                                               start=(gi == 0 and k == 0), stop=(gi == 7 and k == 3))
                        return ins
                    P.op('pe', mm4, reads=['WcZ'] + ['w3_%d' % i for i in range(8)], writes=[PS(pb)])
                    yt = ytmp[t4 % 2]
                    P.op('dve', lambda e, yt=yt, pb=pb, t4=t4, ch=ch: e.scalar_tensor_tensor(
                        yt, u32[:, t4 * 512:(t4 + 1) * 512], dsk[:, ch:ch + 1], psb[pb][:], ALU.mult, ALU.add),
                        reads=['u32', 'dsk', PS(pb)], writes=['ytmp%d' % (t4 % 2)])
                    gelu_tanh(C, gyo[t4 % 2], yt, 'ytmp%d' % (t4 % 2), 'gyo%d' % (t4 % 2))
                    P.dma('sp', gyT_d[ch * 128:(ch + 1) * 128, t4 * 512:(t4 + 1) * 512], gyo[t4 % 2],
                          reads=['gyo%d' % (t4 % 2)], writes=['gyT_d'])
            P.barrier()
            A.reset()
            gy = A.alloc([128, 8, S], BF16)
            for ch in range(8):
                P.dma('sp', gy[:, ch, :], gyT_d[ch * 128:(ch + 1) * 128, :], writes=[('gy', ch)])
            wg = A.alloc([128, 8, 2048], BF16)
            for ch in range(8):
                P.dma('pool', wg[:, ch, :], w_glu[l, ch * 128:(ch + 1) * 128, :], writes=[('wg', ch)])
            wgkeys = [('wg', ch) for ch in range(8)]
            gykeys = [('gy', ch) for ch in range(8)]
            sig = [A.alloc([128, 512]) for i in range(2)]
            yo = [A.alloc([128, 512], BF16) for i in range(2)]
            ci = 0
            for j in range(8):
                for t4 in range(4):
                    pa, pbb = (0, 1) if ci % 2 == 0 else (2, 3)
                    sb_i = ci % 2
                    ci += 1
                    for (pbk, col0) in ((pa, j * 128), (pbb, (8 + j) * 128)):
                        def mmg(e, pbk=pbk, col0=col0, t4=t4):
                            ins = None
                            for chh in range(8):
                                ins = e.matmul(psb[pbk][:], wg[:, chh, col0:col0 + 128], gy[:, chh, t4 * 512:(t4 + 1) * 512],
                                               start=(chh == 0), stop=(chh == 7))
                            return ins
                        P.op('pe', mmg, reads=wgkeys + gykeys, writes=[PS(pbk)])
                    P.op('act', lambda e, sb_i=sb_i, pbb=pbb: e.activation(sig[sb_i], psb[pbb][:], AF.Sigmoid),
                         reads=[PS(pbb)], writes=['sig%d' % sb_i])
                    P.op('dve', lambda e, sb_i=sb_i, pa=pa: e.tensor_tensor(yo[sb_i], psb[pa][:], sig[sb_i], ALU.mult),
                         reads=[PS(pa), 'sig%d' % sb_i], writes=['yo%d' % sb_i])
                    P.dma('sp', ycatT_d[j * 128:(j + 1) * 128, t4 * 512:(t4 + 1) * 512], yo[sb_i],
                          reads=['yo%d' % sb_i], writes=['ycatT_d'])
            if stop_after == 'C':
                return 'STOP'

            return None
        if phase_1() == 'STOP':
            return finish_debug(C)
        def phase_2(l=l, mo=mo):
            P.barrier()
            A.reset()
            NEG = -30000.0
            cw = A.alloc([128, 24, 4])
            P.dma('sp', cw, convw_d[l].rearrange("p (j k) -> p j k", k=4), writes=['cw'])
            normw_all = A.alloc([128, DEPTH])
            P.dma('sp', normw_all, normw_d, writes=['normw'])
            normw = normw_all[:, l:l + 1]
            hp = A.alloc([8, 2])
            P.dma('sp', hp, dnhp_d[:, 2 * l:2 * l + 2], writes=['hp'])
            maskL = A.alloc([64, 64])
            maskAT = A.alloc([64, 64])
            selrow = A.alloc([8, 8, 64])
            sel63 = A.alloc([64, 128])
            P.dma('sp', maskL, maskL_d, writes=['maskL'])
            P.dma('sp', maskAT, maskAT_d, writes=['maskAT'])
            P.dma('sp', selrow, selrow_d.rearrange("k (h m) -> k h m", h=8), writes=['selrow'])
            P.dma('sp', sel63, sel63_d, writes=['sel63'])
            ones1 = A.alloc([128, 128])
            P.op('pool', lambda e: e.memset(ones1, 1.0), writes=['ones1'])
            eps6 = A.alloc([128, 1])
            P.op('pool', lambda e: e.memset(eps6, 1e-6), writes=['eps6'])
            one_c = A.alloc([128, 1])
            P.op('pool', lambda e: e.memset(one_c, 1.0), writes=['one_c'])
            gcum = A.alloc([8, S])
            NCK = S // 64
            beta_tm = A.alloc([64, NCK, 8])
            gcum_tm = A.alloc([64, NCK, 8])
            egc_tm = A.alloc([64, NCK, 8])
            nbeta_tm = A.alloc([64, NCK, 8])
            bg_tm = A.alloc([64, NCK, 8])
            etail_tm = A.alloc([64, NCK, 8])
            eglast = A.alloc([128, NCK, 8])
            naexp = A.alloc([8, 1])
            dn_mark = A.mark()
            cmask = A.alloc([8, S])
            P.dma('sp', cmask, cmask_d, writes=['cmask'])
            Bt = A.alloc([8, S])
            At = A.alloc([8, S])
            P.dma('sp', Bt, projT_d[5120:5128, :], reads=['projT_d'], writes=['Bt'])
            P.dma('sp', At, projT_d[5128:5136, :], reads=['projT_d'], writes=['At'])
            P.op('act', lambda e: e.activation(Bt, Bt, AF.Sigmoid), reads=['Bt'], writes=['Bt'])
            P.op('act', lambda e: e.activation(naexp, hp[:, 0:1], AF.Exp), reads=['hp'], writes=['naexp'])
            P.op('dve', lambda e: e.tensor_scalar(naexp, naexp, -1.0, None, ALU.mult), reads=['naexp'], writes=['naexp'])
            P.op('act', lambda e: e.activation(At, At, AF.Exp, bias=hp[:, 1:2], scale=1.0), reads=['At', 'hp'], writes=['At'])
            P.op('act', lambda e: e.activation(At, At, AF.Ln, bias=one_c[0:8, :], scale=1.0), reads=['At', 'one_c'], writes=['At'])
            P.op('dve', lambda e: e.tensor_scalar(At, At, naexp[:, 0:1], None, ALU.mult), reads=['At', 'naexp'], writes=['At'])
            P.op('dve', lambda e: e.tensor_tensor_scan(gcum, cmask, At, 0.0, ALU.mult, ALU.add),
                 reads=['At', 'cmask'], writes=['gcum'])
            for (src_t, skey, dst_t, dkey, pb) in ((Bt, 'Bt', beta_tm, 'beta_tm', 0), (gcum, 'gcum', gcum_tm, 'gcum_tm', 1)):
                def trs(e, src_t=src_t, pb=pb):
                    ins = None
                    for ck in range(NCK):
                        ins = e.transpose(psb[pb][0:64, ck * 8:(ck + 1) * 8], src_t[0:8, ck * 64:(ck + 1) * 64], ident[0:8, 0:8])
                    return ins
                P.op('pe', trs, reads=[skey, 'ident'], writes=[PS(pb)])
                P.op('dve', lambda e, dst_t=dst_t, pb=pb: e.tensor_copy(dst_t.rearrange("p a b -> p (a b)"), psb[pb][0:64, 0:NCK * 8]),
                     reads=[PS(pb)], writes=[dkey])
            P.op('act', lambda e: e.activation(egc_tm, gcum_tm, AF.Exp), reads=['gcum_tm'], writes=['egc_tm'])
            P.op('dve', lambda e: e.tensor_scalar(nbeta_tm, beta_tm, -1.0, None, ALU.mult), reads=['beta_tm'], writes=['nbeta_tm'])
            P.op('dve', lambda e: e.tensor_tensor(bg_tm, beta_tm, egc_tm, ALU.mult), reads=['beta_tm', 'egc_tm'], writes=['bg_tm'])
            P.op('pe', lambda e: e.matmul(psb[2][:, 0:NCK * 8], sel63, gcum_tm.rearrange("p a b -> p (a b)"), start=True, stop=True),
                 reads=['sel63', 'gcum_tm'], writes=[PS(2)])
            P.op('dve', lambda e: e.tensor_tensor(etail_tm.rearrange("p a b -> p (a b)"), psb[2][0:64, 0:NCK * 8],
                                                  gcum_tm.rearrange("p a b -> p (a b)"), ALU.subtract),
                 reads=[PS(2), 'gcum_tm'], writes=['etail_tm'])
            P.op('act', lambda e: e.activation(etail_tm, etail_tm, AF.Exp), reads=['etail_tm'], writes=['etail_tm'])
            P.op('act', lambda e: e.activation(eglast.rearrange("p a b -> p (a b)"), psb[2][:, 0:NCK * 8], AF.Exp),
                 reads=[PS(2)], writes=['eglast'])
            scal_keys = ['beta_tm', 'gcum_tm', 'egc_tm', 'nbeta_tm', 'bg_tm', 'etail_tm', 'eglast']

            import os as _os
            DNS = _os.environ.get('DN_STOP', '')
            if DNS == 'gates':
                return 'STOP'
            for hg in range(2):
                P.barrier()
                A.release(dn_mark)
                HG = 4
                qT = A.alloc([128, HG, S])
                kT = A.alloc([128, HG, S])
                vT = A.alloc([128, HG, S])
                zs = A.alloc([128, HG, S], BF16)
                yT = A.alloc([128, HG, S], BF16)
                Sst = A.alloc([128, HG, 128])
                P.op('pool', lambda e: e.memset(Sst, 0.0), writes=['Sst'])
                conv_mark = A.mark()
                cin = [A.alloc([128, 3 + S]) for i in range(2)]
                cacc = [A.alloc([128, S]) for i in range(2)]
                csq = A.alloc([128, S])
                crs = A.alloc([128, S])
                for i in range(2):
                    P.op('pool', lambda e, i=i: e.memset(cin[i][:, 0:3], 0.0), writes=['cin%d' % i])
                ci = 0
                for hh in range(HG):
                    h = hg * HG + hh
                    for which, dstT in ((0, qT), (1, kT), (2, vT)):
                        jch = which * 8 + h
                        b = ci % 2
                        ci += 1
                        P.dma('sp', cin[b][:, 3:3 + S], projT_d[1024 + jch * 128:1024 + (jch + 1) * 128, :],
                              reads=['projT_d'], writes=['cin%d' % b])
                        acc = cacc[b]
                        ak = 'cacc%d' % b
                        eng = 'dve'
                        P.op(eng, lambda e, acc=acc, b=b, jch=jch: e.tensor_scalar(acc, cin[b][:, 3:3 + S], cw[:, jch, 3:4], None, ALU.mult),
                             reads=['cin%d' % b, 'cw'], writes=[ak])
                        for kk in range(3):
                            P.op(eng, lambda e, acc=acc, b=b, jch=jch, kk=kk: e.scalar_tensor_tensor(
                                acc, cin[b][:, kk:kk + S], cw[:, jch, kk:kk + 1], acc, ALU.mult, ALU.add),
                                reads=['cin%d' % b, 'cw', ak], writes=[ak])
                        if which == 2:
                            P.op('act', lambda e, acc=acc, hh=hh: e.activation(vT[:, hh, :], acc, AF.Silu),
                                 reads=[ak], writes=[('vT', hh)])
                            continue
                        P.op('act', lambda e, acc=acc: e.activation(acc, acc, AF.Silu), reads=[ak], writes=[ak])
                        P.op('act', lambda e, acc=acc: e.activation(csq, acc, AF.Square), reads=[ak], writes=['csq'])
                        for t4 in range(4):
                            pb = t4 % 2
                            P.op('pe', lambda e, pb=pb, t4=t4: e.matmul(psb[pb][:], ones1, csq[:, t4 * 512:(t4 + 1) * 512], start=True, stop=True),
                                 reads=['ones1', 'csq'], writes=[PS(pb)])
                            P.op('act', lambda e, pb=pb, t4=t4: e.activation(crs[:, t4 * 512:(t4 + 1) * 512], psb[pb][:], AF.Sqrt,
                                                                             bias=eps6, scale=1.0),
                                 reads=[PS(pb), 'eps6'], writes=['crs'])
                        P.op('dve', lambda e: e.reciprocal(crs, crs), reads=['crs'], writes=['crs'])
                        sc = (128.0 ** -0.5) if which == 0 else 1.0
                        P.op('dve', lambda e, acc=acc, dstT=dstT, hh=hh, sc=sc: e.scalar_tensor_tensor(
                            dstT[:, hh, :], acc, sc, crs, ALU.mult, ALU.mult),
                            reads=[ak, 'crs'], writes=[('qT' if which == 0 else 'kT', hh)])
                    b = ci % 2
                    ci += 1
                    P.dma('sp', cin[b][:, 3:3 + S], projT_d[4096 + h * 128:4096 + (h + 1) * 128, :],
                          reads=['projT_d'], writes=['cin%d' % b])
                    P.op('act', lambda e, b=b, hh=hh: e.activation(zs[:, hh, :], cin[b][:, 3:3 + S], AF.Silu),
                         reads=['cin%d' % b], writes=[('zs', hh)])
                qkeys = [('qT', hh) for hh in range(HG)]
                kkeys = [('kT', hh) for hh in range(HG)]
                vkeys = [('vT', hh) for hh in range(HG)]
                if DNS == 'conv':
                    return 'STOP'
                P.barrier()
                A.release(conv_mark)
                NB = 2
                ktm = [A.alloc([64, HG, 128]) for i in range(NB)]
                vb = [A.alloc([64, HG, 128]) for i in range(NB)]
                kbg = [A.alloc([64, HG, 128]) for i in range(NB)]
                ktail = [A.alloc([64, HG, 128]) for i in range(NB)]
                dL = [A.alloc([64, HG, 64]) for i in range(NB)]
                dAT = [A.alloc([64, HG, 64]) for i in range(NB)]
                Pm = [A.alloc([64, 2, HG, 64]) for i in range(NB)]
                XT = [A.alloc([64, HG, 64]) for i in range(NB)]
                AinT = [A.alloc([64, HG, 64]) for i in range(NB)]
                Wv = [A.alloc([64, HG, 128]) for i in range(NB)]
                KcT = [A.alloc([128, HG, 64]) for i in range(NB)]
                vnew = A.alloc([64, HG, 128])
                o1 = A.alloc([64, HG, 128])
                osb = A.alloc([64, HG, 128])
                osq = A.alloc([64, HG, 128])
                oss = A.alloc([64, HG])
                identH = A.alloc([64, HG, 64])
                for hh in range(HG):
                    P.op('dve', lambda e, hh=hh: e.tensor_copy(identH[:, hh, :], ident[0:64, 0:64]), reads=['ident'], writes=['identH'])
                ppb = [0]

                def prep_pb():
                    pb = 4 + (ppb[0] % 4)
                    ppb[0] += 1
                    return pb
                for ck in range(NCK):
                    b = ck % NB
                    c0 = ck * 64
                    bk = lambda nm: '%s%d' % (nm, b)
                    pk = prep_pb()

                    def trk(e, pk=pk, c0=c0):
                        ins = None
                        for hh in range(HG):
                            ins = e.transpose(psb[pk][0:64, hh * 128:(hh + 1) * 128], kT[:, hh, c0:c0 + 64], ident[:])
                        return ins
                    P.op('pe', trk, reads=kkeys + ['ident'], writes=[PS(pk)])
                    pv = prep_pb()

                    def trv(e, pv=pv, c0=c0):
                        ins = None
                        for hh in range(HG):
                            ins = e.transpose(psb[pv][0:64, hh * 128:(hh + 1) * 128], vT[:, hh, c0:c0 + 64], ident[:])
                        return ins
                    P.op('pe', trv, reads=vkeys + ['ident'], writes=[PS(pv)])
                    for hh in range(HG):
                        h = hg * HG + hh
                        P.op('dve', lambda e, b=b, hh=hh, h=h, pk=pk, ck=ck: e.tensor_scalar(
                            kbg[b][:, hh, :], psb[pk][0:64, hh * 128:(hh + 1) * 128], bg_tm[:, ck, h:h + 1], None, ALU.mult),
                            reads=[PS(pk)] + scal_keys, writes=[bk('kbg')])
                        P.op('dve', lambda e, b=b, hh=hh, h=h, pk=pk, ck=ck: e.tensor_scalar(
                            ktail[b][:, hh, :], psb[pk][0:64, hh * 128:(hh + 1) * 128], etail_tm[:, ck, h:h + 1], None, ALU.mult),
                            reads=[PS(pk)] + scal_keys, writes=[bk('ktail')])
                        P.op('act', lambda e, b=b, hh=hh, h=h, pv=pv, ck=ck: e.activation(
                            vb[b][:, hh, :], psb[pv][0:64, hh * 128:(hh + 1) * 128], AF.Copy, scale=beta_tm[:, ck, h:h + 1]),
                            reads=[PS(pv)] + scal_keys, writes=[bk('vb')])
                    if DNS == 'p1':
                        return 'STOP'
                    pbc = prep_pb()

                    def mbc(e, pbc=pbc, c0=c0, hg=hg):
                        ins = None
                        for hh in range(HG):
                            h = hg * HG + hh
                            ins = e.matmul(psb[pbc][0:64, hh * 64:(hh + 1) * 64], selrow[:, h, :], gcum[:, c0:c0 + 64], start=True, stop=True)
                        return ins
                    P.op('pe', mbc, reads=['selrow', 'gcum'], writes=[PS(pbc)])
                    for hh in range(HG):
                        h = hg * HG + hh
                        P.op('dve', lambda e, b=b, hh=hh, h=h, pbc=pbc, ck=ck: e.tensor_scalar(
                            dL[b][:, hh, :], psb[pbc][0:64, hh * 64:(hh + 1) * 64], -1.0, gcum_tm[:, ck, h:h + 1], ALU.mult, ALU.add),
                            reads=[PS(pbc)] + scal_keys, writes=[bk('dL')])
                        P.op('dve', lambda e, b=b, hh=hh, h=h, pbc=pbc, ck=ck: e.tensor_scalar(
                            dAT[b][:, hh, :], psb[pbc][0:64, hh * 64:(hh + 1) * 64], gcum_tm[:, ck, h:h + 1], None, ALU.subtract),
                            reads=[PS(pbc)] + scal_keys, writes=[bk('dAT')])
                    mLb = maskL.unsqueeze(1).to_broadcast([64, HG, 64])
                    mAb = maskAT.unsqueeze(1).to_broadcast([64, HG, 64])
                    P.op('dve', lambda e, b=b, mLb=mLb: e.scalar_tensor_tensor(dL[b], dL[b], 0.0, mLb, ALU.min, ALU.add),
                         reads=[bk('dL'), 'maskL'], writes=[bk('dL')])
                    P.op('dve', lambda e, b=b, mAb=mAb: e.scalar_tensor_tensor(dAT[b], dAT[b], 0.0, mAb, ALU.min, ALU.add),
                         reads=[bk('dAT'), 'maskAT'], writes=[bk('dAT')])
                    P.op('act', lambda e, b=b: e.activation(dL[b], dL[b], AF.Exp), reads=[bk('dL')], writes=[bk('dL')])
                    P.op('act', lambda e, b=b: e.activation(dAT[b], dAT[b], AF.Exp), reads=[bk('dAT')], writes=[bk('dAT')])
                    if DNS == 'p2':
                        return 'STOP'
                    pkk = prep_pb()

                    def mkk(e, pkk=pkk, c0=c0):
                        ins = None
                        for hh in range(HG):
                            ins = e.matmul(psb[pkk][0:64, hh * 64:(hh + 1) * 64], kT[:, hh, c0:c0 + 64], kT[:, hh, c0:c0 + 64], start=True, stop=True)
                        for hh in range(HG):
                            ins = e.matmul(psb[pkk][0:64, 256 + hh * 64:256 + (hh + 1) * 64], kT[:, hh, c0:c0 + 64], qT[:, hh, c0:c0 + 64],
                                           start=True, stop=True)
                        return ins
                    P.op('pe', mkk, reads=kkeys + qkeys, writes=[PS(pkk)])
                    if DNS == 'p2a':
                        return 'STOP'
                    for hh in range(HG):
                        h = hg * HG + hh
                        P.op('dve', lambda e, b=b, hh=hh, h=h, pkk=pkk, ck=ck: e.scalar_tensor_tensor(
                            Pm[b][:, 0, hh, :], psb[pkk][0:64, hh * 64:(hh + 1) * 64], nbeta_tm[:, ck, h:h + 1], dL[b][:, hh, :], ALU.mult, ALU.mult),
                            reads=[PS(pkk), bk('dL')] + scal_keys, writes=[bk('Pm')])
                    P.op('dve', lambda e, b=b, pkk=pkk: e.tensor_tensor(AinT[b].rearrange("p a b -> p (a b)"), psb[pkk][0:64, 256:512],
                                                                       dAT[b].rearrange("p a b -> p (a b)"), ALU.mult),
                         reads=[PS(pkk), bk('dAT')], writes=[bk('AinT')])
                    if DNS == 'p2b':
                        return 'STOP'
                    pq = prep_pb()

                    def trq(e, pq=pq, b=b):
                        ins = None
                        for hh in range(HG):
                            ins = e.transpose(psb[pq][0:64, hh * 64:(hh + 1) * 64], Pm[b][:, 0, hh, :], ident[0:64, 0:64])
                        return ins
                    P.op('pe', trq, reads=[bk('Pm'), 'ident'], writes=[PS(pq)])
                    if DNS == 'p2c':
                        return 'STOP'
                    P.op('act', lambda e, b=b, pq=pq: e.copy(Pm[b][:, 1].rearrange("p a b -> p (a b)"), psb[pq][0:64, 0:256]),
                         reads=[PS(pq)], writes=[bk('Pm')])
                    if DNS == 'p2d':
                        return 'STOP'
                    P.op('dve', lambda e, b=b: e.tensor_tensor(XT[b], Pm[b][:, 1], identH, ALU.add),
                         reads=[bk('Pm'), 'identH'], writes=[bk('XT')])
                    if DNS == 'p3':
                        return 'STOP'
                    for lev in range(1, 6):
                        pp = prep_pb()
                        last = (lev == 5)

                        def msq(e, pp=pp, b=b, last=last):
                            ins = None
                            for hh in range(HG):
                                ins = e.matmul(psb[pp][0:64, hh * 64:(hh + 1) * 64], Pm[b][:, 1, hh, :], Pm[b][:, 0, hh, :], start=True, stop=True)
                            if not last:
                                for hh in range(HG):
                                    ins = e.matmul(psb[pp][0:64, 256 + hh * 64:256 + (hh + 1) * 64], Pm[b][:, 0, hh, :], Pm[b][:, 1, hh, :],
                                                   start=True, stop=True)
                            return ins
                        P.op('pe', msq, reads=[bk('Pm')], writes=[PS(pp)])
                        ncol = 256 if last else 512
                        P.op('act', lambda e, b=b, pp=pp, ncol=ncol: e.copy(Pm[b].rearrange("p a b c -> p (a b c)")[:, 0:ncol], psb[pp][0:64, 0:ncol]),
                             reads=[PS(pp)], writes=[bk('Pm')])
                        px = prep_pb()

                        def mx(e, px=px, b=b):
                            ins = None
                            for hh in range(HG):
                                ins = e.matmul(psb[px][0:64, hh * 64:(hh + 1) * 64], Pm[b][:, 0, hh, :], XT[b][:, hh, :], start=True, stop=True)
                            return ins
                        P.op('pe', mx, reads=[bk('Pm'), bk('XT')], writes=[PS(px)])
                        P.op('dve', lambda e, b=b, px=px: e.tensor_tensor(XT[b].rearrange("p a b -> p (a b)"), XT[b].rearrange("p a b -> p (a b)"),
                                                                         psb[px][0:64, 0:256], ALU.add),
                             reads=[PS(px), bk('XT')], writes=[bk('XT')])
                    if DNS == 'p4':
                        return 'STOP'
                    pw = prep_pb()

                    def mwv(e, pw=pw, b=b):
                        ins = None
                        for hh in range(HG):
                            ins = e.matmul(psb[pw][0:64, hh * 128:(hh + 1) * 128], XT[b][:, hh, :], vb[b][:, hh, :], start=True, stop=True)
                        return ins
                    P.op('pe', mwv, reads=[bk('XT'), bk('vb')], writes=[PS(pw)])
                    P.op('act', lambda e, b=b, pw=pw: e.copy(Wv[b].rearrange("p a b -> p (a b)"), psb[pw][0:64, :]),
                         reads=[PS(pw)], writes=[bk('Wv')])
                    pc = prep_pb()

                    def mkc(e, pc=pc, b=b):
                        ins = None
                        for hh in range(HG):
                            ins = e.matmul(psb[pc][:, hh * 64:(hh + 1) * 64], kbg[b][:, hh, :], XT[b][:, hh, :], start=True, stop=True)
                        return ins
                    P.op('pe', mkc, reads=[bk('XT'), bk('kbg')], writes=[PS(pc)])
                    P.op('dve', lambda e, b=b, pc=pc: e.tensor_copy(KcT[b].rearrange("p a b -> p (a b)"), psb[pc][:, 0:256]),
                         reads=[PS(pc)], writes=[bk('KcT')])
                    if DNS == 'prep0':
                        return 'STOP'
                    def r1(e, b=b):
                        ins = None
                        for hh in range(HG):
                            ins = e.matmul(psb[0][0:64, hh * 128:(hh + 1) * 128], KcT[b][:, hh, :], Sst[:, hh, :], start=True, stop=True)
                        return ins
                    P.op('pe', r1, reads=[bk('KcT'), 'Sst'], writes=[PS(0)])
                    P.op('dve', lambda e, b=b: e.tensor_tensor(vnew.rearrange("p a b -> p (a b)"), Wv[b].rearrange("p a b -> p (a b)"),
                                                               psb[0][0:64, :], ALU.subtract),
                         reads=[PS(0), bk('Wv')], writes=['vnew'])

                    def r2(e, b=b, c0=c0):
                        ins = None
                        for hh in range(HG):
                            ins = e.matmul(psb[1][0:64, hh * 128:(hh + 1) * 128], qT[:, hh, c0:c0 + 64], Sst[:, hh, :], start=True, stop=True)
                        for hh in range(HG):
                            ins = e.matmul(psb[2][0:64, hh * 128:(hh + 1) * 128], AinT[b][:, hh, :], vnew[:, hh, :], start=True, stop=True)
                        for hh in range(HG):
                            ins = e.matmul(psb[3][:, hh * 128:(hh + 1) * 128], ktail[b][:, hh, :], vnew[:, hh, :], start=True, stop=True)
                        return ins
                    P.op('pe', r2, reads=qkeys + ['Sst', bk('AinT'), 'vnew', bk('ktail')], writes=[PS(1), PS(2), PS(3)])
                    for hh in range(HG):
                        h = hg * HG + hh
                        P.op('act', lambda e, hh=hh, h=h, ck=ck: e.activation(o1[:, hh, :], psb[1][0:64, hh * 128:(hh + 1) * 128], AF.Copy,
                                                                             scale=egc_tm[:, ck, h:h + 1]),
                             reads=[PS(1)] + scal_keys, writes=['o1'])
                        P.op('dve', lambda e, hh=hh, h=h, ck=ck: e.scalar_tensor_tensor(
                            Sst[:, hh, :], Sst[:, hh, :], eglast[:, ck, h:h + 1], psb[3][:, hh * 128:(hh + 1) * 128], ALU.mult, ALU.add),
                            reads=[PS(3), 'Sst'] + scal_keys, writes=['Sst'])
                    P.op('dve', lambda e: e.tensor_tensor(osb.rearrange("p a b -> p (a b)"), o1.rearrange("p a b -> p (a b)"),
                                                          psb[2][0:64, :], ALU.add),
                         reads=[PS(2), 'o1'], writes=['osb'])
                    if DNS == 'rec0':
                        return 'STOP'
                    P.op('pool', lambda e: e.tensor_tensor(osq, osb, osb, ALU.mult), reads=['osb'], writes=['osq'])
                    P.op('dve', lambda e: e.tensor_reduce(oss, osq, mybir.AxisListType.X, ALU.add), reads=['osq'], writes=['oss'])
                    P.op('dve', lambda e: e.tensor_scalar(oss, oss, 1.0 / 128.0, 1e-6, ALU.mult, ALU.add), reads=['oss'], writes=['oss'])
                    P.op('act', lambda e: e.activation(oss, oss, AF.Sqrt), reads=['oss'], writes=['oss'])
                    P.op('dve', lambda e: e.reciprocal(oss, oss), reads=['oss'], writes=['oss'])
                    P.op('dve', lambda e: e.tensor_tensor(osb, osb, oss.unsqueeze(2).to_broadcast([64, HG, 128]), ALU.mult),
                         reads=['osb', 'oss'], writes=['osb'])
                    po = prep_pb()

                    def tro(e, po=po):
                        ins = None
                        for hh in range(HG):
                            ins = e.transpose(psb[po][:, hh * 64:(hh + 1) * 64], osb[:, hh, :], ident[0:64, 0:64])
                        return ins
                    P.op('pe', tro, reads=['osb', 'ident'], writes=[PS(po)])
                    P.op('dve', lambda e, po=po, c0=c0: e.scalar_tensor_tensor(
                        yT[:, :, c0:c0 + 64], psb[po][:, 0:256].rearrange("p (a b) -> p a b", a=HG), normw[:, 0:1], zs[:, :, c0:c0 + 64],
                        ALU.mult, ALU.mult),
                        reads=[PS(po), 'normw'] + [('zs', hh) for hh in range(HG)], writes=['yT'])
                for hh in range(HG):
                    h = hg * HG + hh
                    P.dma('sp', ycatT_d[1024 + h * 128:1024 + (h + 1) * 128, :], yT[:, hh, :], reads=['yT'], writes=['ycatT_d'])
            if stop_after == 'D':
                return 'STOP'

            return None
        if phase_2() == 'STOP':
            return finish_debug(C)
        def phase_3(l=l, mo=mo):
            P.barrier()
            A.reset()
            ycat = A.alloc([128, NCH, S], BF16)
            ycat_v = ycatT_d.rearrange("(k p) t -> p k t", p=128)
            for k in range(NCH):
                P.dma('sp', ycat[:, k, :], ycat_v[:, k, :], reads=['ycatT_d'], writes=[('ycat', k)])
            ykeys = [('ycat', k) for k in range(NCH)]
            if stop_after == 'E0':
                return 'STOP'
            wt = [A.alloc([128, NCH, 512], BF16) for i in range(2)]
            xb = [A.alloc([128, 512]) for i in range(3)]
            rb = [A.alloc([128, 512]) for i in range(3)]
            w_out_v = w_out[l].rearrange("(k p) n -> p k n", p=128)
            cnt = 0
            for ng in range(4):
                b = ng % 2
                if not _os.environ.get('E_NOPOOL'):
                    P.dma('pool', wt[b], w_out_v[:, :, ng * 512:(ng + 1) * 512], writes=['wt%d' % b])
                for nc_ in range(4):
                    dch = ng * 4 + nc_
                    for t4 in range(4):
                        pb = cnt % 4
                        xi = cnt % 3
                        cnt += 1

                        def mm(e, b=b, nc_=nc_, t4=t4, pb=pb):
                            ins = None
                            for k in range(NCH):
                                ins = e.matmul(psb[pb][:], wt[b][:, k, nc_ * 128:(nc_ + 1) * 128],
                                               ycat[:, k, t4 * 512:(t4 + 1) * 512], start=(k == 0), stop=(k == NCH - 1))
                            return ins
                        if not _os.environ.get('E_NOPE'):
                            P.op('pe', mm, reads=['wt%d' % b] + ykeys, writes=[PS(pb)])
                        if not _os.environ.get('E_NOX'):
                            P.dma(_os.environ.get('E_XQ', 'sp'), xb[xi], xT_d[dch * 128:(dch + 1) * 128, t4 * 512:(t4 + 1) * 512],
                                  reads=['xT_d'], writes=['xb%d' % xi])
                        if not _os.environ.get('E_NOACT'):
                            P.op('act', lambda e, xi=xi: e.activation(xb[xi], xb[xi], AF.Copy, scale=ALPHA),
                                 reads=['xb%d' % xi], writes=['xb%d' % xi])
                        gt1 = modT[:, mo + 32 + dch:mo + 32 + dch + 1]
                        if not _os.environ.get('E_NODVE'):
                            P.op('dve', lambda e, xi=xi, pb=pb, gt1=gt1: e.scalar_tensor_tensor(rb[xi], psb[pb][:], gt1, xb[xi], ALU.mult, ALU.add),
                                 reads=[PS(pb), 'xb%d' % xi, 'modT'], writes=['rb%d' % xi])
                        if not _os.environ.get('E_NOSTORE'):
                            P.dma('sp', rT_d[dch * 128:(dch + 1) * 128, t4 * 512:(t4 + 1) * 512], rb[xi],
                                  reads=['rb%d' % xi], writes=['rT_d'])
            if stop_after == 'E1':
                return 'STOP'
            P.barrier()
            A.reset()
            L = ln_alloc()
            x1t = [A.alloc([128, NCH, TB]) for i in range(2)]
            hft = [A.alloc([128, NCH, TB], BF16) for i in range(2)]
            rT_v = rT_d.rearrange("(j p) t -> p j t", p=128)
            hfT_v = hfT_d.rearrange("(j p) t -> p j t", p=128)
            lo = l * 64
            for tb in range(S // TB):
                b = tb % 2
                P.dma('sp', L.x[b], rT_v[:, :, tb * TB:(tb + 1) * TB], reads=['rT_d'], writes=['ln_x%d' % b])
                ln_block(L, L.x[b], 'ln_x%d' % b,
                         lambda j: lnp[:, lo + j:lo + j + 1],
                         lambda j: lnp[:, lo + 16 + j:lo + 16 + j + 1],
                         lambda j, b=b: x1t[b][:, j, :],
                         lambda j, b=b: 'x1t%d' % b)
                P.dma('sp', xT_v[:, :, tb * TB:(tb + 1) * TB], x1t[b], reads=['x1t%d' % b], writes=['xT_d'])
                ln_block(L, x1t[b], 'x1t%d' % b,
                         lambda j: modT[:, mo + 64 + j:mo + 64 + j + 1],
                         lambda j: modT[:, mo + 48 + j:mo + 48 + j + 1],
                         lambda j, b=b: hft[b][:, j, :],
                         lambda j, b=b: 'hft%d' % b)
                P.dma('sp', hfT_v[:, :, tb * TB:(tb + 1) * TB], hft[b], reads=['hft%d' % b], writes=['hfT_d'])
            if stop_after == 'E':
                return 'STOP'

            return None
        if phase_3() == 'STOP':
            return finish_debug(C)
        def phase_4(l=l, mo=mo):
            P.barrier()
            A.reset()
            hf = A.alloc([128, NCH, S], BF16)
            hfT_v = hfT_d.rearrange("(k p) t -> p k t", p=128)
            for k in range(NCH):
                P.dma('sp', hf[:, k, :], hfT_v[:, k, :], reads=['hfT_d'], writes=[('hf', k)])
            hkeys = [('hf', k) for k in range(NCH)]
            wt = [A.alloc([128, NCH, 512], BF16) for i in range(2)]
            stg = [A.alloc([128, 512]) for i in range(4)]
            wq_v = wq_d[l].rearrange("(k p) n -> p k n", p=128)
            cnt = 0
            for ng in range(4):
                b = ng % 2
                P.dma('pool', wt[b], wq_v[:, :, ng * 512:(ng + 1) * 512], writes=['wt%d' % b])
                for nc_ in range(4):
                    for t4 in range(4):
                        pb = cnt % 4
                        cnt += 1

                        def mm(e, b=b, nc_=nc_, t4=t4, pb=pb):
                            ins = None
                            for k in range(NCH):
                                ins = e.matmul(psb[pb][:], wt[b][:, k, nc_ * 128:(nc_ + 1) * 128],
                                               hf[:, k, t4 * 512:(t4 + 1) * 512], start=(k == 0), stop=(k == NCH - 1))
                            return ins
                        P.op('pe', mm, reads=['wt%d' % b] + hkeys, writes=[PS(pb)])
                        if pb % 2 == 0:
                            P.op('act', lambda e, pb=pb: e.copy(stg[pb], psb[pb][:]), reads=[PS(pb)], writes=['stg%d' % pb])
                        else:
                            P.op('dve', lambda e, pb=pb: e.tensor_copy(stg[pb], psb[pb][:]), reads=[PS(pb)], writes=['stg%d' % pb])
                        r0 = ng * 512 + nc_ * 128
                        P.dma('sp', qT_d[r0:r0 + 128, t4 * 512:(t4 + 1) * 512], stg[pb], reads=['stg%d' % pb], writes=['qT_d'])
            P.barrier()
            A.reset()
            NEGB = -1.0e30
            U32 = mybir.dt.uint32
            keysT = A.alloc([128, 16, 128])
            P.dma('sp', keysT, keysT_d[l].rearrange("p (c k) -> p c k", c=16), writes=['keysT'])
            iota = A.alloc([128, 128])
            P.dma('sp', iota, iota_d, writes=['iota'])
            qt = [A.alloc([128, 16, 128]) for i in range(2)]
            sc = A.alloc([128, 16, 128])
            sc2 = A.alloc([128, 16, 128])
            tv = A.alloc([128, 16, 16])
            tiu = A.alloc([128, 16, 16]).bitcast(U32)
            tif = A.alloc([128, 16, 16])
            cand = A.alloc([128, 8, 256])
            cand2 = A.alloc([128, 8, 256])
            bv = A.alloc([128, 8, 16])
            bpu = A.alloc([128, 8, 16]).bitcast(U32)
            rcu = A.alloc([128, 2, 8, 16]).bitcast(U32)
            rcf = A.alloc([128, 2, 8, 16])
            eq = A.alloc([128, 8, 16, 16])
            sel = A.alloc([128, 3, 8, 16])
            gz = A.alloc([128, 8])
            selT = A.alloc([128, 3, 128])
            OJ = A.alloc([128, 128, 128], BF16)
            OI = A.alloc([128, 128, 128], BF16)
            X = [A.alloc([128, 128, 128], BF16) for i in range(2)]
            qT_v = qT_d.rearrange("(c p) t -> p c t", p=128)
            Gd_v = Gd.rearrange("i j t -> j i t")
            evn = 0
            for tt in range(S // 128):
                b = tt % 2
                P.dma('sp', qt[b], qT_v[:, :, tt * 128:(tt + 1) * 128], reads=['qT_d'], writes=['qt%d' % b])
                for c4 in range(4):
                    def msc(e, b=b, c4=c4):
                        ins = None
                        for cc in range(4):
                            c = c4 * 4 + cc
                            ins = e.matmul(psb[c4][:, cc * 128:(cc + 1) * 128], qt[b][:, c, :], keysT[:, c, :], start=True, stop=True)
                        return ins
                    P.op('pe', msc, reads=['qt%d' % b, 'keysT'], writes=[PS(c4)])
                    if c4 % 2 == 0:
                        P.op('act', lambda e, c4=c4: e.copy(sc[:, c4 * 4:(c4 + 1) * 4, :].rearrange("p a b -> p (a b)"), psb[c4][:]),
                             reads=[PS(c4)], writes=[('sc', c4)])
                    else:
                        P.op('dve', lambda e, c4=c4: e.tensor_copy(sc[:, c4 * 4:(c4 + 1) * 4, :].rearrange("p a b -> p (a b)"), psb[c4][:]),
                             reads=[PS(c4)], writes=[('sc', c4)])
                for c in range(16):
                    sk = ('sc', c // 4)
                    P.op('dve', lambda e, c=c: e.max(tv[:, c, 0:8], sc[:, c, :]), reads=[sk], writes=[('tv', c)])
                    P.op('dve', lambda e, c=c: e.max_index(tiu[:, c, 0:8], tv[:, c, 0:8], sc[:, c, :]), reads=[sk, ('tv', c)], writes=[('tiu', c)])
                    P.op('dve', lambda e, c=c: e.match_replace(sc2[:, c, :], tv[:, c, 0:8], sc[:, c, :], NEGB), reads=[sk, ('tv', c)], writes=[('sc2', c)])
                    P.op('dve', lambda e, c=c: e.max(tv[:, c, 8:16], sc2[:, c, :]), reads=[('sc2', c)], writes=[('tv', c)])
                    P.op('dve', lambda e, c=c: e.max_index(tiu[:, c, 8:16], tv[:, c, 8:16], sc2[:, c, :]), reads=[('sc2', c), ('tv', c)], writes=[('tiu', c)])
                tvk = [('tv', c) for c in range(16)]
                tik = [('tiu', c) for c in range(16)]
                P.op('dve', lambda e: e.tensor_copy(tif, tiu), reads=tik, writes=['tif'])
                tv4 = tv.rearrange("p (h two) k -> p h two k", two=2)
                P.op('dve', lambda e, tv4=tv4: e.tensor_tensor(
                    cand.rearrange("p h (r c) -> p h r c", r=16),
                    tv4[:, :, 0, :].unsqueeze(3).to_broadcast([128, 8, 16, 16]),
                    tv4[:, :, 1, :].unsqueeze(2).to_broadcast([128, 8, 16, 16]), ALU.add),
                    reads=tvk, writes=['cand'])
                for h in range(8):
                    P.op('dve', lambda e, h=h: e.max(bv[:, h, 0:8], cand[:, h, :]), reads=['cand'], writes=[('bv', h)])
                    P.op('dve', lambda e, h=h: e.max_index(bpu[:, h, 0:8], bv[:, h, 0:8], cand[:, h, :]), reads=['cand', ('bv', h)], writes=[('bpu', h)])
                    P.op('dve', lambda e, h=h: e.match_replace(cand2[:, h, :], bv[:, h, 0:8], cand[:, h, :], NEGB), reads=['cand', ('bv', h)], writes=[('cand2', h)])
                    P.op('dve', lambda e, h=h: e.max(bv[:, h, 8:16], cand2[:, h, :]), reads=[('cand2', h)], writes=[('bv', h)])
                    P.op('dve', lambda e, h=h: e.max_index(bpu[:, h, 8:16], bv[:, h, 8:16], cand2[:, h, :]), reads=[('cand2', h), ('bv', h)], writes=[('bpu', h)])
                bvk = [('bv', h) for h in range(8)]
                bpk = [('bpu', h) for h in range(8)]
                P.op('dve', lambda e: e.tensor_scalar(rcu[:, 0], bpu, 4, None, ALU.logical_shift_right), reads=bpk, writes=['rcu0'])
                P.op('dve', lambda e: e.tensor_scalar(rcu[:, 1], bpu, 15, None, ALU.bitwise_and), reads=bpk, writes=['rcu1'])
                P.op('dve', lambda e: e.tensor_copy(rcf, rcu), reads=['rcu0', 'rcu1'], writes=['rcf'])
                tif4 = tif.rearrange("p (h two) k -> p h two k", two=2)
                io16 = iota[:, 0:16].unsqueeze(1).unsqueeze(1).to_broadcast([128, 8, 16, 16])
                for w in range(2):
                    P.op('dve', lambda e, w=w, io16=io16: e.tensor_tensor(eq, rcf[:, w].unsqueeze(3).to_broadcast([128, 8, 16, 16]), io16, ALU.is_equal),
                         reads=['rcf', 'iota'], writes=['eq'])
                    P.op('dve', lambda e, w=w, tif4=tif4: e.tensor_tensor(eq, eq, tif4[:, :, w, :].unsqueeze(2).to_broadcast([128, 8, 16, 16]), ALU.mult),
                         reads=['eq', 'tif'], writes=['eq'])
                    P.op('dve', lambda e, w=w: e.tensor_reduce(sel[:, w], eq, mybir.AxisListType.X, ALU.add), reads=['eq'], writes=[('sel', w)])
                P.op('dve', lambda e: e.tensor_tensor(sel[:, 2], bv, bv[:, :, 0:1].to_broadcast([128, 8, 16]), ALU.subtract),
                     reads=bvk, writes=[('sel', 2)])
                P.op('act', lambda e: e.activation(sel[:, 2], sel[:, 2], AF.Exp), reads=[('sel', 2)], writes=[('sel', 2)])
                P.op('dve', lambda e: e.tensor_reduce(gz, sel[:, 2], mybir.AxisListType.X, ALU.add), reads=[('sel', 2)], writes=['gz'])
                P.op('dve', lambda e: e.reciprocal(gz, gz), reads=['gz'], writes=['gz'])
                P.op('dve', lambda e: e.tensor_tensor(sel[:, 2], sel[:, 2], gz.unsqueeze(2).to_broadcast([128, 8, 16]), ALU.mult),
                     reads=[('sel', 2), 'gz'], writes=[('sel', 2)])
                def trs(e):
                    ins = None
                    for w in range(3):
                        ins = e.transpose(psb[4][:, w * 128:(w + 1) * 128], sel[:, w].rearrange("p h k -> p (h k)"), ident[:])
                    return ins
                P.op('pe', trs, reads=[('sel', 0), ('sel', 1), ('sel', 2), 'ident'], writes=[PS(4)])
                P.op('act', lambda e: e.copy(selT.rearrange("p a b -> p (a b)"), psb[4][:, 0:384]), reads=[PS(4)], writes=['selT'])
                iob = iota.unsqueeze(1).to_broadcast([128, 128, 128])
                P.op('dve', lambda e, iob=iob: e.tensor_tensor(OJ, iob, selT[:, 1, :].unsqueeze(2).to_broadcast([128, 128, 128]), ALU.is_equal),
                     reads=['selT', 'iota'], writes=['OJ'])
                P.op('dve', lambda e, iob=iob: e.tensor_tensor(OI, iob, selT[:, 0, :].unsqueeze(2).to_broadcast([128, 128, 128]), ALU.is_equal),
                     reads=['selT', 'iota'], writes=['OI'])
                P.op('pool', lambda e: e.tensor_tensor(OI, OI, selT[:, 2, :].unsqueeze(2).to_broadcast([128, 128, 128]), ALU.mult),
                     reads=['selT', 'OI'], writes=['OI'])
                Xv = X[b].rearrange("p i t -> p t i")
                for t4 in range(32):
                    pb = 5 + (t4 % 3)

                    def mg(e, pb=pb, t4=t4):
                        ins = None
                        for q_ in range(4):
                            t = t4 * 4 + q_
                            ins = e.matmul(psb[pb][:, q_ * 128:(q_ + 1) * 128], OJ[:, t, :], OI[:, t, :], start=True, stop=True)
                        return ins
                    P.op('pe', mg, reads=['OJ', 'OI'], writes=[PS(pb)])
                    src_ps = psb[pb][:].rearrange("p (a b) -> p a b", a=4)
                    dst = Xv[:, t4 * 4:(t4 + 1) * 4, :]
                    if evn % 2 == 0:
                        P.op('act', lambda e, dst=dst, src_ps=src_ps: e.copy(dst, src_ps), reads=[PS(pb)], writes=[('X', b, t4)])
                    else:
                        P.op('dve', lambda e, dst=dst, src_ps=src_ps: e.tensor_copy(dst, src_ps), reads=[PS(pb)], writes=[('X', b, t4)])
                    evn += 1
                xkeys = [('X', b, t4) for t4 in range(32)]
                for iq in range(4):
                    P.dma('act', Gd_v[:, iq * 32:(iq + 1) * 32, tt * 128:(tt + 1) * 128], X[b][:, iq * 32:(iq + 1) * 32, :],
                          reads=xkeys, writes=['Gd'])
            if stop_after == 'F3':
                return 'STOP'
            P.barrier()
            A.reset()
            TP = 512
            hfb = A.alloc([128, NCH, TP], BF16)
            acc = A.alloc([128, NCH, TP])
            ut = [A.alloc([128, NCH, 512], BF16) for i in range(2)]
            vt = [A.alloc([128, 16, 512], BF16) for i in range(2)]
            PT = A.alloc([128, 16, TP], BF16)
            gl = [A.alloc([128, TP], BF16) for i in range(4)]
            gt_ = [A.alloc([128, TP], BF16) for i in range(4)]
            xb = [A.alloc([128, TP]) for i in range(2)]
            rb = [A.alloc([128, TP]) for i in range(2)]
            uT_v = uT_d[l].rearrange("(k p) e -> p k e", p=128)
            v_v = v_d[l].rearrange("(g ic j) d -> g j ic d", ic=16, j=128)
            ui = 0
            vi = 0
            gi_ = 0
            for ps_ in range(S // TP):
                t0 = ps_ * TP
                for k in range(NCH):
                    P.dma('sp', hfb[:, k, :], hfT_v[:, k, t0:t0 + TP], reads=['hfT_d'], writes=[('hfb', k)])
                hbk = [('hfb', k) for k in range(NCH)]
                for eg in range(8):
                    for ib in range(4):
                        ub = ui % 2
                        ui += 1
                        e0 = eg * 2048 + ib * 512
                        P.dma('pool', ut[ub], uT_v[:, :, e0:e0 + 512], writes=['ut%d' % ub])
                        for i4 in range(4):
                            ic = ib * 4 + i4
                            i_abs = eg * 16 + ic
                            gb = gi_ % 4
                            gi_ += 1
                            pb = gb % 4
                            P.dma('act', gt_[gb], Gd[i_abs, :, t0:t0 + TP], reads=['Gd'], writes=['gt%d' % gb])

                            def ms(e, ub=ub, i4=i4, pb=pb):
                                ins = None
                                for k in range(NCH):
                                    ins = e.matmul(psb[pb][:], ut[ub][:, k, i4 * 128:(i4 + 1) * 128], hfb[:, k, :],
                                                   start=(k == 0), stop=(k == NCH - 1))
                                return ins
                            P.op('pe', ms, reads=['ut%d' % ub] + hbk, writes=[PS(pb)])
                            P.op('act', lambda e, gb=gb, pb=pb: e.activation(gl[gb], psb[pb][:], AF.Gelu_apprx_tanh),
                                 reads=[PS(pb)], writes=['gl%d' % gb])
                            P.op('dve', lambda e, gb=gb, ic=ic: e.tensor_tensor(PT[:, ic, :], gl[gb], gt_[gb], ALU.mult),
                                 reads=['gl%d' % gb, 'gt%d' % gb], writes=[('PT', ic)])
                    ptk = [('PT', ic) for ic in range(16)]
                    for dq in range(4):
                        vb_ = vi % 2
                        vi += 1
                        P.dma('pool', vt[vb_], v_v[eg][:, :, dq * 512:(dq + 1) * 512], writes=['vt%d' % vb_])
                        for d4 in range(4):
                            dch = dq * 4 + d4
                            pb = 4 + (dch % 4)

                            def mv(e, vb_=vb_, d4=d4, pb=pb):
                                ins = None
                                for ic in range(16):
                                    ins = e.matmul(psb[pb][:], vt[vb_][:, ic, d4 * 128:(d4 + 1) * 128], PT[:, ic, :],
                                                   start=(ic == 0), stop=(ic == 15))
                                return ins
                            P.op('pe', mv, reads=['vt%d' % vb_] + ptk, writes=[PS(pb)])
                            if eg == 0:
                                P.op('act', lambda e, dch=dch, pb=pb: e.copy(acc[:, dch, :], psb[pb][:]), reads=[PS(pb)], writes=[('acc', dch)])
                            else:
                                P.op('dve', lambda e, dch=dch, pb=pb: e.tensor_tensor(acc[:, dch, :], acc[:, dch, :], psb[pb][:], ALU.add),
                                     reads=[PS(pb), ('acc', dch)], writes=[('acc', dch)])
                for dch in range(NCH):
                    xi = dch % 2
                    P.dma('sp', xb[xi], xT_d[dch * 128:(dch + 1) * 128, t0:t0 + TP], reads=['xT_d'], writes=['xb%d' % xi])
                    P.op('act', lambda e, xi=xi: e.activation(xb[xi], xb[xi], AF.Copy, scale=ALPHA), reads=['xb%d' % xi], writes=['xb%d' % xi])
                    gt2 = modT[:, mo + 80 + dch:mo + 80 + dch + 1]
                    P.op('dve', lambda e, xi=xi, dch=dch, gt2=gt2: e.scalar_tensor_tensor(rb[xi], acc[:, dch, :], gt2, xb[xi], ALU.mult, ALU.add),
                         reads=[('acc', dch), 'xb%d' % xi, 'modT'], writes=['rb%d' % xi])
                    P.dma('sp', rT_d[dch * 128:(dch + 1) * 128, t0:t0 + TP], rb[xi], reads=['rb%d' % xi], writes=['rT_d'])
            if stop_after == 'F4':
                return 'STOP'
            P.barrier()
            A.reset()
            L = ln_alloc()
            x2t = [A.alloc([128, NCH, TB]) for i in range(2)]
            rT_v = rT_d.rearrange("(j p) t -> p j t", p=128)
            lo = l * 64 + 32
            for tb in range(S // TB):
                b = tb % 2
                P.dma('sp', L.x[b], rT_v[:, :, tb * TB:(tb + 1) * TB], reads=['rT_d'], writes=['ln_x%d' % b])
                ln_block(L, L.x[b], 'ln_x%d' % b,
                         lambda j: lnp[:, lo + j:lo + j + 1],
                         lambda j: lnp[:, lo + 16 + j:lo + 16 + j + 1],
                         lambda j, b=b: x2t[b][:, j, :],
                         lambda j, b=b: 'x2t%d' % b)
                P.dma('sp', xT_v[:, :, tb * TB:(tb + 1) * TB], x2t[b], reads=['x2t%d' % b], writes=['xT_d'])
            if stop_after == 'G':
                return 'STOP'

            return None
        if phase_4() == 'STOP':
            return finish_debug(C)
    def phase_z():
        P.barrier()
        A.reset()
        xf = [A.alloc([128, NCH, 128]) for i in range(2)]
        xo = [A.alloc([128, D]) for i in range(2)]
        ev = 0
        for tt in range(S // 128):
            b = tt % 2
            P.dma('sp', xf[b], xT_v[:, :, tt * 128:(tt + 1) * 128], reads=['xT_d'], writes=['xf%d' % b])
            for g in range(4):
                pb = g % 2

                def tr(e, b=b, g=g, pb=pb):
                    ins = None
                    for jj in range(4):
                        j = g * 4 + jj
                        ins = e.transpose(psb[pb][:, jj * 128:(jj + 1) * 128], xf[b][:, j, :], ident[:])
                    return ins
                P.op('pe', tr, reads=['xf%d' % b, 'ident'], writes=[PS(pb)])
                dst = xo[b][:, g * 512:(g + 1) * 512]
                if ev % 2 == 0:
                    P.op('act', lambda e, dst=dst, pb=pb: e.copy(dst, psb[pb][:]), reads=[PS(pb)], writes=[('xo', b, g)])
                else:
                    P.op('dve', lambda e, dst=dst, pb=pb: e.tensor_copy(dst, psb[pb][:]), reads=[PS(pb)], writes=[('xo', b, g)])
                ev += 1
            P.dma('sp', out_d[tt * 128:(tt + 1) * 128, :], xo[b], reads=[('xo', b, g) for g in range(4)], writes=['out'])
    phase_z()
    return finish_debug(C)


def finish_debug(C):
    C.P.wait_all('sp', C.dram_written)
    C.P.finish()
    return C.nc


def make_in_maps(inputs, n_cores=8):
    f32 = np.float32
    maps = []
    ident = np.eye(128, dtype=f32)
    b_ada_col = np.ascontiguousarray(
        np.asarray(inputs['b_ada'], f32).reshape(DEPTH, 96, 128).transpose(2, 0, 1).reshape(128, DEPTH * 96))
    lam_re = np.asarray(inputs['ssm_lam_re'], f32)
    lam_im = np.asarray(inputs['ssm_lam_im'], f32)
    lstep = np.asarray(inputs['ssm_log_step'], f32)
    s5A = np.zeros((DEPTH, 128, 3, 64), f32)
    for l in range(DEPTH):
        for h in range(2):
            s5A[l, h * 64:(h + 1) * 64, 0, :] = lam_re[l].T
            s5A[l, h * 64:(h + 1) * 64, 1, :] = lam_im[l].T
        s5A[l, :, 2, :] = lstep[l][None, :]
    s5A = s5A.reshape(DEPTH, 128, 192)
    b_re = np.asarray(inputs['ssm_b_re'], f32)
    b_im = np.asarray(inputs['ssm_b_im'], f32)
    braw = np.zeros((DEPTH, 64, 128, 128), f32)
    for g in range(64):
        r0 = 16 * (g % 8)
        braw[:, g, r0:r0 + 16, 0:64] = b_re[:, g].transpose(0, 2, 1)
        braw[:, g, r0:r0 + 16, 64:128] = b_im[:, g].transpose(0, 2, 1)
    c_re = np.asarray(inputs['ssm_c_re'], f32)
    c_im = np.asarray(inputs['ssm_c_im'], f32)
    crci = np.zeros((DEPTH, 128, 2, 64, 16), f32)
    for h in range(2):
        crci[:, h * 64:(h + 1) * 64, 0] = c_re.transpose(0, 3, 1, 2)
        crci[:, h * 64:(h + 1) * 64, 1] = c_im.transpose(0, 3, 1, 2)
    crci = crci.reshape(DEPTH, 128, 2048)
    dskip_col = np.ascontiguousarray(
        np.asarray(inputs['ssm_d'], f32).reshape(DEPTH, 8, 128).transpose(2, 0, 1).reshape(128, DEPTH * 8))
    sk = np.asarray(inputs['peer_sub_keys'], f32)
    keysT = np.ascontiguousarray(sk.reshape(DEPTH, 16, 128, 128).transpose(0, 3, 1, 2).reshape(DEPTH, 128, 2048))
    peer_uT = np.ascontiguousarray(np.asarray(inputs['peer_u'], f32).transpose(0, 2, 1))
    iota128 = np.tile(np.arange(128, dtype=f32)[None, :], (128, 1))
    lnp_col = np.zeros((128, DEPTH, 4, 16), f32)
    for i_, nm in enumerate(['ln1_g', 'ln1_b', 'ln2_g', 'ln2_b']):
        lnp_col[:, :, i_, :] = np.asarray(inputs[nm], f32).reshape(DEPTH, 16, 128).transpose(2, 0, 1)
    lnp_col = lnp_col.reshape(128, DEPTH * 64)
    cwv = np.asarray(inputs['dn_conv_w'], f32)
    convw_col = np.ascontiguousarray(cwv.reshape(DEPTH, 4, 24, 128).transpose(0, 3, 2, 1).reshape(DEPTH, 128, 96))
    dnhp = np.zeros((8, DEPTH * 2), f32)
    dnhp[:, 0::2] = np.asarray(inputs['dn_a_log'], f32).T
    dnhp[:, 1::2] = np.asarray(inputs['dn_dt_bias'], f32).T
    normw_col = np.ascontiguousarray(np.asarray(inputs['dn_norm_w'], f32).T)
    ii = np.arange(64)
    NEG = -30000.0
    maskL = np.where(ii[None, :] < ii[:, None], 0.0, NEG).astype(f32)
    maskAT = np.where(ii[:, None] <= ii[None, :], 0.0, NEG).astype(f32)
    selrow = np.zeros((8, 8, 64), f32)
    for h in range(8):
        selrow[h, h, :] = 1.0
    selrow = selrow.reshape(8, 512)
    sel63 = np.zeros((64, 128), f32)
    sel63[63, :] = 1.0
    cmask = np.ones((8, S), f32)
    cmask[:, 0::64] = 0.0
    for c in range(n_cores):
        b = c % 4
        m = {
            'x': np.ascontiguousarray(np.asarray(inputs['x'][b], f32)),
            'c_col': np.ascontiguousarray(np.asarray(inputs['c'][b], f32).reshape(NCH, 128).T),
            'w_ada': np.asarray(inputs['w_ada'], f32),
            'b_ada_col': b_ada_col,
            'w_in': np.asarray(inputs['w_in'], f32),
            'ident': ident,
            's5A': s5A, 'braw': braw, 'crci': crci, 'dskip_col': dskip_col,
            'ssm_w_glu': np.asarray(inputs['ssm_w_glu'], f32),
            'convw_col': convw_col, 'dnhp': dnhp, 'normw_col': normw_col, 'maskL': maskL, 'maskAT': maskAT,
            'selrow': selrow, 'sel63': sel63, 'cmask': cmask,
            'w_out': np.asarray(inputs['w_out'], f32), 'lnp_col': lnp_col,
            'peer_w_query': np.asarray(inputs['peer_w_query'], f32), 'keysT': keysT,
            'peer_uT': peer_uT, 'peer_v': np.asarray(inputs['peer_v'], f32), 'iota128': iota128,
        }
        maps.append(m)
    return maps


def kernel(**inputs):
    nc = build()
    in_maps = make_in_maps(inputs)
    res = run_bass_kernel_spmd(nc, in_maps, core_ids=list(range(8)))
    out = np.stack([np.asarray(res.results[b]['out']) for b in range(4)], axis=0)
    return out.astype(np.float32)
```

```python
import math
import os as _os
import numpy as np
from contextlib import ExitStack
import concourse.bass as bass
import concourse.mybir as mybir
from concourse.bass_utils import run_bass_kernel_spmd

F32 = mybir.dt.float32
BF16 = mybir.dt.bfloat16
AF = mybir.ActivationFunctionType
ALU = mybir.AluOpType

D = 2048
S = 2048
DEPTH = 4
NCH = 16
N_IN = 5136
ALPHA = (2.0 * DEPTH) ** 0.25
LN_EPS = 1e-5

ENGS = ['pe', 'act', 'dve', 'pool', 'sp']
NDS = 8


class Prog:
    def __init__(self, nc, es):
        self.nc = nc
        self.es = es
        self.ops = {e: [] for e in ENGS}
        self.sem = {e: es.enter_context(nc.semaphore('s_' + e)) for e in ENGS}
        self.cnt = {e: 0 for e in ENGS}
        self.seen = {e: {} for e in ENGS}
        self.res = {}
        self.dsem = {e: [es.enter_context(nc.semaphore('d_%s%d' % (e, i))) for i in range(NDS)]
                     for e in ('sp', 'pool', 'act')}
        self.dcnt = {e: [0] * NDS for e in ('sp', 'pool', 'act')}
        self.dnext = {e: 0 for e in ('sp', 'pool', 'act')}
        self.n_ins = 0
        self.rec = None

    def replay(self, item):
        rec, self.rec = self.rec, None
        try:
            if item[0] == 'op':
                self.op(*item[1:])
            else:
                self.dma(item[1], item[2], item[3], item[4], item[5], **item[6])
        finally:
            self.rec = rec

    def replay_merged(self, a, b, ratio=4):
        ia = ib = 0
        while ia < len(a) or ib < len(b):
            for _ in range(ratio):
                if ia < len(a):
                    self.replay(a[ia])
                    ia += 1
            if ib < len(b):
                self.replay(b[ib])
                ib += 1

    def _deps(self, eng, reads, writes):
        toks = []
        for r in reads:
            st = self.res.get(r)
            if st is not None and st['w'] is not None:
                toks.append(st['w'])
        for w in writes:
            st = self.res.get(w)
            if st is not None:
                if st['w'] is not None:
                    toks.append(st['w'])
                toks.extend(st['r'].values())
        waits = []
        for (key, s, v, e) in toks:
            if e == 'pe' and eng == 'pe':
                continue
            if self.seen[eng].get(key, 0) >= v:
                continue
            self.seen[eng][key] = v
            waits.append((s, v))
        return waits

    def _update(self, tok, reads, writes):
        for r in reads:
            st = self.res.setdefault(r, {'w': None, 'r': {}})
            st['r'][tok[0]] = tok
        for w in writes:
            self.res[w] = {'w': tok, 'r': {}}

    def op(self, eng, fn, reads=(), writes=()):
        if self.rec is not None:
            self.rec.append(('op', eng, fn, list(reads), list(writes)))
            return None
        waits = self._deps(eng, reads, writes)
        self.cnt[eng] += 1
        mysem = self.sem[eng]
        tok = ('e_' + eng, mysem, self.cnt[eng], eng)

        def run(eo, waits=waits, fn=fn, mysem=mysem):
            for (s, v) in waits:
                eo.wait_ge(s, v)
            ins = fn(eo)
            ins.then_inc(mysem, 1)
        self.ops[eng].append(run)
        self._update(tok, reads, writes)
        self.n_ins += 1
        return tok

    def dma(self, q, out, in_, reads=(), writes=(), **kw):
        if self.rec is not None:
            self.rec.append(('dma', q, out, in_, list(reads), list(writes), kw))
            return None
        waits = self._deps(q, reads, writes)
        j = self.dnext[q]
        self.dnext[q] = (j + 1) % NDS
        s = self.dsem[q][j]
        prev = self.dcnt[q][j]
        key = 'd_%s%d' % (q, j)
        if prev > 0 and self.seen[q].get(key, 0) < prev:
            waits.append((s, prev))
            self.seen[q][key] = prev
        self.dcnt[q][j] = prev + 16
        tok = (key, s, prev + 16, None)

        def run(eo, waits=waits, s=s, out=out, in_=in_, kw=kw):
            for (ws, v) in waits:
                eo.wait_ge(ws, v)
            eo.dma_start(out=out, in_=in_, **kw).then_inc(s, 16)
        self.ops[q].append(run)
        self._update(tok, reads, writes)
        self.n_ins += 1
        return tok

    def barrier(self):
        allw = []
        for x in ENGS:
            if self.cnt[x] > 0:
                allw.append(('e_' + x, self.sem[x], self.cnt[x], x))
        for q in self.dsem:
            for j in range(NDS):
                if self.dcnt[q][j] > 0:
                    allw.append(('d_%s%d' % (q, j), self.dsem[q][j], self.dcnt[q][j], None))
        for eng in ENGS:
            waits = []
            for (key, s, v, x) in allw:
                if x == eng and eng in ('pe', 'sp'):
                    continue
                if self.seen[eng].get(key, 0) >= v:
                    continue
                self.seen[eng][key] = v
                waits.append((s, v))

            def run(eo, waits=waits):
                for (s, v) in waits:
                    eo.wait_ge(s, v)
            self.ops[eng].append(run)
        self.res = {}

    def wait_all(self, eng, keys):
        waits = self._deps(eng, keys, ())

        def run(eo, waits=waits):
            for (s, v) in waits:
                eo.wait_ge(s, v)
        self.ops[eng].append(run)

    def finish(self):
        nc = self.nc
        with nc.Block() as block:
            @block.tensor
            def _(e):
                for f in self.ops['pe']:
                    f(e)

            @block.scalar
            def _(e):
                for f in self.ops['act']:
                    f(e)

            @block.vector
            def _(e):
                for f in self.ops['dve']:
                    f(e)

            @block.gpsimd
            def _(e):
                for f in self.ops['pool']:
                    f(e)

            @block.sync
            def _(e):
                for f in self.ops['sp']:
                    f(e)


class Ctx:
    pass


class Arena:
    def __init__(self, nc, es, words):
        self.t = es.enter_context(nc.sbuf_tensor('arena', [128, words], F32))
        self.tb = self.t.bitcast(BF16)
        self.words = words
        self.off = 0

    def reset(self):
        self.off = 0

    def mark(self):
        return self.off

    def release(self, m):
        self.off = m

    def alloc(self, shape, dt=F32):
        n = 1
        for s_ in shape[1:]:
            n *= s_
        esz = 4 if dt == F32 else 2
        nbytes = (n * esz + 63) // 64 * 64
        o = self.off
        self.off += nbytes
        assert self.off <= self.words * 4, "arena overflow %d" % self.off
        base = self.t if dt == F32 else self.tb
        v = base[:shape[0], o // esz:o // esz + n]
        if len(shape) == 3:
            v = v.rearrange("p (a b) -> p a b", a=shape[1])
        elif len(shape) == 4:
            v = v.rearrange("p (a b c) -> p a b c", a=shape[1], b=shape[2])
        return v


GELU_MODE = ['native']


def gelu_tanh(C, dst, src, src_key, dst_key):
    P = C.P
    if GELU_MODE[0] == 'native':
        P.op('act', lambda e: e.activation(dst, src, AF.Gelu_apprx_tanh), reads=[src_key], writes=[dst_key])
        return
    t = C.gelu_tmp[C.gelu_i % 2]
    tk = 'gelu_tmp%d' % (C.gelu_i % 2)
    C.gelu_i += 1
    P.op('dve', lambda e: e.tensor_tensor(t, src, src, ALU.mult), reads=[src_key], writes=[tk])
    P.op('dve', lambda e: e.tensor_scalar(t, t, 0.044715, 1.0, ALU.mult, ALU.add), reads=[tk], writes=[tk])
    P.op('dve', lambda e: e.tensor_tensor(t, t, src, ALU.mult), reads=[tk, src_key], writes=[tk])
    P.op('act', lambda e: e.activation(t, t, AF.Sigmoid, scale=1.5957691216057308), reads=[tk], writes=[tk])
    P.op('dve', lambda e: e.tensor_tensor(dst, t, src, ALU.mult), reads=[tk, src_key], writes=[dst_key])


def build(n_layers=DEPTH, stop_after=None, debug=False):
    nc = bass.Bass("TRN2", target_bir_lowering=False)
    es = ExitStack()
    P = Prog(nc, es)
    C = Ctx()
    C.nc, C.P, C.es = nc, P, es
    C.debug = debug
    C.dram_written = ['out']

    def din(name, shape, dt=F32):
        return nc.dram_tensor(name, list(shape), dt, kind="ExternalInput").ap()

    def dscr(name, shape, dt=F32):
        kind = "ExternalOutput" if (debug and name in debug) else "Internal"
        C.dram_written.append(name)
        return nc.dram_tensor(name, list(shape), dt, kind=kind).ap()

    def sb(name, shape, dt=F32):
        return es.enter_context(nc.sbuf_tensor(name, list(shape), dt))

    x_in = din("x", [S, D])
    c_col = din("c_col", [128, NCH])
    w_ada = din("w_ada", [DEPTH, D, 6 * D])
    b_ada = din("b_ada_col", [128, DEPTH * 96])
    w_in = din("w_in", [DEPTH, D, N_IN])
    ident_d = din("ident", [128, 128])
    s5A_d = din("s5A", [DEPTH, 128, 3 * 64])
    braw_d = din("braw", [DEPTH, 64, 128, 128])
    crci_d = din("crci", [DEPTH, 128, 2 * 1024])
    dskip_d = din("dskip_col", [128, DEPTH * 8])
    w_glu = din("ssm_w_glu", [DEPTH, 1024, 2048])
    convw_d = din("convw_col", [DEPTH, 128, 96])
    w_out = din("w_out", [DEPTH, D, D])
    wq_d = din("peer_w_query", [DEPTH, D, D])
    keysT_d = din("keysT", [DEPTH, 128, 16 * 128])
    uT_d = din("peer_uT", [DEPTH, D, 16384])
    v_d = din("peer_v", [DEPTH, 16384, D])
    iota_d = din("iota128", [128, 128])
    lnp_d = din("lnp_col", [128, DEPTH * 64])
    dnhp_d = din("dnhp", [8, DEPTH * 2])
    normw_d = din("normw_col", [128, DEPTH])
    maskL_d = din("maskL", [64, 64])
    maskAT_d = din("maskAT", [64, 64])
    selrow_d = din("selrow", [8, 8 * 64])
    sel63_d = din("sel63", [64, 128])
    cmask_d = din("cmask", [8, S])
    out_d = nc.dram_tensor("out", [S, D], F32, kind="ExternalOutput").ap()

    xT_d = dscr("xT_d", [D, S])
    projT_d = dscr("projT_d", [N_IN, S])
    ycatT_d = dscr("ycatT_d", [D, S], BF16)
    gyT_d = dscr("gyT_d", [1024, S], BF16)
    rT_d = dscr("rT_d", [D, S])
    hfT_d = dscr("hfT_d", [D, S], BF16)
    qT_d = dscr("qT_d", [D, S])
    Gd = dscr("Gd", [128, 128, S], BF16)

    ident = sb("ident_sb", [128, 128])
    ones_m = sb("ones_m", [128, 128])
    modT = sb("modT", [128, DEPTH * 96])
    cact = sb("cact", [128, NCH])
    eps_c = sb("eps_c", [128, 1])
    lnp = sb("lnp", [128, DEPTH * 64])
    psb = [es.enter_context(nc.psum_tensor("psb%d" % i, [128, 512], F32)) for i in range(8)]
    PS = lambda i: 'ps%d' % i
    A = Arena(nc, es, 51 * 1024)
    C.ident, C.ones_m, C.modT, C.psb, C.A = ident, ones_m, modT, psb, A

    P.dma('sp', ident[:], ident_d, writes=['ident'])
    P.dma('sp', lnp[:], lnp_d, writes=['lnp'])
    P.op('dve', lambda e: e.memset(ones_m[:], 1.0 / D), writes=['ones_m'])
    P.op('dve', lambda e: e.memset(eps_c[:], LN_EPS), writes=['eps_c'])

    xtm = [A.alloc([128, D]) for i in range(2)]
    xst = [A.alloc([128, NCH, 128]) for i in range(2)]
    ev = 0
    for tt in range(S // 128):
        b = tt % 2
        P.dma('sp', xtm[b], x_in[tt * 128:(tt + 1) * 128, :], writes=['xtm%d' % b])
        for g in range(4):
            pb = g % 2

            def tr(e, b=b, g=g, pb=pb):
                ins = None
                for jj in range(4):
                    j = g * 4 + jj
                    ins = e.transpose(psb[pb][:, jj * 128:(jj + 1) * 128],
                                      xtm[b][:, j * 128:(j + 1) * 128], ident[:])
                return ins
            P.op('pe', tr, reads=['xtm%d' % b, 'ident'], writes=[PS(pb)])
            dst = xst[b][:, g * 4:(g + 1) * 4, :]
            src = psb[pb][:].rearrange("p (j t) -> p j t", j=4)
            if ev % 2 == 0:
                P.op('act', lambda e, dst=dst, src=src: e.copy(dst, src),
                     reads=[PS(pb)], writes=[('xst', b, g)])
            else:
                P.op('dve', lambda e, dst=dst, src=src: e.tensor_copy(dst, src),
                     reads=[PS(pb)], writes=[('xst', b, g)])
            ev += 1
        P.dma('sp', xT_d.rearrange("(j p) t -> p j t", p=128)[:, :, tt * 128:(tt + 1) * 128],
              xst[b], reads=[('xst', b, g) for g in range(4)], writes=['xT_d'])

    P.barrier()
    A.reset()
    P.dma('sp', cact[:], c_col, writes=['cact'])
    P.op('act', lambda e: e.activation(cact[:], cact[:], AF.Silu), reads=['cact'], writes=['cact'])
    bada = A.alloc([128, DEPTH * 96])
    P.dma('sp', bada, b_ada, writes=['bada'])
    WA_COLS = 3072
    wa = [A.alloc([128, WA_COLS]) for i in range(3)]
    wi = 0
    for l in range(n_layers):
        for cb in range(6 * D // WA_COLS):
            for k in range(NCH):
                b = wi % 3
                wi += 1
                P.dma('sp' if (wi % 2) else 'act', wa[b],
                      w_ada[l, k * 128:(k + 1) * 128, cb * WA_COLS:(cb + 1) * WA_COLS],
                      writes=['wa%d' % b])

                def mm(e, b=b, k=k, cb=cb):
                    ins = None
                    for n in range(WA_COLS // 128):
                        col = cb * (WA_COLS // 128) + n
                        ins = e.matmul(psb[7][:, col:col + 1], wa[b][:, n * 128:(n + 1) * 128],
                                       cact[:, k:k + 1], start=(k == 0 and n == 0 and cb == 0),
                                       stop=(k == NCH - 1), skip_group_check=True)
                    return ins
                P.op('pe', mm, reads=['wa%d' % b, 'cact'], writes=[PS(7)])
        P.op('dve', lambda e, l=l: e.tensor_tensor(modT[:, l * 96:(l + 1) * 96], psb[7][:, 0:96],
                                                  bada[:, l * 96:(l + 1) * 96], ALU.add),
             reads=[PS(7), 'bada'], writes=['modT'])
    if stop_after == 'A':
        return finish_debug(C)

    for l in range(n_layers):
        for c0 in (16, 64):
            sl = modT[:, l * 96 + c0:l * 96 + c0 + 16]
            P.op('dve', lambda e, sl=sl: e.tensor_scalar(sl, sl, 1.0, None, ALU.add),
                 reads=['modT'], writes=['modT'])

    TB = 256
    C.TB = TB

    def ln_alloc():
        L = Ctx()
        L.x = [A.alloc([128, NCH, TB]) for i in range(2)]
        L.sq = A.alloc([128, NCH, TB])
        L.m = A.alloc([128, TB])
        L.v = A.alloc([128, TB])
        L.r = A.alloc([128, TB])
        L.nmr = A.alloc([128, TB])
        L.t = [A.alloc([128, TB]) for i in range(2)]
        return L

    def ln_block(L, xt, xt_key, a_of, b_of, out_of, out_key_of):
        mean_ps = psb[6][:, 0:TB]
        msq_ps = psb[7][:, 0:TB]

        def mm1(e):
            ins = None
            for j in range(NCH):
                ins = e.matmul(mean_ps, ones_m[:], xt[:, j, :], start=(j == 0), stop=(j == NCH - 1))
            return ins
        P.op('pe', mm1, reads=[xt_key, 'ones_m'], writes=[PS(6)])
        P.op('act', lambda e: e.activation(L.sq, xt, AF.Square), reads=[xt_key], writes=['ln_sq'])

        def mm2(e):
            ins = None
            for j in range(NCH):
                ins = e.matmul(msq_ps, ones_m[:], L.sq[:, j, :], start=(j == 0), stop=(j == NCH - 1))
            return ins
        P.op('pe', mm2, reads=['ln_sq', 'ones_m'], writes=[PS(7)])
        P.op('act', lambda e: e.copy(L.m, mean_ps), reads=[PS(6)], writes=['ln_m'])
        P.op('dve', lambda e: e.scalar_tensor_tensor(L.v, L.m, -1.0, L.m, ALU.mult, ALU.mult),
             reads=['ln_m'], writes=['ln_v'])
        P.op('dve', lambda e: e.tensor_tensor(L.v, L.v, msq_ps, ALU.add),
             reads=['ln_v', PS(7)], writes=['ln_v'])
        P.op('act', lambda e: e.activation(L.v, L.v, AF.Sqrt, bias=eps_c[:], scale=1.0),
             reads=['ln_v', 'eps_c'], writes=['ln_v'])
        P.op('dve', lambda e: e.reciprocal(L.r, L.v), reads=['ln_v'], writes=['ln_r'])
        P.op('dve', lambda e: e.scalar_tensor_tensor(L.nmr, L.m, -1.0, L.r, ALU.mult, ALU.mult),
             reads=['ln_m', 'ln_r'], writes=['ln_nmr'])
        for j in range(NCH):
            tb = j % 2
            t = L.t[tb]
            a = a_of(j)
            P.op('dve', lambda e, t=t, j=j, a=a: e.scalar_tensor_tensor(t, xt[:, j, :], a, L.r,
                                                                      ALU.mult, ALU.mult),
                 reads=[xt_key, 'ln_r', 'modT'], writes=['ln_t%d' % tb])
            P.op('dve', lambda e, t=t, a=a: e.scalar_tensor_tensor(t, L.nmr, a, t, ALU.mult, ALU.add),
                 reads=['ln_nmr', 'ln_t%d' % tb, 'modT'], writes=['ln_t%d' % tb])
            o = out_of(j)
            bcol = b_of(j)
            P.op('act', lambda e, o=o, t=t, bcol=bcol: e.activation(o, t, AF.Identity, bias=bcol, scale=1.0),
                 reads=['ln_t%d' % tb, 'modT'], writes=[out_key_of(j)])

    xT_v = xT_d.rearrange("(j p) t -> p j t", p=128)

    for l in range(n_layers):
        mo = l * 96
        def phase_0(l=l, mo=mo):
            P.barrier()
            A.reset()
            hT = A.alloc([128, NCH, S], BF16)
            L = ln_alloc()
            for tb in range(S // TB):
                b = tb % 2
                P.dma('sp', L.x[b], xT_v[:, :, tb * TB:(tb + 1) * TB], reads=['xT_d'], writes=['ln_x%d' % b])
                ln_block(L, L.x[b], 'ln_x%d' % b,
                         lambda j: modT[:, mo + 16 + j:mo + 16 + j + 1],
                         lambda j: modT[:, mo + j:mo + j + 1],
                         lambda j, tb=tb: hT[:, j, tb * TB:(tb + 1) * TB],
                         lambda j, tb=tb: ('hT', tb))
            wt = [A.alloc([128, NCH, 512], BF16) for i in range(2)]
            stg = [A.alloc([128, 512]) for i in range(4)]
            w_in_v = w_in[l].rearrange("(k p) n -> p k n", p=128)
            hkeys = [('hT', tb) for tb in range(S // TB)]
            cnt = 0
            for ng in range((N_IN + 511) // 512):
                n0 = ng * 512
                nw = min(512, N_IN - n0)
                b = ng % 2
                P.dma('pool', wt[b][:, :, :nw], w_in_v[:, :, n0:n0 + nw], writes=['wt%d' % b])
                for nc_ in range((nw + 127) // 128):
                    m = min(128, nw - nc_ * 128)
                    for t4 in range(S // 512):
                        pb = cnt % 4
                        cnt += 1

                        def mm(e, b=b, nc_=nc_, m=m, t4=t4, pb=pb):
                            ins = None
                            for k in range(NCH):
                                ins = e.matmul(psb[pb][:m, :], wt[b][:, k, nc_ * 128:nc_ * 128 + m],
                                               hT[:, k, t4 * 512:(t4 + 1) * 512],
                                               start=(k == 0), stop=(k == NCH - 1))
                            return ins
                        P.op('pe', mm, reads=['wt%d' % b] + hkeys, writes=[PS(pb)])
                        if pb % 2 == 0:
                            P.op('act', lambda e, pb=pb, m=m: e.copy(stg[pb][:m, :], psb[pb][:m, :]),
                                 reads=[PS(pb)], writes=['stg%d' % pb])
                        else:
                            P.op('dve', lambda e, pb=pb, m=m: e.tensor_copy(stg[pb][:m, :], psb[pb][:m, :]),
                                 reads=[PS(pb)], writes=['stg%d' % pb])
                        r0 = n0 + nc_ * 128
                        P.dma('sp', projT_d[r0:r0 + m, t4 * 512:(t4 + 1) * 512], stg[pb][:m, :],
                              reads=['stg%d' % pb], writes=['projT_d'])
            if stop_after == 'B':
                return 'STOP'

            return None
        if phase_0() == 'STOP':
            return finish_debug(C)
        def phase_1(l=l, mo=mo):
            P.barrier()
            A.reset()
            NL = list(range(8)) + [8 * k for k in range(1, 8)] + [64 * k for k in range(1, 8)] + [0, 512, 1024, 1536]
            NP_ = len(NL)
            TWO_PI = 2.0 * math.pi
            MAGIC = 12582912.0
            COL1 = A.alloc([128, 22, 64])
            COL2 = A.alloc([128, 22, 64])
            WcF = A.alloc([128, 4, 64, 16])
            PIp = A.alloc([128, 64])
            identb = A.alloc([128, 128], BF16)
            dsk = A.alloc([128, 8])
            s5_mark = A.mark()
            s5a = A.alloc([128, 3, 64])
            P.dma('sp', s5a, s5A_d[l].rearrange("p (a g) -> p a g", a=3), writes=['s5a'])
            crci = A.alloc([128, 2, 64, 16])
            P.dma('sp', crci, crci_d[l].rearrange("p (a g c) -> p a g c", a=2, g=64), writes=['crci'])
            P.dma('sp', dsk, dskip_d[:, l * 8:(l + 1) * 8], writes=['dsk'])
            NV = A.alloc([128, NP_, 64])
            for i, n in enumerate(NL):
                P.op('pool', lambda e, i=i, n=n: e.memset(NV[:, i, :], float(n)), writes=['NV'])
            stp = A.alloc([128, 64])
            thd = A.alloc([128, 2, 64])
            P.op('act', lambda e: e.activation(stp, s5a[:, 2, :], AF.Exp), reads=['s5a'], writes=['stp'])
            P.op('dve', lambda e: e.tensor_tensor(thd[:, 0, :], s5a[:, 1, :], stp, ALU.mult), reads=['s5a', 'stp'], writes=['thd'])
            P.op('dve', lambda e: e.tensor_tensor(thd[:, 1, :], s5a[:, 0, :], stp, ALU.mult), reads=['s5a', 'stp', 'thd'], writes=['thd'])
            ang = A.alloc([128, NP_, 64])
            tq = A.alloc([128, NP_, 64])
            sinv = A.alloc([128, NP_, 64])
            cosv = A.alloc([128, NP_, 64])
            mag = A.alloc([128, NP_, 64])
            bc = lambda ap2: ap2.unsqueeze(1).to_broadcast([128, NP_, 64])
            P.op('dve', lambda e: e.tensor_tensor(ang, NV, bc(thd[:, 0, :]), ALU.mult), reads=['NV', 'thd'], writes=['ang'])

            def sin_of(dst, key, shift):
                if shift != 0.0:
                    P.op('dve', lambda e: e.tensor_scalar(dst, ang, shift, None, ALU.add), reads=['ang'], writes=[key])
                    srcx = dst
                else:
                    srcx = ang
                P.op('dve', lambda e: e.tensor_scalar(tq, srcx, 1.0 / TWO_PI, MAGIC, ALU.mult, ALU.add),
                     reads=['ang', key], writes=['tq'])
                P.op('dve', lambda e: e.tensor_scalar(tq, tq, MAGIC, None, ALU.subtract), reads=['tq'], writes=['tq'])
                C1 = 6.28125
                C2 = TWO_PI - C1
                P.op('dve', lambda e: e.scalar_tensor_tensor(dst, tq, -C1, srcx, ALU.mult, ALU.add),
                     reads=['tq', 'ang', key], writes=[key])
                P.op('dve', lambda e: e.scalar_tensor_tensor(dst, tq, -C2, dst, ALU.mult, ALU.add),
                     reads=['tq', key], writes=[key])
                P.op('dve', lambda e: e.tensor_scalar(dst, dst, 3.14159, -3.14159, ALU.min, ALU.max), reads=[key], writes=[key])
                P.op('act', lambda e: e.activation(dst, dst, AF.Sin), reads=[key], writes=[key])
            sin_of(sinv, 'sinv', 0.0)
            sin_of(cosv, 'cosv', 0.5 * math.pi)
            P.op('dve', lambda e: e.tensor_tensor(mag, NV, bc(thd[:, 1, :]), ALU.mult), reads=['NV', 'thd'], writes=['mag'])
            P.op('act', lambda e: e.activation(mag, mag, AF.Exp), reads=['mag'], writes=['mag'])
            P.op('dve', lambda e: e.tensor_tensor(cosv, cosv, mag, ALU.mult), reads=['cosv', 'mag'], writes=['cosv'])
            P.op('dve', lambda e: e.tensor_tensor(sinv, sinv, mag, ALU.mult), reads=['sinv', 'mag'], writes=['sinv'])
            ar, ai = cosv, sinv
            cf = A.alloc([128, 6, 64])
            P.op('dve', lambda e: e.tensor_scalar(cf[:, 0, :], ar[:, 1, :], -1.0, None, ALU.add), reads=['cosv'], writes=['cf0'])
            P.op('dve', lambda e: e.tensor_tensor(cf[:, 1, :], s5a[:, 0, :], s5a[:, 0, :], ALU.mult), reads=['s5a'], writes=['cf1'])
            P.op('dve', lambda e: e.tensor_tensor(cf[:, 4, :], s5a[:, 1, :], s5a[:, 1, :], ALU.mult), reads=['s5a'], writes=['cf4'])
            P.op('dve', lambda e: e.tensor_tensor(cf[:, 1, :], cf[:, 1, :], cf[:, 4, :], ALU.add), reads=['cf1', 'cf4'], writes=['cf1'])
            P.op('dve', lambda e: e.reciprocal(cf[:, 5, :], cf[:, 1, :]), reads=['cf1'], writes=['cf5'])
            P.op('dve', lambda e: e.tensor_tensor(cf[:, 2, :], cf[:, 0, :], s5a[:, 0, :], ALU.mult), reads=['cf0', 's5a'], writes=['cf2'])
            P.op('dve', lambda e: e.tensor_tensor(cf[:, 4, :], ai[:, 1, :], s5a[:, 1, :], ALU.mult), reads=['sinv', 's5a', 'cf4'], writes=['cf4'])
            P.op('dve', lambda e: e.tensor_tensor(cf[:, 2, :], cf[:, 2, :], cf[:, 4, :], ALU.add), reads=['cf2', 'cf4'], writes=['cf2'])
            P.op('dve', lambda e: e.tensor_tensor(cf[:, 2, :], cf[:, 2, :], cf[:, 5, :], ALU.mult), reads=['cf2', 'cf5'], writes=['cf2'])
            P.op('dve', lambda e: e.tensor_tensor(cf[:, 3, :], ai[:, 1, :], s5a[:, 0, :], ALU.mult), reads=['sinv', 's5a'], writes=['cf3'])
            P.op('dve', lambda e: e.tensor_tensor(cf[:, 4, :], cf[:, 0, :], s5a[:, 1, :], ALU.mult), reads=['cf0', 's5a', 'cf2'], writes=['cf4'])
            P.op('dve', lambda e: e.tensor_tensor(cf[:, 3, :], cf[:, 3, :], cf[:, 4, :], ALU.subtract), reads=['cf3', 'cf4'], writes=['cf3'])
            P.op('dve', lambda e: e.tensor_tensor(cf[:, 3, :], cf[:, 3, :], cf[:, 5, :], ALU.mult), reads=['cf3', 'cf5'], writes=['cf3'])
            FR = A.alloc([128, 22, 64])
            FI = A.alloc([128, 22, 64])
            ftmp = A.alloc([128, 8, 64])
            bc8 = lambda ap2: ap2.unsqueeze(1).to_broadcast([128, 8, 64])
            P.op('dve', lambda e: e.tensor_tensor(FR[:, 0:8, :], ar[:, 0:8, :], bc8(cf[:, 2, :]), ALU.mult), reads=['cosv', 'cf2'], writes=['FR'])
            P.op('dve', lambda e: e.tensor_tensor(ftmp, ai[:, 0:8, :], bc8(cf[:, 3, :]), ALU.mult), reads=['sinv', 'cf3'], writes=['ftmp'])
            P.op('dve', lambda e: e.tensor_tensor(FR[:, 0:8, :], FR[:, 0:8, :], ftmp, ALU.subtract), reads=['FR', 'ftmp'], writes=['FR'])
            P.op('dve', lambda e: e.tensor_tensor(FI[:, 0:8, :], ar[:, 0:8, :], bc8(cf[:, 3, :]), ALU.mult), reads=['cosv', 'cf3'], writes=['FI'])
            P.op('dve', lambda e: e.tensor_tensor(ftmp, ai[:, 0:8, :], bc8(cf[:, 2, :]), ALU.mult), reads=['sinv', 'cf2', 'FR'], writes=['ftmp'])
            P.op('dve', lambda e: e.tensor_tensor(FI[:, 0:8, :], FI[:, 0:8, :], ftmp, ALU.add), reads=['FI', 'ftmp'], writes=['FI'])
            P.op('dve', lambda e: e.tensor_copy(FR[:, 8:22, :], ar[:, 8:22, :]), reads=['cosv', 'FR'], writes=['FR'])
            P.op('dve', lambda e: e.tensor_copy(FI[:, 8:22, :], ai[:, 8:22, :]), reads=['sinv', 'FI'], writes=['FI'])
            P.op('dve', lambda e: e.tensor_copy(COL1[0:64], FR[0:64]), reads=['FR'], writes=['COL1a'])
            P.op('dve', lambda e: e.tensor_scalar(COL1[64:128], FI[64:128], -1.0, None, ALU.mult), reads=['FI'], writes=['COL1b'])
            P.op('dve', lambda e: e.tensor_copy(COL2[0:64], FI[0:64]), reads=['FI'], writes=['COL2a'])
            P.op('dve', lambda e: e.tensor_copy(COL2[64:128], FR[64:128]), reads=['FR'], writes=['COL2b'])
            colkeys = ['COL1a', 'COL1b', 'COL2a', 'COL2b']
            X1 = A.alloc([128, 4, 64])
            X2 = A.alloc([128, 4, 64])
            P.op('dve', lambda e: e.tensor_copy(X1[0:64], ar[0:64, 22:26, :]), reads=['cosv'], writes=['X1a'])
            P.op('dve', lambda e: e.tensor_scalar(X1[64:128], ai[64:128, 22:26, :], -1.0, None, ALU.mult), reads=['sinv'], writes=['X1b'])
            P.op('dve', lambda e: e.tensor_scalar(X2[0:64], ai[0:64, 22:26, :], -1.0, None, ALU.mult), reads=['sinv'], writes=['X2a'])
            P.op('dve', lambda e: e.tensor_scalar(X2[64:128], ar[64:128, 22:26, :], -1.0, None, ALU.mult), reads=['cosv'], writes=['X2b'])
            wct = A.alloc([128, 64, 16])
            for k in range(4):
                x1b = X1[:, k, :].unsqueeze(2).to_broadcast([128, 64, 16])
                x2b = X2[:, k, :].unsqueeze(2).to_broadcast([128, 64, 16])
                P.op('dve', lambda e, k=k, x1b=x1b: e.tensor_tensor(WcF[:, k], crci[:, 0], x1b, ALU.mult),
                     reads=['crci', 'X1a', 'X1b'], writes=[('WcF', k)])
                P.op('dve', lambda e, k=k, x2b=x2b: e.tensor_tensor(wct, crci[:, 1], x2b, ALU.mult),
                     reads=['crci', 'X2a', 'X2b'], writes=['wct'])
                P.op('dve', lambda e, k=k: e.tensor_tensor(WcF[:, k], WcF[:, k], wct, ALU.add),
                     reads=[('WcF', k), 'wct'], writes=[('WcF', k)])
            P.op('dve', lambda e: e.tensor_copy(PIp[0:64], ident[0:64, 0:64]), reads=['ident'], writes=['PIa'])
            P.op('dve', lambda e: e.tensor_copy(PIp[64:128], ident[64:128, 64:128]), reads=['ident'], writes=['PIb'])
            P.op('dve', lambda e: e.tensor_copy(identb, ident[:]), reads=['ident'], writes=['identb'])

            if debug and 'dbgS5' in debug:
                for nm, t_, shp in (('dbg_col1', COL1, [128, 22 * 64]), ('dbg_col2', COL2, [128, 22 * 64]),
                                    ('dbg_wcf', WcF, [128, 4 * 64 * 16])):
                    dd = nc.dram_tensor(nm, shp, F32, kind="ExternalOutput").ap()
                    flat = t_.rearrange("p a b -> p (a b)") if len(t_.shape) == 3 else t_.rearrange("p a b c -> p (a b c)")
                    P.dma('sp', dd, flat, reads=['COL1a', 'COL1b', 'COL2a', 'COL2b'] + [('WcF', k) for k in range(4)], writes=[nm])
                    C.dram_written.append(nm)
            P.barrier()
            A.release(s5_mark)
            colkeys = []
            PAD0, PAD1, PAD2, PAD3 = 8, 56, 448, 1536
            Wd = A.alloc([128, 22, 8, 128], BF16)
            WcZ = A.alloc([128, 4, 8, 128], BF16)
            Bw = A.alloc([128, 8, 128], BF16)
            u32 = A.alloc([128, S])
            ubf = A.alloc([128, S], BF16)
            z0 = [A.alloc([128, PAD0 + S], BF16) for i in range(2)]
            w1 = [A.alloc([128, PAD1 + S], BF16) for i in range(2)]
            w2 = [A.alloc([128, PAD2 + S], BF16) for i in range(2)]
            w3 = [A.alloc([128, PAD3 + S], BF16) for i in range(8)]
            gyo = [A.alloc([128, 512], BF16) for i in range(2)]
            ytmp = [A.alloc([128, 512]) for i in range(2)]
            C.gelu_tmp = [A.alloc([128, 512]) for i in range(2)]
            C.gelu_i = 0
            P.op('pool', lambda e: e.memset(WcZ, 0.0), writes=['WcZ'])
            for i in range(2):
                P.op('pool', lambda e, i=i: e.memset(z0[i][:, 0:PAD0], 0.0), writes=['z0_%d' % i])
                P.op('pool', lambda e, i=i: e.memset(w1[i][:, 0:PAD1], 0.0), writes=['w1_%d' % i])
                P.op('pool', lambda e, i=i: e.memset(w2[i][:, 0:PAD2], 0.0), writes=['w2_%d' % i])
            for i in range(8):
                P.op('pool', lambda e, i=i: e.memset(w3[i][:, 0:PAD3], 0.0), writes=['w3_%d' % i])
            evc = [0]

            def evac(dst, pb, key, reads_extra=()):
                if evc[0] % 2 == 0:
                    P.op('act', lambda e: e.copy(dst, psb[pb][:]), reads=[PS(pb)], writes=[key])
                else:
                    P.op('dve', lambda e: e.tensor_copy(dst, psb[pb][:]), reads=[PS(pb)], writes=[key])
                evc[0] += 1
            pbc = [0]

            def nextpb():
                pb = pbc[0] % 6
                pbc[0] += 1
                return pb

            import os as _os
            for ch in ([7, 6, 5, 4, 3, 2, 1, 0] if _os.environ.get('S5REV') else range(8)):
                g0 = ch * 8
                P.dma('sp', u32, projT_d[ch * 128:(ch + 1) * 128, :], reads=['projT_d'], writes=['u32'])
                P.op('pool', lambda e: e.tensor_copy(ubf, u32), reads=['u32'], writes=['ubf'])
                P.dma('pool', Bw, braw_d[l, g0:g0 + 8].rearrange("g k m -> k g m"), writes=['Bw'])
                for n in range(22):
                    c1 = COL1[:, n, g0:g0 + 8].unsqueeze(2).to_broadcast([128, 8, 64])
                    c2 = COL2[:, n, g0:g0 + 8].unsqueeze(2).to_broadcast([128, 8, 64])
                    pib = PIp.unsqueeze(1).to_broadcast([128, 8, 64])
                    eng = 'dve' if n % 2 == 0 else 'pool'
                    P.op(eng, lambda e, n=n, c1=c1, pib=pib: e.tensor_tensor(Wd[:, n, :, 0:64], pib, c1, ALU.mult),
                         reads=colkeys + ['PIa', 'PIb'], writes=[('Wd', n, 0)])
                    P.op(eng, lambda e, n=n, c2=c2, pib=pib: e.tensor_tensor(Wd[:, n, :, 64:128], pib, c2, ALU.mult),
                         reads=colkeys + ['PIa', 'PIb'], writes=[('Wd', n, 1)])
                wdkeys = [('Wd', n, h) for n in range(22) for h in range(2)]
                for k in range(4):
                    for i in range(8):
                        P.op("pool", lambda e, k=k, i=i, g0=g0: e.tensor_copy(WcZ[:, k, i, 16 * i:16 * i + 16], WcF[:, k, g0 + i, :]),
                             reads=[('WcF', k)], writes=['WcZ'])
                for gi in range(8):
                    zb = gi % 2
                    for t4 in range(4):
                        pb = nextpb()
                        P.op('pe', lambda e, pb=pb, gi=gi, t4=t4: e.matmul(psb[pb][:], Bw[:, gi, :], ubf[:, t4 * 512:(t4 + 1) * 512],
                                                                          start=True, stop=True),
                             reads=['Bw', 'ubf'], writes=[PS(pb)])
                        evac(z0[zb][:, PAD0 + t4 * 512:PAD0 + (t4 + 1) * 512], pb, 'z0_%d' % zb)
                    for (lev, src, skey, pad_s, dst, dkey, pad_d, stride, nbase) in (
                            (1, z0[zb], 'z0_%d' % zb, PAD0, w1[zb], 'w1_%d' % zb, PAD1, 1, 0),
                            (2, w1[zb], 'w1_%d' % zb, PAD1, w2[zb], 'w2_%d' % zb, PAD2, 8, 7),
                            (3, w2[zb], 'w2_%d' % zb, PAD2, w3[gi], 'w3_%d' % gi, PAD3, 64, 14)):
                        for t4 in range(4):
                            pb = nextpb()

                            def mm(e, pb=pb, lev=lev, src=src, pad_s=pad_s, stride=stride, nbase=nbase, t4=t4, gi=gi):
                                ins = None
                                for k in range(8):
                                    if lev == 1:
                                        w = Wd[:, k, gi, :]
                                    elif k == 0:
                                        w = identb
                                    else:
                                        w = Wd[:, nbase + k, gi, :]
                                    o = pad_s + t4 * 512 - stride * k
                                    ins = e.matmul(psb[pb][:], w, src[:, o:o + 512], start=(k == 0), stop=(k == 7))
                                return ins
                            P.op('pe', mm, reads=wdkeys + ['identb', skey], writes=[PS(pb)])
                            evac(dst[:, pad_d + t4 * 512:pad_d + (t4 + 1) * 512], pb, dkey)
                for t4 in range(4):
                    pb = nextpb()

                    def mm4(e, pb=pb, t4=t4):
                        ins = None
                        for gi in range(8):
                            for k in range(4):
                                o = PAD3 + t4 * 512 - 512 * k
                                ins = e.matmul(psb[pb][:], WcZ[:, k, gi, :], w3[gi][:, o:o + 512],
                                               start=(gi == 0 and k == 0), stop=(gi == 7 and k == 3))
                        return ins
                    P.op('pe', mm4, reads=['WcZ'] + ['w3_%d' % i for i in range(8)], writes=[PS(pb)])
                    yt = ytmp[t4 % 2]
                    P.op('dve', lambda e, yt=yt, pb=pb, t4=t4, ch=ch: e.scalar_tensor_tensor(
                        yt, u32[:, t4 * 512:(t4 + 1) * 512], dsk[:, ch:ch + 1], psb[pb][:], ALU.mult, ALU.add),
                        reads=['u32', 'dsk', PS(pb)], writes=['ytmp%d' % (t4 % 2)])
                    gelu_tanh(C, gyo[t4 % 2], yt, 'ytmp%d' % (t4 % 2), 'gyo%d' % (t4 % 2))
                    P.dma('sp', gyT_d[ch * 128:(ch + 1) * 128, t4 * 512:(t4 + 1) * 512], gyo[t4 % 2],
                          reads=['gyo%d' % (t4 % 2)], writes=['gyT_d'])
            P.barrier()
            A.reset()
            gy = A.alloc([128, 8, S], BF16)
            for ch in range(8):
                P.dma('sp', gy[:, ch, :], gyT_d[ch * 128:(ch + 1) * 128, :], writes=[('gy', ch)])
            wg = A.alloc([128, 8, 2048], BF16)
            for ch in range(8):
                P.dma('pool', wg[:, ch, :], w_glu[l, ch * 128:(ch + 1) * 128, :], writes=[('wg', ch)])
            wgkeys = [('wg', ch) for ch in range(8)]
            gykeys = [('gy', ch) for ch in range(8)]
            sig = [A.alloc([128, 512]) for i in range(2)]
            yo = [A.alloc([128, 512], BF16) for i in range(2)]
            ci = 0
            for j in range(8):
                for t4 in range(4):
                    pa, pbb = (0, 1) if ci % 2 == 0 else (2, 3)
                    sb_i = ci % 2
                    ci += 1
                    for (pbk, col0) in ((pa, j * 128), (pbb, (8 + j) * 128)):
                        def mmg(e, pbk=pbk, col0=col0, t4=t4):
                            ins = None
                            for chh in range(8):
                                ins = e.matmul(psb[pbk][:], wg[:, chh, col0:col0 + 128], gy[:, chh, t4 * 512:(t4 + 1) * 512],
                                               start=(chh == 0), stop=(chh == 7))
                            return ins
                        P.op('pe', mmg, reads=wgkeys + gykeys, writes=[PS(pbk)])
                    P.op('act', lambda e, sb_i=sb_i, pbb=pbb: e.activation(sig[sb_i], psb[pbb][:], AF.Sigmoid),
                         reads=[PS(pbb)], writes=['sig%d' % sb_i])
                    P.op('dve', lambda e, sb_i=sb_i, pa=pa: e.tensor_tensor(yo[sb_i], psb[pa][:], sig[sb_i], ALU.mult),
                         reads=[PS(pa), 'sig%d' % sb_i], writes=['yo%d' % sb_i])
                    P.dma('sp', ycatT_d[j * 128:(j + 1) * 128, t4 * 512:(t4 + 1) * 512], yo[sb_i],
                          reads=['yo%d' % sb_i], writes=['ycatT_d'])
            if stop_after == 'C':
                return 'STOP'

            return None
        if phase_1() == 'STOP':
            return finish_debug(C)
        def phase_2(l=l, mo=mo):
            P.barrier()
            A.reset()
            NEG = -30000.0
            cw = A.alloc([128, 24, 4])
            P.dma('sp', cw, convw_d[l].rearrange("p (j k) -> p j k", k=4), writes=['cw'])
            normw_all = A.alloc([128, DEPTH])
            P.dma('sp', normw_all, normw_d, writes=['normw'])
            normw = normw_all[:, l:l + 1]
            hp = A.alloc([8, 2])
            P.dma('sp', hp, dnhp_d[:, 2 * l:2 * l + 2], writes=['hp'])
            maskL = A.alloc([64, 64])
            maskAT = A.alloc([64, 64])
            selrow = A.alloc([8, 8, 64])
            sel63 = A.alloc([64, 128])
            P.dma('sp', maskL, maskL_d, writes=['maskL'])
            P.dma('sp', maskAT, maskAT_d, writes=['maskAT'])
            P.dma('sp', selrow, selrow_d.rearrange("k (h m) -> k h m", h=8), writes=['selrow'])
            P.dma('sp', sel63, sel63_d, writes=['sel63'])
            ones1 = A.alloc([128, 128])
            P.op('pool', lambda e: e.memset(ones1, 1.0), writes=['ones1'])
            eps6 = A.alloc([128, 1])
            P.op('pool', lambda e: e.memset(eps6, 1e-6), writes=['eps6'])
            one_c = A.alloc([128, 1])
            P.op('pool', lambda e: e.memset(one_c, 1.0), writes=['one_c'])
            gcum = A.alloc([8, S])
            NCK = S // 64
            beta_tm = A.alloc([64, NCK, 8])
            gcum_tm = A.alloc([64, NCK, 8])
            egc_tm = A.alloc([64, NCK, 8])
            nbeta_tm = A.alloc([64, NCK, 8])
            bg_tm = A.alloc([64, NCK, 8])
            etail_tm = A.alloc([64, NCK, 8])
            eglast = A.alloc([128, NCK, 8])
            naexp = A.alloc([8, 1])
            dn_mark = A.mark()
            cmask = A.alloc([8, S])
            P.dma('sp', cmask, cmask_d, writes=['cmask'])
            Bt = A.alloc([8, S])
            At = A.alloc([8, S])
            P.dma('sp', Bt, projT_d[5120:5128, :], reads=['projT_d'], writes=['Bt'])
            P.dma('sp', At, projT_d[5128:5136, :], reads=['projT_d'], writes=['At'])
            P.op('act', lambda e: e.activation(Bt, Bt, AF.Sigmoid), reads=['Bt'], writes=['Bt'])
            P.op('act', lambda e: e.activation(naexp, hp[:, 0:1], AF.Exp), reads=['hp'], writes=['naexp'])
            P.op('dve', lambda e: e.tensor_scalar(naexp, naexp, -1.0, None, ALU.mult), reads=['naexp'], writes=['naexp'])
            P.op('act', lambda e: e.activation(At, At, AF.Exp, bias=hp[:, 1:2], scale=1.0), reads=['At', 'hp'], writes=['At'])
            P.op('act', lambda e: e.activation(At, At, AF.Ln, bias=one_c[0:8, :], scale=1.0), reads=['At', 'one_c'], writes=['At'])
            P.op('dve', lambda e: e.tensor_scalar(At, At, naexp[:, 0:1], None, ALU.mult), reads=['At', 'naexp'], writes=['At'])
            P.op('dve', lambda e: e.tensor_tensor_scan(gcum, cmask, At, 0.0, ALU.mult, ALU.add),
                 reads=['At', 'cmask'], writes=['gcum'])
            for (src_t, skey, dst_t, dkey, pb) in ((Bt, 'Bt', beta_tm, 'beta_tm', 0), (gcum, 'gcum', gcum_tm, 'gcum_tm', 1)):
                def trs(e, src_t=src_t, pb=pb):
                    ins = None
                    for ck in range(NCK):
                        ins = e.transpose(psb[pb][0:64, ck * 8:(ck + 1) * 8], src_t[0:8, ck * 64:(ck + 1) * 64], ident[0:8, 0:8])
                    return ins
                P.op('pe', trs, reads=[skey, 'ident'], writes=[PS(pb)])
                P.op('dve', lambda e, dst_t=dst_t, pb=pb: e.tensor_copy(dst_t.rearrange("p a b -> p (a b)"), psb[pb][0:64, 0:NCK * 8]),
                     reads=[PS(pb)], writes=[dkey])
            P.op('act', lambda e: e.activation(egc_tm, gcum_tm, AF.Exp), reads=['gcum_tm'], writes=['egc_tm'])
            P.op('dve', lambda e: e.tensor_scalar(nbeta_tm, beta_tm, -1.0, None, ALU.mult), reads=['beta_tm'], writes=['nbeta_tm'])
            P.op('dve', lambda e: e.tensor_tensor(bg_tm, beta_tm, egc_tm, ALU.mult), reads=['beta_tm', 'egc_tm'], writes=['bg_tm'])
            P.op('pe', lambda e: e.matmul(psb[2][:, 0:NCK * 8], sel63, gcum_tm.rearrange("p a b -> p (a b)"), start=True, stop=True),
                 reads=['sel63', 'gcum_tm'], writes=[PS(2)])
            P.op('dve', lambda e: e.tensor_tensor(etail_tm.rearrange("p a b -> p (a b)"), psb[2][0:64, 0:NCK * 8],
                                                  gcum_tm.rearrange("p a b -> p (a b)"), ALU.subtract),
                 reads=[PS(2), 'gcum_tm'], writes=['etail_tm'])
            P.op('act', lambda e: e.activation(etail_tm, etail_tm, AF.Exp), reads=['etail_tm'], writes=['etail_tm'])
            P.op('act', lambda e: e.activation(eglast.rearrange("p a b -> p (a b)"), psb[2][:, 0:NCK * 8], AF.Exp),
                 reads=[PS(2)], writes=['eglast'])
            scal_keys = ['beta_tm', 'gcum_tm', 'egc_tm', 'nbeta_tm', 'bg_tm', 'etail_tm', 'eglast']

            import os as _os
            DNS = _os.environ.get('DN_STOP', '')
            if DNS == 'gates':
                return 'STOP'
            for hg in range(2):
                P.barrier()
                A.release(dn_mark)
                HG = 4
                qT = A.alloc([128, HG, S])
                kT = A.alloc([128, HG, S])
                vT = A.alloc([128, HG, S])
                zs = A.alloc([128, HG, S], BF16)
                yT = A.alloc([128, HG, S], BF16)
                Sst = A.alloc([128, HG, 128])
                P.op('pool', lambda e: e.memset(Sst, 0.0), writes=['Sst'])
                conv_mark = A.mark()
                cin = [A.alloc([128, 3 + S]) for i in range(2)]
                cacc = [A.alloc([128, S]) for i in range(2)]
                csq = A.alloc([128, S])
                crs = A.alloc([128, S])
                for i in range(2):
                    P.op('pool', lambda e, i=i: e.memset(cin[i][:, 0:3], 0.0), writes=['cin%d' % i])
                ci = 0
                for hh in range(HG):
                    h = hg * HG + hh
                    for which, dstT in ((0, qT), (1, kT), (2, vT)):
                        jch = which * 8 + h
                        b = ci % 2
                        ci += 1
                        P.dma('sp', cin[b][:, 3:3 + S], projT_d[1024 + jch * 128:1024 + (jch + 1) * 128, :],
                              reads=['projT_d'], writes=['cin%d' % b])
                        acc = cacc[b]
                        ak = 'cacc%d' % b
                        eng = 'dve'
                        P.op(eng, lambda e, acc=acc, b=b, jch=jch: e.tensor_scalar(acc, cin[b][:, 3:3 + S], cw[:, jch, 3:4], None, ALU.mult),
                             reads=['cin%d' % b, 'cw'], writes=[ak])
                        for kk in range(3):
                            P.op(eng, lambda e, acc=acc, b=b, jch=jch, kk=kk: e.scalar_tensor_tensor(
                                acc, cin[b][:, kk:kk + S], cw[:, jch, kk:kk + 1], acc, ALU.mult, ALU.add),
                                reads=['cin%d' % b, 'cw', ak], writes=[ak])
                        if which == 2:
                            P.op('act', lambda e, acc=acc, hh=hh: e.activation(vT[:, hh, :], acc, AF.Silu),
                                 reads=[ak], writes=[('vT', hh)])
                            continue
                        P.op('act', lambda e, acc=acc: e.activation(acc, acc, AF.Silu), reads=[ak], writes=[ak])
                        P.op('act', lambda e, acc=acc: e.activation(csq, acc, AF.Square), reads=[ak], writes=['csq'])
                        for t4 in range(4):
                            pb = t4 % 2
                            P.op('pe', lambda e, pb=pb, t4=t4: e.matmul(psb[pb][:], ones1, csq[:, t4 * 512:(t4 + 1) * 512], start=True, stop=True),
                                 reads=['ones1', 'csq'], writes=[PS(pb)])
                            P.op('act', lambda e, pb=pb, t4=t4: e.activation(crs[:, t4 * 512:(t4 + 1) * 512], psb[pb][:], AF.Sqrt,
                                                                             bias=eps6, scale=1.0),
                                 reads=[PS(pb), 'eps6'], writes=['crs'])
                        P.op('dve', lambda e: e.reciprocal(crs, crs), reads=['crs'], writes=['crs'])
                        sc = (128.0 ** -0.5) if which == 0 else 1.0
                        P.op('dve', lambda e, acc=acc, dstT=dstT, hh=hh, sc=sc: e.scalar_tensor_tensor(
                            dstT[:, hh, :], acc, sc, crs, ALU.mult, ALU.mult),
                            reads=[ak, 'crs'], writes=[('qT' if which == 0 else 'kT', hh)])
                    b = ci % 2
                    ci += 1
                    P.dma('sp', cin[b][:, 3:3 + S], projT_d[4096 + h * 128:4096 + (h + 1) * 128, :],
                          reads=['projT_d'], writes=['cin%d' % b])
                    P.op('act', lambda e, b=b, hh=hh: e.activation(zs[:, hh, :], cin[b][:, 3:3 + S], AF.Silu),
                         reads=['cin%d' % b], writes=[('zs', hh)])
                qkeys = [('qT', hh) for hh in range(HG)]
                kkeys = [('kT', hh) for hh in range(HG)]
                vkeys = [('vT', hh) for hh in range(HG)]
                if DNS == 'conv':
                    return 'STOP'
                P.barrier()
                A.release(conv_mark)
                NB = 2
                ktm = [A.alloc([64, HG, 128]) for i in range(NB)]
                vb = [A.alloc([64, HG, 128]) for i in range(NB)]
                kbg = [A.alloc([64, HG, 128]) for i in range(NB)]
                ktail = [A.alloc([64, HG, 128]) for i in range(NB)]
                dL = [A.alloc([64, HG, 64]) for i in range(NB)]
                dAT = [A.alloc([64, HG, 64]) for i in range(NB)]
                Pm = [A.alloc([64, 2, HG, 64]) for i in range(NB)]
                XT = [A.alloc([64, HG, 64]) for i in range(NB)]
                AinT = [A.alloc([64, HG, 64]) for i in range(NB)]
                Wv = [A.alloc([64, HG, 128]) for i in range(NB)]
                KcT = [A.alloc([128, HG, 64]) for i in range(NB)]
                vnew = A.alloc([64, HG, 128])
                o1 = A.alloc([64, HG, 128])
                osb = A.alloc([64, HG, 128])
                osq = A.alloc([64, HG, 128])
                oss = A.alloc([64, HG])
                identH = A.alloc([64, HG, 64])
                for hh in range(HG):
                    P.op('dve', lambda e, hh=hh: e.tensor_copy(identH[:, hh, :], ident[0:64, 0:64]), reads=['ident'], writes=['identH'])
                ppb = [0]

                def prep_pb():
                    pb = 4 + (ppb[0] % 4)
                    ppb[0] += 1
                    return pb
                pending_rec = []
                for ck in range(NCK):
                    b = ck % NB
                    c0 = ck * 64
                    bk = lambda nm, b=b: '%s%d' % (nm, b)
                    P.rec = []
                    pk = prep_pb()

                    def trk(e, pk=pk, c0=c0):
                        ins = None
                        for hh in range(HG):
                            ins = e.transpose(psb[pk][0:64, hh * 128:(hh + 1) * 128], kT[:, hh, c0:c0 + 64], ident[:])
                        return ins
                    P.op('pe', trk, reads=kkeys + ['ident'], writes=[PS(pk)])
                    pv = prep_pb()

                    def trv(e, pv=pv, c0=c0):
                        ins = None
                        for hh in range(HG):
                            ins = e.transpose(psb[pv][0:64, hh * 128:(hh + 1) * 128], vT[:, hh, c0:c0 + 64], ident[:])
                        return ins
                    P.op('pe', trv, reads=vkeys + ['ident'], writes=[PS(pv)])
                    for hh in range(HG):
                        h = hg * HG + hh
                        P.op('dve', lambda e, b=b, hh=hh, h=h, pk=pk, ck=ck: e.tensor_scalar(
                            kbg[b][:, hh, :], psb[pk][0:64, hh * 128:(hh + 1) * 128], bg_tm[:, ck, h:h + 1], None, ALU.mult),
                            reads=[PS(pk)] + scal_keys, writes=[bk('kbg')])
                        P.op('dve', lambda e, b=b, hh=hh, h=h, pk=pk, ck=ck: e.tensor_scalar(
                            ktail[b][:, hh, :], psb[pk][0:64, hh * 128:(hh + 1) * 128], etail_tm[:, ck, h:h + 1], None, ALU.mult),
                            reads=[PS(pk)] + scal_keys, writes=[bk('ktail')])
                        P.op('act', lambda e, b=b, hh=hh, h=h, pv=pv, ck=ck: e.activation(
                            vb[b][:, hh, :], psb[pv][0:64, hh * 128:(hh + 1) * 128], AF.Copy, scale=beta_tm[:, ck, h:h + 1]),
                            reads=[PS(pv)] + scal_keys, writes=[bk('vb')])
                    pbc = prep_pb()

                    def mbc(e, pbc=pbc, c0=c0, hg=hg):
                        ins = None
                        for hh in range(HG):
                            h = hg * HG + hh
                            ins = e.matmul(psb[pbc][0:64, hh * 64:(hh + 1) * 64], selrow[:, h, :], gcum[:, c0:c0 + 64], start=True, stop=True)
                        return ins
                    P.op('pe', mbc, reads=['selrow', 'gcum'], writes=[PS(pbc)])
                    for hh in range(HG):
                        h = hg * HG + hh
                        P.op('dve', lambda e, b=b, hh=hh, h=h, pbc=pbc, ck=ck: e.tensor_scalar(
                            dL[b][:, hh, :], psb[pbc][0:64, hh * 64:(hh + 1) * 64], -1.0, gcum_tm[:, ck, h:h + 1], ALU.mult, ALU.add),
                            reads=[PS(pbc)] + scal_keys, writes=[bk('dL')])
                        P.op('dve', lambda e, b=b, hh=hh, h=h, pbc=pbc, ck=ck: e.tensor_scalar(
                            dAT[b][:, hh, :], psb[pbc][0:64, hh * 64:(hh + 1) * 64], gcum_tm[:, ck, h:h + 1], None, ALU.subtract),
                            reads=[PS(pbc)] + scal_keys, writes=[bk('dAT')])
                    mLb = maskL.unsqueeze(1).to_broadcast([64, HG, 64])
                    mAb = maskAT.unsqueeze(1).to_broadcast([64, HG, 64])
                    P.op('dve', lambda e, b=b, mLb=mLb: e.scalar_tensor_tensor(dL[b], dL[b], 0.0, mLb, ALU.min, ALU.add),
                         reads=[bk('dL'), 'maskL'], writes=[bk('dL')])
                    P.op('dve', lambda e, b=b, mAb=mAb: e.scalar_tensor_tensor(dAT[b], dAT[b], 0.0, mAb, ALU.min, ALU.add),
                         reads=[bk('dAT'), 'maskAT'], writes=[bk('dAT')])
                    P.op('act', lambda e, b=b: e.activation(dL[b], dL[b], AF.Exp), reads=[bk('dL')], writes=[bk('dL')])
                    P.op('act', lambda e, b=b: e.activation(dAT[b], dAT[b], AF.Exp), reads=[bk('dAT')], writes=[bk('dAT')])
                    pkk = prep_pb()

                    def mkk(e, pkk=pkk, c0=c0):
                        ins = None
                        for hh in range(HG):
                            ins = e.matmul(psb[pkk][0:64, hh * 64:(hh + 1) * 64], kT[:, hh, c0:c0 + 64], kT[:, hh, c0:c0 + 64], start=True, stop=True)
                        for hh in range(HG):
                            ins = e.matmul(psb[pkk][0:64, 256 + hh * 64:256 + (hh + 1) * 64], kT[:, hh, c0:c0 + 64], qT[:, hh, c0:c0 + 64],
                                           start=True, stop=True)
                        return ins
                    P.op('pe', mkk, reads=kkeys + qkeys, writes=[PS(pkk)])
                    for hh in range(HG):
                        h = hg * HG + hh
                        P.op('dve', lambda e, b=b, hh=hh, h=h, pkk=pkk, ck=ck: e.scalar_tensor_tensor(
                            Pm[b][:, 0, hh, :], psb[pkk][0:64, hh * 64:(hh + 1) * 64], nbeta_tm[:, ck, h:h + 1], dL[b][:, hh, :], ALU.mult, ALU.mult),
                            reads=[PS(pkk), bk('dL')] + scal_keys, writes=[bk('Pm')])
                    P.op('dve', lambda e, b=b, pkk=pkk: e.tensor_tensor(AinT[b].rearrange("p a b -> p (a b)"), psb[pkk][0:64, 256:512],
                                                                       dAT[b].rearrange("p a b -> p (a b)"), ALU.mult),
                         reads=[PS(pkk), bk('dAT')], writes=[bk('AinT')])
                    pq = prep_pb()

                    def trq(e, pq=pq, b=b):
                        ins = None
                        for hh in range(HG):
                            ins = e.transpose(psb[pq][0:64, hh * 64:(hh + 1) * 64], Pm[b][:, 0, hh, :], ident[0:64, 0:64])
                        return ins
                    P.op('pe', trq, reads=[bk('Pm'), 'ident'], writes=[PS(pq)])
                    P.op('act', lambda e, b=b, pq=pq: e.copy(Pm[b][:, 1].rearrange("p a b -> p (a b)"), psb[pq][0:64, 0:256]),
                         reads=[PS(pq)], writes=[bk('Pm')])
                    P.op('dve', lambda e, b=b: e.tensor_tensor(XT[b], Pm[b][:, 1], identH, ALU.add),
                         reads=[bk('Pm'), 'identH'], writes=[bk('XT')])
                    for lev in range(1, 6):
                        pp = prep_pb()
                        last = (lev == 5)

                        def msq(e, pp=pp, b=b, last=last):
                            ins = None
                            for hh in range(HG):
                                ins = e.matmul(psb[pp][0:64, hh * 64:(hh + 1) * 64], Pm[b][:, 1, hh, :], Pm[b][:, 0, hh, :], start=True, stop=True)
                            if not last:
                                for hh in range(HG):
                                    ins = e.matmul(psb[pp][0:64, 256 + hh * 64:256 + (hh + 1) * 64], Pm[b][:, 0, hh, :], Pm[b][:, 1, hh, :],
                                                   start=True, stop=True)
                            return ins
                        P.op('pe', msq, reads=[bk('Pm')], writes=[PS(pp)])
                        ncol = 256 if last else 512
                        P.op('act', lambda e, b=b, pp=pp, ncol=ncol: e.copy(Pm[b].rearrange("p a b c -> p (a b c)")[:, 0:ncol], psb[pp][0:64, 0:ncol]),
                             reads=[PS(pp)], writes=[bk('Pm')])
                        px = prep_pb()

                        def mx(e, px=px, b=b):
                            ins = None
                            for hh in range(HG):
                                ins = e.matmul(psb[px][0:64, hh * 64:(hh + 1) * 64], Pm[b][:, 0, hh, :], XT[b][:, hh, :], start=True, stop=True)
                            return ins
                        P.op('pe', mx, reads=[bk('Pm'), bk('XT')], writes=[PS(px)])
                        P.op('dve', lambda e, b=b, px=px: e.tensor_tensor(XT[b].rearrange("p a b -> p (a b)"), XT[b].rearrange("p a b -> p (a b)"),
                                                                         psb[px][0:64, 0:256], ALU.add),
                             reads=[PS(px), bk('XT')], writes=[bk('XT')])
                    pw = prep_pb()

                    def mwv(e, pw=pw, b=b):
                        ins = None
                        for hh in range(HG):
                            ins = e.matmul(psb[pw][0:64, hh * 128:(hh + 1) * 128], XT[b][:, hh, :], vb[b][:, hh, :], start=True, stop=True)
                        return ins
                    P.op('pe', mwv, reads=[bk('XT'), bk('vb')], writes=[PS(pw)])
                    P.op('act', lambda e, b=b, pw=pw: e.copy(Wv[b].rearrange("p a b -> p (a b)"), psb[pw][0:64, :]),
                         reads=[PS(pw)], writes=[bk('Wv')])
                    pc = prep_pb()

                    def mkc(e, pc=pc, b=b):
                        ins = None
                        for hh in range(HG):
                            ins = e.matmul(psb[pc][:, hh * 64:(hh + 1) * 64], kbg[b][:, hh, :], XT[b][:, hh, :], start=True, stop=True)
                        return ins
                    P.op('pe', mkc, reads=[bk('XT'), bk('kbg')], writes=[PS(pc)])
                    P.op('dve', lambda e, b=b, pc=pc: e.tensor_copy(KcT[b].rearrange("p a b -> p (a b)"), psb[pc][:, 0:256]),
                         reads=[PS(pc)], writes=[bk('KcT')])
                    prep_list = P.rec
                    P.rec = []
                    def r1(e, b=b):
                        ins = None
                        for hh in range(HG):
                            ins = e.matmul(psb[0][0:64, hh * 128:(hh + 1) * 128], KcT[b][:, hh, :], Sst[:, hh, :], start=True, stop=True)
                        return ins
                    P.op('pe', r1, reads=[bk('KcT'), 'Sst'], writes=[PS(0)])
                    P.op('dve', lambda e, b=b: e.tensor_tensor(vnew.rearrange("p a b -> p (a b)"), Wv[b].rearrange("p a b -> p (a b)"),
                                                               psb[0][0:64, :], ALU.subtract),
                         reads=[PS(0), bk('Wv')], writes=['vnew'])

                    def r2(e, b=b, c0=c0):
                        ins = None
                        for hh in range(HG):
                            ins = e.matmul(psb[1][0:64, hh * 128:(hh + 1) * 128], qT[:, hh, c0:c0 + 64], Sst[:, hh, :], start=True, stop=True)
                        for hh in range(HG):
                            ins = e.matmul(psb[2][0:64, hh * 128:(hh + 1) * 128], AinT[b][:, hh, :], vnew[:, hh, :], start=True, stop=True)
                        for hh in range(HG):
                            ins = e.matmul(psb[3][:, hh * 128:(hh + 1) * 128], ktail[b][:, hh, :], vnew[:, hh, :], start=True, stop=True)
                        return ins
                    P.op('pe', r2, reads=qkeys + ['Sst', bk('AinT'), 'vnew', bk('ktail')], writes=[PS(1), PS(2), PS(3)])
                    for hh in range(HG):
                        h = hg * HG + hh
                        P.op('act', lambda e, hh=hh, h=h, ck=ck: e.activation(o1[:, hh, :], psb[1][0:64, hh * 128:(hh + 1) * 128], AF.Copy,
                                                                             scale=egc_tm[:, ck, h:h + 1]),
                             reads=[PS(1)] + scal_keys, writes=['o1'])
                        P.op('dve', lambda e, hh=hh, h=h, ck=ck: e.scalar_tensor_tensor(
                            Sst[:, hh, :], Sst[:, hh, :], eglast[:, ck, h:h + 1], psb[3][:, hh * 128:(hh + 1) * 128], ALU.mult, ALU.add),
                            reads=[PS(3), 'Sst'] + scal_keys, writes=['Sst'])
                    P.op('dve', lambda e: e.tensor_tensor(osb.rearrange("p a b -> p (a b)"), o1.rearrange("p a b -> p (a b)"),
                                                          psb[2][0:64, :], ALU.add),
                         reads=[PS(2), 'o1'], writes=['osb'])
                    P.op('pool', lambda e: e.tensor_tensor(osq, osb, osb, ALU.mult), reads=['osb'], writes=['osq'])
                    P.op('dve', lambda e: e.tensor_reduce(oss, osq, mybir.AxisListType.X, ALU.add), reads=['osq'], writes=['oss'])
                    P.op('dve', lambda e: e.tensor_scalar(oss, oss, 1.0 / 128.0, 1e-6, ALU.mult, ALU.add), reads=['oss'], writes=['oss'])
                    P.op('act', lambda e: e.activation(oss, oss, AF.Sqrt), reads=['oss'], writes=['oss'])
                    P.op('dve', lambda e: e.reciprocal(oss, oss), reads=['oss'], writes=['oss'])
                    P.op('dve', lambda e: e.tensor_tensor(osb, osb, oss.unsqueeze(2).to_broadcast([64, HG, 128]), ALU.mult),
                         reads=['osb', 'oss'], writes=['osb'])
                    po = prep_pb()

                    def tro(e, po=po):
                        ins = None
                        for hh in range(HG):
                            ins = e.transpose(psb[po][:, hh * 64:(hh + 1) * 64], osb[:, hh, :], ident[0:64, 0:64])
                        return ins
                    P.op('pe', tro, reads=['osb', 'ident'], writes=[PS(po)])
                    P.op('dve', lambda e, po=po, c0=c0: e.scalar_tensor_tensor(
                        yT[:, :, c0:c0 + 64], psb[po][:, 0:256].rearrange("p (a b) -> p a b", a=HG), normw[:, 0:1], zs[:, :, c0:c0 + 64],
                        ALU.mult, ALU.mult),
                        reads=[PS(po), 'normw'] + [('zs', hh) for hh in range(HG)], writes=['yT'])
                    rec_list = P.rec
                    P.rec = None
                    P.replay_merged(prep_list, pending_rec, ratio=4)
                    pending_rec = rec_list
                P.replay_merged(pending_rec, [])
                for hh in range(HG):
                    h = hg * HG + hh
                    P.dma('sp', ycatT_d[1024 + h * 128:1024 + (h + 1) * 128, :], yT[:, hh, :], reads=['yT'], writes=['ycatT_d'])
            if stop_after == 'D':
                return 'STOP'

            return None
        if phase_2() == 'STOP':
            return finish_debug(C)
        def phase_3(l=l, mo=mo):
            P.barrier()
            A.reset()
            ycat = A.alloc([128, NCH, S], BF16)
            ycat_v = ycatT_d.rearrange("(k p) t -> p k t", p=128)
            for k in range(NCH):
                P.dma('sp', ycat[:, k, :], ycat_v[:, k, :], reads=['ycatT_d'], writes=[('ycat', k)])
            ykeys = [('ycat', k) for k in range(NCH)]
            if stop_after == 'E0':
                return 'STOP'
            wt = [A.alloc([128, NCH, 512], BF16) for i in range(2)]
            xb = [A.alloc([128, 512]) for i in range(3)]
            rb = [A.alloc([128, 512]) for i in range(3)]
            w_out_v = w_out[l].rearrange("(k p) n -> p k n", p=128)
            cnt = 0
            for ng in range(4):
                b = ng % 2
                if not _os.environ.get('E_NOPOOL'):
                    P.dma('pool', wt[b], w_out_v[:, :, ng * 512:(ng + 1) * 512], writes=['wt%d' % b])
                for nc_ in range(4):
                    dch = ng * 4 + nc_
                    for t4 in range(4):
                        pb = cnt % 4
                        xi = cnt % 3
                        cnt += 1

                        def mm(e, b=b, nc_=nc_, t4=t4, pb=pb):
                            ins = None
                            for k in range(NCH):
                                ins = e.matmul(psb[pb][:], wt[b][:, k, nc_ * 128:(nc_ + 1) * 128],
                                               ycat[:, k, t4 * 512:(t4 + 1) * 512], start=(k == 0), stop=(k == NCH - 1))
                            return ins
                        if not _os.environ.get('E_NOPE'):
                            P.op('pe', mm, reads=['wt%d' % b] + ykeys, writes=[PS(pb)])
                        if not _os.environ.get('E_NOX'):
                            P.dma(_os.environ.get('E_XQ', 'sp'), xb[xi], xT_d[dch * 128:(dch + 1) * 128, t4 * 512:(t4 + 1) * 512],
                                  reads=['xT_d'], writes=['xb%d' % xi])
                        if not _os.environ.get('E_NOACT'):
                            P.op('act', lambda e, xi=xi: e.activation(xb[xi], xb[xi], AF.Copy, scale=ALPHA),
                                 reads=['xb%d' % xi], writes=['xb%d' % xi])
                        gt1 = modT[:, mo + 32 + dch:mo + 32 + dch + 1]
                        if not _os.environ.get('E_NODVE'):
                            P.op('dve', lambda e, xi=xi, pb=pb, gt1=gt1: e.scalar_tensor_tensor(rb[xi], psb[pb][:], gt1, xb[xi], ALU.mult, ALU.add),
                                 reads=[PS(pb), 'xb%d' % xi, 'modT'], writes=['rb%d' % xi])
                        if not _os.environ.get('E_NOSTORE'):
                            P.dma('sp', rT_d[dch * 128:(dch + 1) * 128, t4 * 512:(t4 + 1) * 512], rb[xi],
                                  reads=['rb%d' % xi], writes=['rT_d'])
            if stop_after == 'E1':
                return 'STOP'
            P.barrier()
            A.reset()
            L = ln_alloc()
            x1t = [A.alloc([128, NCH, TB]) for i in range(2)]
            hft = [A.alloc([128, NCH, TB], BF16) for i in range(2)]
            rT_v = rT_d.rearrange("(j p) t -> p j t", p=128)
            hfT_v = hfT_d.rearrange("(j p) t -> p j t", p=128)
            lo = l * 64
            for tb in range(S // TB):
                b = tb % 2
                P.dma('sp', L.x[b], rT_v[:, :, tb * TB:(tb + 1) * TB], reads=['rT_d'], writes=['ln_x%d' % b])
                ln_block(L, L.x[b], 'ln_x%d' % b,
                         lambda j: lnp[:, lo + j:lo + j + 1],
                         lambda j: lnp[:, lo + 16 + j:lo + 16 + j + 1],
                         lambda j, b=b: x1t[b][:, j, :],
                         lambda j, b=b: 'x1t%d' % b)
                P.dma('sp', xT_v[:, :, tb * TB:(tb + 1) * TB], x1t[b], reads=['x1t%d' % b], writes=['xT_d'])
                ln_block(L, x1t[b], 'x1t%d' % b,
                         lambda j: modT[:, mo + 64 + j:mo + 64 + j + 1],
                         lambda j: modT[:, mo + 48 + j:mo + 48 + j + 1],
                         lambda j, b=b: hft[b][:, j, :],
                         lambda j, b=b: 'hft%d' % b)
                P.dma('sp', hfT_v[:, :, tb * TB:(tb + 1) * TB], hft[b], reads=['hft%d' % b], writes=['hfT_d'])
            if stop_after == 'E':
                return 'STOP'

            return None
        if phase_3() == 'STOP':
            return finish_debug(C)
        def phase_4(l=l, mo=mo):
            P.barrier()
            A.reset()
            hf = A.alloc([128, NCH, S], BF16)
            hfT_v = hfT_d.rearrange("(k p) t -> p k t", p=128)
            for k in range(NCH):
                P.dma('sp', hf[:, k, :], hfT_v[:, k, :], reads=['hfT_d'], writes=[('hf', k)])
            hkeys = [('hf', k) for k in range(NCH)]
            wt = [A.alloc([128, NCH, 512], BF16) for i in range(2)]
            stg = [A.alloc([128, 512]) for i in range(4)]
            wq_v = wq_d[l].rearrange("(k p) n -> p k n", p=128)
            cnt = 0
            for ng in range(4):
                b = ng % 2
                P.dma('pool', wt[b], wq_v[:, :, ng * 512:(ng + 1) * 512], writes=['wt%d' % b])
                for nc_ in range(4):
                    for t4 in range(4):
                        pb = cnt % 4
                        cnt += 1

                        def mm(e, b=b, nc_=nc_, t4=t4, pb=pb):
                            ins = None
                            for k in range(NCH):
                                ins = e.matmul(psb[pb][:], wt[b][:, k, nc_ * 128:(nc_ + 1) * 128],
                                               hf[:, k, t4 * 512:(t4 + 1) * 512], start=(k == 0), stop=(k == NCH - 1))
                            return ins
                        P.op('pe', mm, reads=['wt%d' % b] + hkeys, writes=[PS(pb)])
                        if pb % 2 == 0:
                            P.op('act', lambda e, pb=pb: e.copy(stg[pb], psb[pb][:]), reads=[PS(pb)], writes=['stg%d' % pb])
                        else:
                            P.op('dve', lambda e, pb=pb: e.tensor_copy(stg[pb], psb[pb][:]), reads=[PS(pb)], writes=['stg%d' % pb])
                        r0 = ng * 512 + nc_ * 128
                        P.dma('sp', qT_d[r0:r0 + 128, t4 * 512:(t4 + 1) * 512], stg[pb], reads=['stg%d' % pb], writes=['qT_d'])
            P.barrier()
            A.reset()
            NEGB = -1.0e30
            U32 = mybir.dt.uint32
            keysT = A.alloc([128, 16, 128])
            P.dma('sp', keysT, keysT_d[l].rearrange("p (c k) -> p c k", c=16), writes=['keysT'])
            iota = A.alloc([128, 128])
            P.dma('sp', iota, iota_d, writes=['iota'])
            qt = [A.alloc([128, 16, 128]) for i in range(2)]
            sc = A.alloc([128, 16, 128])
            sc2 = A.alloc([128, 16, 128])
            tv = A.alloc([128, 16, 16])
            tiu = A.alloc([128, 16, 16]).bitcast(U32)
            tif = A.alloc([128, 16, 16])
            cand = A.alloc([128, 8, 256])
            cand2 = A.alloc([128, 8, 256])
            bv = A.alloc([128, 8, 16])
            bpu = A.alloc([128, 8, 16]).bitcast(U32)
            rcu = A.alloc([128, 2, 8, 16]).bitcast(U32)
            rcf = A.alloc([128, 2, 8, 16])
            eq = A.alloc([128, 8, 16, 16])
            sel = A.alloc([128, 3, 8, 16])
            gz = A.alloc([128, 8])
            selT = A.alloc([128, 3, 128])
            OJ = A.alloc([128, 128, 128], BF16)
            OI = A.alloc([128, 128, 128], BF16)
            X = [A.alloc([128, 128, 128], BF16) for i in range(2)]
            qT_v = qT_d.rearrange("(c p) t -> p c t", p=128)
            Gd_v = Gd.rearrange("i j t -> j i t")
            evn = 0
            for tt in range(S // 128):
                b = tt % 2
                P.dma('sp', qt[b], qT_v[:, :, tt * 128:(tt + 1) * 128], reads=['qT_d'], writes=['qt%d' % b])
                for c4 in range(4):
                    def msc(e, b=b, c4=c4):
                        ins = None
                        for cc in range(4):
                            c = c4 * 4 + cc
                            ins = e.matmul(psb[c4][:, cc * 128:(cc + 1) * 128], qt[b][:, c, :], keysT[:, c, :], start=True, stop=True)
                        return ins
                    P.op('pe', msc, reads=['qt%d' % b, 'keysT'], writes=[PS(c4)])
                    if c4 % 2 == 0:
                        P.op('act', lambda e, c4=c4: e.copy(sc[:, c4 * 4:(c4 + 1) * 4, :].rearrange("p a b -> p (a b)"), psb[c4][:]),
                             reads=[PS(c4)], writes=[('sc', c4)])
                    else:
                        P.op('dve', lambda e, c4=c4: e.tensor_copy(sc[:, c4 * 4:(c4 + 1) * 4, :].rearrange("p a b -> p (a b)"), psb[c4][:]),
                             reads=[PS(c4)], writes=[('sc', c4)])
                chains = []
                for c in range(16):
                    P.rec = []
                    sk = ('sc', c // 4)
                    P.op('dve', lambda e, c=c: e.max(tv[:, c, 0:8], sc[:, c, :]), reads=[sk], writes=[('tv', c)])
                    P.op('dve', lambda e, c=c: e.max_index(tiu[:, c, 0:8], tv[:, c, 0:8], sc[:, c, :]), reads=[sk, ('tv', c)], writes=[('tiu', c)])
                    P.op('dve', lambda e, c=c: e.match_replace(sc2[:, c, :], tv[:, c, 0:8], sc[:, c, :], NEGB), reads=[sk, ('tv', c)], writes=[('sc2', c)])
                    P.op('dve', lambda e, c=c: e.max(tv[:, c, 8:16], sc2[:, c, :]), reads=[('sc2', c)], writes=[('tv', c)])
                    P.op('dve', lambda e, c=c: e.max_index(tiu[:, c, 8:16], tv[:, c, 8:16], sc2[:, c, :]), reads=[('sc2', c), ('tv', c)], writes=[('tiu', c)])
                    chains.append(P.rec)
                    P.rec = None
                for step in range(5):
                    for ch_ in chains:
                        P.replay(ch_[step])
                tvk = [('tv', c) for c in range(16)]
                tik = [('tiu', c) for c in range(16)]
                P.op('dve', lambda e: e.tensor_copy(tif, tiu), reads=tik, writes=['tif'])
                tv4 = tv.rearrange("p (h two) k -> p h two k", two=2)
                P.op('dve', lambda e, tv4=tv4: e.tensor_tensor(
                    cand.rearrange("p h (r c) -> p h r c", r=16),
                    tv4[:, :, 0, :].unsqueeze(3).to_broadcast([128, 8, 16, 16]),
                    tv4[:, :, 1, :].unsqueeze(2).to_broadcast([128, 8, 16, 16]), ALU.add),
                    reads=tvk, writes=['cand'])
                chains = []
                for h in range(8):
                    P.rec = []
                    P.op('dve', lambda e, h=h: e.max(bv[:, h, 0:8], cand[:, h, :]), reads=['cand'], writes=[('bv', h)])
                    P.op('dve', lambda e, h=h: e.max_index(bpu[:, h, 0:8], bv[:, h, 0:8], cand[:, h, :]), reads=['cand', ('bv', h)], writes=[('bpu', h)])
                    P.op('dve', lambda e, h=h: e.match_replace(cand2[:, h, :], bv[:, h, 0:8], cand[:, h, :], NEGB), reads=['cand', ('bv', h)], writes=[('cand2', h)])
                    P.op('dve', lambda e, h=h: e.max(bv[:, h, 8:16], cand2[:, h, :]), reads=[('cand2', h)], writes=[('bv', h)])
                    P.op('dve', lambda e, h=h: e.max_index(bpu[:, h, 8:16], bv[:, h, 8:16], cand2[:, h, :]), reads=[('cand2', h), ('bv', h)], writes=[('bpu', h)])
                    chains.append(P.rec)
                    P.rec = None
                for step in range(5):
                    for ch_ in chains:
                        P.replay(ch_[step])
                bvk = [('bv', h) for h in range(8)]
                bpk = [('bpu', h) for h in range(8)]
                P.op('dve', lambda e: e.tensor_scalar(rcu[:, 0], bpu, 4, None, ALU.logical_shift_right), reads=bpk, writes=['rcu0'])
                P.op('dve', lambda e: e.tensor_scalar(rcu[:, 1], bpu, 15, None, ALU.bitwise_and), reads=bpk, writes=['rcu1'])
                P.op('dve', lambda e: e.tensor_copy(rcf, rcu), reads=['rcu0', 'rcu1'], writes=['rcf'])
                tif4 = tif.rearrange("p (h two) k -> p h two k", two=2)
                io16 = iota[:, 0:16].unsqueeze(1).unsqueeze(1).to_broadcast([128, 8, 16, 16])
                for w in range(2):
                    P.op('dve', lambda e, w=w, io16=io16: e.tensor_tensor(eq, rcf[:, w].unsqueeze(3).to_broadcast([128, 8, 16, 16]), io16, ALU.is_equal),
                         reads=['rcf', 'iota'], writes=['eq'])
                    P.op('dve', lambda e, w=w, tif4=tif4: e.tensor_tensor(eq, eq, tif4[:, :, w, :].unsqueeze(2).to_broadcast([128, 8, 16, 16]), ALU.mult),
                         reads=['eq', 'tif'], writes=['eq'])
                    P.op('dve', lambda e, w=w: e.tensor_reduce(sel[:, w], eq, mybir.AxisListType.X, ALU.add), reads=['eq'], writes=[('sel', w)])
                P.op('dve', lambda e: e.tensor_tensor(sel[:, 2], bv, bv[:, :, 0:1].to_broadcast([128, 8, 16]), ALU.subtract),
                     reads=bvk, writes=[('sel', 2)])
                P.op('act', lambda e: e.activation(sel[:, 2], sel[:, 2], AF.Exp), reads=[('sel', 2)], writes=[('sel', 2)])
                P.op('dve', lambda e: e.tensor_reduce(gz, sel[:, 2], mybir.AxisListType.X, ALU.add), reads=[('sel', 2)], writes=['gz'])
                P.op('dve', lambda e: e.reciprocal(gz, gz), reads=['gz'], writes=['gz'])
                P.op('dve', lambda e: e.tensor_tensor(sel[:, 2], sel[:, 2], gz.unsqueeze(2).to_broadcast([128, 8, 16]), ALU.mult),
                     reads=[('sel', 2), 'gz'], writes=[('sel', 2)])
                def trs(e):
                    ins = None
                    for w in range(3):
                        ins = e.transpose(psb[4][:, w * 128:(w + 1) * 128], sel[:, w].rearrange("p h k -> p (h k)"), ident[:])
                    return ins
                P.op('pe', trs, reads=[('sel', 0), ('sel', 1), ('sel', 2), 'ident'], writes=[PS(4)])
                P.op('act', lambda e: e.copy(selT.rearrange("p a b -> p (a b)"), psb[4][:, 0:384]), reads=[PS(4)], writes=['selT'])
                iob = iota.unsqueeze(1).to_broadcast([128, 128, 128])
                P.op('dve', lambda e, iob=iob: e.tensor_tensor(OJ, iob, selT[:, 1, :].unsqueeze(2).to_broadcast([128, 128, 128]), ALU.is_equal),
                     reads=['selT', 'iota'], writes=['OJ'])
                P.op('dve', lambda e, iob=iob: e.tensor_tensor(OI, iob, selT[:, 0, :].unsqueeze(2).to_broadcast([128, 128, 128]), ALU.is_equal),
                     reads=['selT', 'iota'], writes=['OI'])
                P.op('pool', lambda e: e.tensor_tensor(OI, OI, selT[:, 2, :].unsqueeze(2).to_broadcast([128, 128, 128]), ALU.mult),
                     reads=['selT', 'OI'], writes=['OI'])
                Xv = X[b].rearrange("p i t -> p t i")
                for t4 in range(32):
                    pb = 5 + (t4 % 3)

                    def mg(e, pb=pb, t4=t4):
                        ins = None
                        for q_ in range(4):
                            t = t4 * 4 + q_
                            ins = e.matmul(psb[pb][:, q_ * 128:(q_ + 1) * 128], OJ[:, t, :], OI[:, t, :], start=True, stop=True)
                        return ins
                    P.op('pe', mg, reads=['OJ', 'OI'], writes=[PS(pb)])
                    src_ps = psb[pb][:].rearrange("p (a b) -> p a b", a=4)
                    dst = Xv[:, t4 * 4:(t4 + 1) * 4, :]
                    if evn % 4 != 3:
                        P.op('act', lambda e, dst=dst, src_ps=src_ps: e.copy(dst, src_ps), reads=[PS(pb)], writes=[('X', b, t4)])
                    else:
                        P.op('dve', lambda e, dst=dst, src_ps=src_ps: e.tensor_copy(dst, src_ps), reads=[PS(pb)], writes=[('X', b, t4)])
                    evn += 1
                xkeys = [('X', b, t4) for t4 in range(32)]
                for iq in range(4):
                    P.dma('act', Gd_v[:, iq * 32:(iq + 1) * 32, tt * 128:(tt + 1) * 128], X[b][:, iq * 32:(iq + 1) * 32, :],
                          reads=xkeys, writes=['Gd'])
            if stop_after == 'F3':
                return 'STOP'
            P.barrier()
            A.reset()
            TP = 512
            hfb = A.alloc([128, NCH, TP], BF16)
            acc = A.alloc([128, NCH, TP])
            ut = [A.alloc([128, NCH, 512], BF16) for i in range(2)]
            vt = [A.alloc([128, 16, 512], BF16) for i in range(2)]
            PT = A.alloc([128, 16, TP], BF16)
            gl = [A.alloc([128, TP], BF16) for i in range(4)]
            gt_ = [A.alloc([128, TP], BF16) for i in range(4)]
            xb = [A.alloc([128, TP]) for i in range(2)]
            rb = [A.alloc([128, TP]) for i in range(2)]
            uT_v = uT_d[l].rearrange("(k p) e -> p k e", p=128)
            v_v = v_d[l].rearrange("(g ic j) d -> g j ic d", ic=16, j=128)
            ui = 0
            vi = 0
            gi_ = 0
            for ps_ in range(S // TP):
                t0 = ps_ * TP
                for k in range(NCH):
                    P.dma('sp', hfb[:, k, :], hfT_v[:, k, t0:t0 + TP], reads=['hfT_d'], writes=[('hfb', k)])
                hbk = [('hfb', k) for k in range(NCH)]
                for eg in range(8):
                    for ib in range(4):
                        ub = ui % 2
                        ui += 1
                        e0 = eg * 2048 + ib * 512
                        P.dma('pool', ut[ub], uT_v[:, :, e0:e0 + 512], writes=['ut%d' % ub])
                        for i4 in range(4):
                            ic = ib * 4 + i4
                            i_abs = eg * 16 + ic
                            gb = gi_ % 4
                            gi_ += 1
                            pb = gb % 4
                            P.dma('act', gt_[gb], Gd[i_abs, :, t0:t0 + TP], reads=['Gd'], writes=['gt%d' % gb])

                            def ms(e, ub=ub, i4=i4, pb=pb):
                                ins = None
                                for k in range(NCH):
                                    ins = e.matmul(psb[pb][:], ut[ub][:, k, i4 * 128:(i4 + 1) * 128], hfb[:, k, :],
                                                   start=(k == 0), stop=(k == NCH - 1))
                                return ins
                            P.op('pe', ms, reads=['ut%d' % ub] + hbk, writes=[PS(pb)])
                            P.op('act', lambda e, gb=gb, pb=pb: e.activation(gl[gb], psb[pb][:], AF.Gelu_apprx_tanh),
                                 reads=[PS(pb)], writes=['gl%d' % gb])
                            P.op('dve', lambda e, gb=gb, ic=ic: e.tensor_tensor(PT[:, ic, :], gl[gb], gt_[gb], ALU.mult),
                                 reads=['gl%d' % gb, 'gt%d' % gb], writes=[('PT', ic)])
                    ptk = [('PT', ic) for ic in range(16)]
                    for dq in range(4):
                        vb_ = vi % 2
                        vi += 1
                        P.dma('pool', vt[vb_], v_v[eg][:, :, dq * 512:(dq + 1) * 512], writes=['vt%d' % vb_])
                        for d4 in range(4):
                            dch = dq * 4 + d4
                            pb = 4 + (dch % 4)

                            def mv(e, vb_=vb_, d4=d4, pb=pb):
                                ins = None
                                for ic in range(16):
                                    ins = e.matmul(psb[pb][:], vt[vb_][:, ic, d4 * 128:(d4 + 1) * 128], PT[:, ic, :],
                                                   start=(ic == 0), stop=(ic == 15))
                                return ins
                            P.op('pe', mv, reads=['vt%d' % vb_] + ptk, writes=[PS(pb)])
                            if eg == 0:
                                P.op('act', lambda e, dch=dch, pb=pb: e.copy(acc[:, dch, :], psb[pb][:]), reads=[PS(pb)], writes=[('acc', dch)])
                            else:
                                P.op('dve', lambda e, dch=dch, pb=pb: e.tensor_tensor(acc[:, dch, :], acc[:, dch, :], psb[pb][:], ALU.add),
                                     reads=[PS(pb), ('acc', dch)], writes=[('acc', dch)])
                for dch in range(NCH):
                    xi = dch % 2
                    P.dma('sp', xb[xi], xT_d[dch * 128:(dch + 1) * 128, t0:t0 + TP], reads=['xT_d'], writes=['xb%d' % xi])
                    P.op('act', lambda e, xi=xi: e.activation(xb[xi], xb[xi], AF.Copy, scale=ALPHA), reads=['xb%d' % xi], writes=['xb%d' % xi])
                    gt2 = modT[:, mo + 80 + dch:mo + 80 + dch + 1]
                    P.op('dve', lambda e, xi=xi, dch=dch, gt2=gt2: e.scalar_tensor_tensor(rb[xi], acc[:, dch, :], gt2, xb[xi], ALU.mult, ALU.add),
                         reads=[('acc', dch), 'xb%d' % xi, 'modT'], writes=['rb%d' % xi])
                    P.dma('sp', rT_d[dch * 128:(dch + 1) * 128, t0:t0 + TP], rb[xi], reads=['rb%d' % xi], writes=['rT_d'])
            if stop_after == 'F4':
                return 'STOP'
            P.barrier()
            A.reset()
            L = ln_alloc()
            x2t = [A.alloc([128, NCH, TB]) for i in range(2)]
            rT_v = rT_d.rearrange("(j p) t -> p j t", p=128)
            lo = l * 64 + 32
            for tb in range(S // TB):
                b = tb % 2
                P.dma('sp', L.x[b], rT_v[:, :, tb * TB:(tb + 1) * TB], reads=['rT_d'], writes=['ln_x%d' % b])
                ln_block(L, L.x[b], 'ln_x%d' % b,
                         lambda j: lnp[:, lo + j:lo + j + 1],
                         lambda j: lnp[:, lo + 16 + j:lo + 16 + j + 1],
                         lambda j, b=b: x2t[b][:, j, :],
                         lambda j, b=b: 'x2t%d' % b)
                P.dma('sp', xT_v[:, :, tb * TB:(tb + 1) * TB], x2t[b], reads=['x2t%d' % b], writes=['xT_d'])
            if stop_after == 'G':
                return 'STOP'

            return None
        if phase_4() == 'STOP':
            return finish_debug(C)
    def phase_z():
        P.barrier()
        A.reset()
        xf = [A.alloc([128, NCH, 128]) for i in range(2)]
        xo = [A.alloc([128, D]) for i in range(2)]
        ev = 0
        for tt in range(S // 128):
            b = tt % 2
            P.dma('sp', xf[b], xT_v[:, :, tt * 128:(tt + 1) * 128], reads=['xT_d'], writes=['xf%d' % b])
            for g in range(4):
                pb = g % 2

                def tr(e, b=b, g=g, pb=pb):
                    ins = None
                    for jj in range(4):
                        j = g * 4 + jj
                        ins = e.transpose(psb[pb][:, jj * 128:(jj + 1) * 128], xf[b][:, j, :], ident[:])
                    return ins
                P.op('pe', tr, reads=['xf%d' % b, 'ident'], writes=[PS(pb)])
                dst = xo[b][:, g * 512:(g + 1) * 512]
                if ev % 2 == 0:
                    P.op('act', lambda e, dst=dst, pb=pb: e.copy(dst, psb[pb][:]), reads=[PS(pb)], writes=[('xo', b, g)])
                else:
                    P.op('dve', lambda e, dst=dst, pb=pb: e.tensor_copy(dst, psb[pb][:]), reads=[PS(pb)], writes=[('xo', b, g)])
                ev += 1
            P.dma('sp', out_d[tt * 128:(tt + 1) * 128, :], xo[b], reads=[('xo', b, g) for g in range(4)], writes=['out'])
    phase_z()
    return finish_debug(C)


def finish_debug(C):
    C.P.wait_all('sp', C.dram_written)
    C.P.finish()
    return C.nc


def make_in_maps(inputs, n_cores=8):
    f32 = np.float32
    maps = []
    ident = np.eye(128, dtype=f32)
    b_ada_col = np.ascontiguousarray(
        np.asarray(inputs['b_ada'], f32).reshape(DEPTH, 96, 128).transpose(2, 0, 1).reshape(128, DEPTH * 96))
    lam_re = np.asarray(inputs['ssm_lam_re'], f32)
    lam_im = np.asarray(inputs['ssm_lam_im'], f32)
    lstep = np.asarray(inputs['ssm_log_step'], f32)
    s5A = np.zeros((DEPTH, 128, 3, 64), f32)
    for l in range(DEPTH):
        for h in range(2):
            s5A[l, h * 64:(h + 1) * 64, 0, :] = lam_re[l].T
            s5A[l, h * 64:(h + 1) * 64, 1, :] = lam_im[l].T
        s5A[l, :, 2, :] = lstep[l][None, :]
    s5A = s5A.reshape(DEPTH, 128, 192)
    b_re = np.asarray(inputs['ssm_b_re'], f32)
    b_im = np.asarray(inputs['ssm_b_im'], f32)
    braw = np.zeros((DEPTH, 64, 128, 128), f32)
    for g in range(64):
        r0 = 16 * (g % 8)
        braw[:, g, r0:r0 + 16, 0:64] = b_re[:, g].transpose(0, 2, 1)
        braw[:, g, r0:r0 + 16, 64:128] = b_im[:, g].transpose(0, 2, 1)
    c_re = np.asarray(inputs['ssm_c_re'], f32)
    c_im = np.asarray(inputs['ssm_c_im'], f32)
    crci = np.zeros((DEPTH, 128, 2, 64, 16), f32)
    for h in range(2):
        crci[:, h * 64:(h + 1) * 64, 0] = c_re.transpose(0, 3, 1, 2)
        crci[:, h * 64:(h + 1) * 64, 1] = c_im.transpose(0, 3, 1, 2)
    crci = crci.reshape(DEPTH, 128, 2048)
    dskip_col = np.ascontiguousarray(
        np.asarray(inputs['ssm_d'], f32).reshape(DEPTH, 8, 128).transpose(2, 0, 1).reshape(128, DEPTH * 8))
    sk = np.asarray(inputs['peer_sub_keys'], f32)
    keysT = np.ascontiguousarray(sk.reshape(DEPTH, 16, 128, 128).transpose(0, 3, 1, 2).reshape(DEPTH, 128, 2048))
    peer_uT = np.ascontiguousarray(np.asarray(inputs['peer_u'], f32).transpose(0, 2, 1))
    iota128 = np.tile(np.arange(128, dtype=f32)[None, :], (128, 1))
    lnp_col = np.zeros((128, DEPTH, 4, 16), f32)
    for i_, nm in enumerate(['ln1_g', 'ln1_b', 'ln2_g', 'ln2_b']):
        lnp_col[:, :, i_, :] = np.asarray(inputs[nm], f32).reshape(DEPTH, 16, 128).transpose(2, 0, 1)
    lnp_col = lnp_col.reshape(128, DEPTH * 64)
    cwv = np.asarray(inputs['dn_conv_w'], f32)
    convw_col = np.ascontiguousarray(cwv.reshape(DEPTH, 4, 24, 128).transpose(0, 3, 2, 1).reshape(DEPTH, 128, 96))
    dnhp = np.zeros((8, DEPTH * 2), f32)
    dnhp[:, 0::2] = np.asarray(inputs['dn_a_log'], f32).T
    dnhp[:, 1::2] = np.asarray(inputs['dn_dt_bias'], f32).T
    normw_col = np.ascontiguousarray(np.asarray(inputs['dn_norm_w'], f32).T)
    ii = np.arange(64)
    NEG = -30000.0
    maskL = np.where(ii[None, :] < ii[:, None], 0.0, NEG).astype(f32)
    maskAT = np.where(ii[:, None] <= ii[None, :], 0.0, NEG).astype(f32)
    selrow = np.zeros((8, 8, 64), f32)
    for h in range(8):
        selrow[h, h, :] = 1.0
    selrow = selrow.reshape(8, 512)
    sel63 = np.zeros((64, 128), f32)
    sel63[63, :] = 1.0
    cmask = np.ones((8, S), f32)
    cmask[:, 0::64] = 0.0
    for c in range(n_cores):
        b = c % 4
        m = {
            'x': np.ascontiguousarray(np.asarray(inputs['x'][b], f32)),
            'c_col': np.ascontiguousarray(np.asarray(inputs['c'][b], f32).reshape(NCH, 128).T),
            'w_ada': np.asarray(inputs['w_ada'], f32),
            'b_ada_col': b_ada_col,
            'w_in': np.asarray(inputs['w_in'], f32),
            'ident': ident,
            's5A': s5A, 'braw': braw, 'crci': crci, 'dskip_col': dskip_col,
            'ssm_w_glu': np.asarray(inputs['ssm_w_glu'], f32),
            'convw_col': convw_col, 'dnhp': dnhp, 'normw_col': normw_col, 'maskL': maskL, 'maskAT': maskAT,
            'selrow': selrow, 'sel63': sel63, 'cmask': cmask,
            'w_out': np.asarray(inputs['w_out'], f32), 'lnp_col': lnp_col,
            'peer_w_query': np.asarray(inputs['peer_w_query'], f32), 'keysT': keysT,
            'peer_uT': peer_uT, 'peer_v': np.asarray(inputs['peer_v'], f32), 'iota128': iota128,
        }
        maps.append(m)
    return maps


def kernel(**inputs):
    nc = build()
    in_maps = make_in_maps(inputs)
    res = run_bass_kernel_spmd(nc, in_maps, core_ids=list(range(8)))
    out = np.stack([np.asarray(res.results[b]['out']) for b in range(4)], axis=0)
    return out.astype(np.float32)
```

```python
import math
import os as _os
import numpy as np
from contextlib import ExitStack
import concourse.bass as bass
import concourse.mybir as mybir
from concourse.bass_utils import run_bass_kernel_spmd

F32 = mybir.dt.float32
BF16 = mybir.dt.bfloat16
AF = mybir.ActivationFunctionType
ALU = mybir.AluOpType

D = 2048
S = 2048
DEPTH = 4
NCH = 16
N_IN = 5136
ALPHA = (2.0 * DEPTH) ** 0.25
LN_EPS = 1e-5

ENGS = ['pe', 'act', 'dve', 'pool', 'sp']
NDS = 8


class Prog:
    def __init__(self, nc, es):
        self.nc = nc
        self.es = es
        self.ops = {e: [] for e in ENGS}
        self.sem = {e: es.enter_context(nc.semaphore('s_' + e)) for e in ENGS}
        self.cnt = {e: 0 for e in ENGS}
        self.seen = {e: {} for e in ENGS}
        self.res = {}
        self.dsem = {e: [es.enter_context(nc.semaphore('d_%s%d' % (e, i))) for i in range(NDS)]
                     for e in ('sp', 'pool', 'act')}
        self.dcnt = {e: [0] * NDS for e in ('sp', 'pool', 'act')}
        self.dnext = {e: 0 for e in ('sp', 'pool', 'act')}
        self.n_ins = 0
        self.rec = None

    def replay(self, item):
        rec, self.rec = self.rec, None
        try:
            if item[0] == 'op':
                self.op(*item[1:])
            else:
                self.dma(item[1], item[2], item[3], item[4], item[5], **item[6])
        finally:
            self.rec = rec

    def replay_merged(self, a, b, ratio=4):
        ia = ib = 0
        while ia < len(a) or ib < len(b):
            for _ in range(ratio):
                if ia < len(a):
                    self.replay(a[ia])
                    ia += 1
            if ib < len(b):
                self.replay(b[ib])
                ib += 1

    def _deps(self, eng, reads, writes):
        toks = []
        for r in reads:
            st = self.res.get(r)
            if st is not None and st['w'] is not None:
                toks.append(st['w'])
        for w in writes:
            st = self.res.get(w)
            if st is not None:
                if st['w'] is not None:
                    toks.append(st['w'])
                toks.extend(st['r'].values())
        waits = []
        for (key, s, v, e) in toks:
            if e == 'pe' and eng == 'pe':
                continue
            if self.seen[eng].get(key, 0) >= v:
                continue
            self.seen[eng][key] = v
            waits.append((s, v))
        return waits

    def _update(self, tok, reads, writes):
        for r in reads:
            st = self.res.setdefault(r, {'w': None, 'r': {}})
            st['r'][tok[0]] = tok
        for w in writes:
            self.res[w] = {'w': tok, 'r': {}}

    def op(self, eng, fn, reads=(), writes=()):
        if self.rec is not None:
            self.rec.append(('op', eng, fn, list(reads), list(writes)))
            return None
        waits = self._deps(eng, reads, writes)
        self.cnt[eng] += 1
        mysem = self.sem[eng]
        tok = ('e_' + eng, mysem, self.cnt[eng], eng)

        def run(eo, waits=waits, fn=fn, mysem=mysem):
            for (s, v) in waits:
                eo.wait_ge(s, v)
            ins = fn(eo)
            ins.then_inc(mysem, 1)
        self.ops[eng].append(run)
        self._update(tok, reads, writes)
        self.n_ins += 1
        return tok

    def dma(self, q, out, in_, reads=(), writes=(), **kw):
        if self.rec is not None:
            self.rec.append(('dma', q, out, in_, list(reads), list(writes), kw))
            return None
        waits = self._deps(q, reads, writes)
        j = self.dnext[q]
        self.dnext[q] = (j + 1) % NDS
        s = self.dsem[q][j]
        prev = self.dcnt[q][j]
        key = 'd_%s%d' % (q, j)
        if prev > 0 and self.seen[q].get(key, 0) < prev:
            waits.append((s, prev))
            self.seen[q][key] = prev
        self.dcnt[q][j] = prev + 16
        tok = (key, s, prev + 16, None)

        def run(eo, waits=waits, s=s, out=out, in_=in_, kw=kw):
            for (ws, v) in waits:
                eo.wait_ge(ws, v)
            eo.dma_start(out=out, in_=in_, **kw).then_inc(s, 16)
        self.ops[q].append(run)
        self._update(tok, reads, writes)
        self.n_ins += 1
        return tok

    def barrier(self):
        allw = []
        for x in ENGS:
            if self.cnt[x] > 0:
                allw.append(('e_' + x, self.sem[x], self.cnt[x], x))
        for q in self.dsem:
            for j in range(NDS):
                if self.dcnt[q][j] > 0:
                    allw.append(('d_%s%d' % (q, j), self.dsem[q][j], self.dcnt[q][j], None))
        for eng in ENGS:
            waits = []
            for (key, s, v, x) in allw:
                if x == eng and eng in ('pe', 'sp'):
                    continue
                if self.seen[eng].get(key, 0) >= v:
                    continue
                self.seen[eng][key] = v
                waits.append((s, v))

            def run(eo, waits=waits):
                for (s, v) in waits:
                    eo.wait_ge(s, v)
            self.ops[eng].append(run)
        self.res = {}

    def wait_all(self, eng, keys):
        waits = self._deps(eng, keys, ())

        def run(eo, waits=waits):
            for (s, v) in waits:
                eo.wait_ge(s, v)
        self.ops[eng].append(run)

    def finish(self):
        nc = self.nc
        with nc.Block() as block:
            @block.tensor
            def _(e):
                for f in self.ops['pe']:
                    f(e)

            @block.scalar
            def _(e):
                for f in self.ops['act']:
                    f(e)

            @block.vector
            def _(e):
                for f in self.ops['dve']:
                    f(e)

            @block.gpsimd
            def _(e):
                for f in self.ops['pool']:
                    f(e)

            @block.sync
            def _(e):
                for f in self.ops['sp']:
                    f(e)


class Ctx:
    pass


class Arena:
    def __init__(self, nc, es, words):
        self.t = es.enter_context(nc.sbuf_tensor('arena', [128, words], F32))
        self.tb = self.t.bitcast(BF16)
        self.words = words
        self.off = 0

    def reset(self):
        self.off = 0

    def mark(self):
        return self.off

    def release(self, m):
        self.off = m

    def alloc(self, shape, dt=F32):
        n = 1
        for s_ in shape[1:]:
            n *= s_
        esz = 4 if dt == F32 else 2
        nbytes = (n * esz + 63) // 64 * 64
        o = self.off
        self.off += nbytes
        assert self.off <= self.words * 4, "arena overflow %d" % self.off
        base = self.t if dt == F32 else self.tb
        v = base[:shape[0], o // esz:o // esz + n]
        if len(shape) == 3:
            v = v.rearrange("p (a b) -> p a b", a=shape[1])
        elif len(shape) == 4:
            v = v.rearrange("p (a b c) -> p a b c", a=shape[1], b=shape[2])
        return v


GELU_MODE = ['native']


def gelu_tanh(C, dst, src, src_key, dst_key):
    P = C.P
    if GELU_MODE[0] == 'native':
        P.op('act', lambda e: e.activation(dst, src, AF.Gelu_apprx_tanh), reads=[src_key], writes=[dst_key])
        return
    t = C.gelu_tmp[C.gelu_i % 2]
    tk = 'gelu_tmp%d' % (C.gelu_i % 2)
    C.gelu_i += 1
    P.op('dve', lambda e: e.tensor_tensor(t, src, src, ALU.mult), reads=[src_key], writes=[tk])
    P.op('dve', lambda e: e.tensor_scalar(t, t, 0.044715, 1.0, ALU.mult, ALU.add), reads=[tk], writes=[tk])
    P.op('dve', lambda e: e.tensor_tensor(t, t, src, ALU.mult), reads=[tk, src_key], writes=[tk])
    P.op('act', lambda e: e.activation(t, t, AF.Sigmoid, scale=1.5957691216057308), reads=[tk], writes=[tk])
    P.op('dve', lambda e: e.tensor_tensor(dst, t, src, ALU.mult), reads=[tk, src_key], writes=[dst_key])


def build(n_layers=DEPTH, stop_after=None, debug=False):
    nc = bass.Bass("TRN2", target_bir_lowering=False)
    es = ExitStack()
    P = Prog(nc, es)
    C = Ctx()
    C.nc, C.P, C.es = nc, P, es
    C.debug = debug
    C.dram_written = ['out']

    def din(name, shape, dt=F32):
        return nc.dram_tensor(name, list(shape), dt, kind="ExternalInput").ap()

    def dscr(name, shape, dt=F32):
        kind = "ExternalOutput" if (debug and name in debug) else "Internal"
        C.dram_written.append(name)
        return nc.dram_tensor(name, list(shape), dt, kind=kind).ap()

    def sb(name, shape, dt=F32):
        return es.enter_context(nc.sbuf_tensor(name, list(shape), dt))

    x_in = din("x", [S, D])
    c_col = din("c_col", [128, NCH])
    w_ada = din("w_ada", [DEPTH, D, 6 * D])
    b_ada = din("b_ada_col", [128, DEPTH * 96])
    w_in = din("w_in", [DEPTH, D, N_IN])
    ident_d = din("ident", [128, 128])
    s5A_d = din("s5A", [DEPTH, 128, 3 * 64])
    braw_d = din("braw", [DEPTH, 64, 128, 128])
    crci_d = din("crci", [DEPTH, 128, 2 * 1024])
    dskip_d = din("dskip_col", [128, DEPTH * 8])
    w_glu = din("ssm_w_glu", [DEPTH, 1024, 2048])
    convw_d = din("convw_col", [DEPTH, 128, 96])
    w_out = din("w_out", [DEPTH, D, D])
    wq_d = din("peer_w_query", [DEPTH, D, D])
    keysT_d = din("keysT", [DEPTH, 128, 16 * 128])
    uT_d = din("peer_uT", [DEPTH, D, 16384])
    v_d = din("peer_v", [DEPTH, 16384, D])
    iota_d = din("iota128", [128, 128])
    lnp_d = din("lnp_col", [128, DEPTH * 64])
    dnhp_d = din("dnhp", [8, DEPTH * 2])
    normw_d = din("normw_col", [128, DEPTH])
    maskL_d = din("maskL", [64, 64])
    maskAT_d = din("maskAT", [64, 64])
    selrow_d = din("selrow", [8, 8 * 64])
    sel63_d = din("sel63", [64, 128])
    cmask_d = din("cmask", [8, S])
    out_d = nc.dram_tensor("out", [S, D], F32, kind="ExternalOutput").ap()

    xT_d = dscr("xT_d", [D, S])
    projT_d = dscr("projT_d", [N_IN, S])
    ycatT_d = dscr("ycatT_d", [D, S], BF16)
    gyT_d = dscr("gyT_d", [1024, S], BF16)
    rT_d = dscr("rT_d", [D, S])
    hfT_d = dscr("hfT_d", [D, S], BF16)
    qT_d = dscr("qT_d", [D, S])
    Gd = dscr("Gd", [128, 128, S], BF16)

    ident = sb("ident_sb", [128, 128])
    ones_m = sb("ones_m", [128, 128])
    modT = sb("modT", [128, DEPTH * 96])
    cact = sb("cact", [128, NCH])
    eps_c = sb("eps_c", [128, 1])
    lnp = sb("lnp", [128, DEPTH * 64])
    psb = [es.enter_context(nc.psum_tensor("psb%d" % i, [128, 512], F32)) for i in range(8)]
    PS = lambda i: 'ps%d' % i
    A = Arena(nc, es, 51 * 1024)
    C.ident, C.ones_m, C.modT, C.psb, C.A = ident, ones_m, modT, psb, A

    P.dma('sp', ident[:], ident_d, writes=['ident'])
    P.dma('sp', lnp[:], lnp_d, writes=['lnp'])
    P.op('dve', lambda e: e.memset(ones_m[:], 1.0 / D), writes=['ones_m'])
    P.op('dve', lambda e: e.memset(eps_c[:], LN_EPS), writes=['eps_c'])

    xtm = [A.alloc([128, D]) for i in range(2)]
    xst = [A.alloc([128, NCH, 128]) for i in range(2)]
    ev = 0
    for tt in range(S // 128):
        b = tt % 2
        P.dma('sp', xtm[b], x_in[tt * 128:(tt + 1) * 128, :], writes=['xtm%d' % b])
        for g in range(4):
            pb = g % 2

            def tr(e, b=b, g=g, pb=pb):
                ins = None
                for jj in range(4):
                    j = g * 4 + jj
                    ins = e.transpose(psb[pb][:, jj * 128:(jj + 1) * 128],
                                      xtm[b][:, j * 128:(j + 1) * 128], ident[:])
                return ins
            P.op('pe', tr, reads=['xtm%d' % b, 'ident'], writes=[PS(pb)])
            dst = xst[b][:, g * 4:(g + 1) * 4, :]
            src = psb[pb][:].rearrange("p (j t) -> p j t", j=4)
            if ev % 2 == 0:
                P.op('act', lambda e, dst=dst, src=src: e.copy(dst, src),
                     reads=[PS(pb)], writes=[('xst', b, g)])
            else:
                P.op('dve', lambda e, dst=dst, src=src: e.tensor_copy(dst, src),
                     reads=[PS(pb)], writes=[('xst', b, g)])
            ev += 1
        P.dma('sp', xT_d.rearrange("(j p) t -> p j t", p=128)[:, :, tt * 128:(tt + 1) * 128],
              xst[b], reads=[('xst', b, g) for g in range(4)], writes=['xT_d'])

    P.barrier()
    A.reset()
    P.dma('sp', cact[:], c_col, writes=['cact'])
    P.op('act', lambda e: e.activation(cact[:], cact[:], AF.Silu), reads=['cact'], writes=['cact'])
    bada = A.alloc([128, DEPTH * 96])
    P.dma('sp', bada, b_ada, writes=['bada'])
    WA_COLS = 3072
    wa = [A.alloc([128, WA_COLS]) for i in range(3)]
    wi = 0
    for l in range(n_layers):
        for cb in range(6 * D // WA_COLS):
            for k in range(NCH):
                b = wi % 3
                wi += 1
                P.dma('sp' if (wi % 2) else 'act', wa[b],
                      w_ada[l, k * 128:(k + 1) * 128, cb * WA_COLS:(cb + 1) * WA_COLS],
                      writes=['wa%d' % b])

                def mm(e, b=b, k=k, cb=cb):
                    ins = None
                    for n in range(WA_COLS // 128):
                        col = cb * (WA_COLS // 128) + n
                        ins = e.matmul(psb[7][:, col:col + 1], wa[b][:, n * 128:(n + 1) * 128],
                                       cact[:, k:k + 1], start=(k == 0 and n == 0 and cb == 0),
                                       stop=(k == NCH - 1), skip_group_check=True)
                    return ins
                P.op('pe', mm, reads=['wa%d' % b, 'cact'], writes=[PS(7)])
        P.op('dve', lambda e, l=l: e.tensor_tensor(modT[:, l * 96:(l + 1) * 96], psb[7][:, 0:96],
                                                  bada[:, l * 96:(l + 1) * 96], ALU.add),
             reads=[PS(7), 'bada'], writes=['modT'])
    if stop_after == 'A':
        return finish_debug(C)

    for l in range(n_layers):
        for c0 in (16, 64):
            sl = modT[:, l * 96 + c0:l * 96 + c0 + 16]
            P.op('dve', lambda e, sl=sl: e.tensor_scalar(sl, sl, 1.0, None, ALU.add),
                 reads=['modT'], writes=['modT'])

    TB = 256
    C.TB = TB

    def ln_alloc():
        L = Ctx()
        L.x = [A.alloc([128, NCH, TB]) for i in range(2)]
        L.sq = A.alloc([128, NCH, TB])
        L.m = A.alloc([128, TB])
        L.v = A.alloc([128, TB])
        L.r = A.alloc([128, TB])
        L.nmr = A.alloc([128, TB])
        L.t = [A.alloc([128, TB]) for i in range(2)]
        return L

    def ln_block(L, xt, xt_key, a_of, b_of, out_of, out_key_of):
        mean_ps = psb[6][:, 0:TB]
        msq_ps = psb[7][:, 0:TB]

        def mm1(e):
            ins = None
            for j in range(NCH):
                ins = e.matmul(mean_ps, ones_m[:], xt[:, j, :], start=(j == 0), stop=(j == NCH - 1))
            return ins
        P.op('pe', mm1, reads=[xt_key, 'ones_m'], writes=[PS(6)])
        P.op('act', lambda e: e.activation(L.sq, xt, AF.Square), reads=[xt_key], writes=['ln_sq'])

        def mm2(e):
            ins = None
            for j in range(NCH):
                ins = e.matmul(msq_ps, ones_m[:], L.sq[:, j, :], start=(j == 0), stop=(j == NCH - 1))
            return ins
        P.op('pe', mm2, reads=['ln_sq', 'ones_m'], writes=[PS(7)])
        P.op('act', lambda e: e.copy(L.m, mean_ps), reads=[PS(6)], writes=['ln_m'])
        P.op('dve', lambda e: e.scalar_tensor_tensor(L.v, L.m, -1.0, L.m, ALU.mult, ALU.mult),
             reads=['ln_m'], writes=['ln_v'])
        P.op('dve', lambda e: e.tensor_tensor(L.v, L.v, msq_ps, ALU.add),
             reads=['ln_v', PS(7)], writes=['ln_v'])
        P.op('act', lambda e: e.activation(L.v, L.v, AF.Sqrt, bias=eps_c[:], scale=1.0),
             reads=['ln_v', 'eps_c'], writes=['ln_v'])
        P.op('dve', lambda e: e.reciprocal(L.r, L.v), reads=['ln_v'], writes=['ln_r'])
        P.op('dve', lambda e: e.scalar_tensor_tensor(L.nmr, L.m, -1.0, L.r, ALU.mult, ALU.mult),
             reads=['ln_m', 'ln_r'], writes=['ln_nmr'])
        for j in range(NCH):
            tb = j % 2
            t = L.t[tb]
            a = a_of(j)
            P.op('dve', lambda e, t=t, j=j, a=a: e.scalar_tensor_tensor(t, xt[:, j, :], a, L.r,
                                                                      ALU.mult, ALU.mult),
                 reads=[xt_key, 'ln_r', 'modT'], writes=['ln_t%d' % tb])
            P.op('dve', lambda e, t=t, a=a: e.scalar_tensor_tensor(t, L.nmr, a, t, ALU.mult, ALU.add),
                 reads=['ln_nmr', 'ln_t%d' % tb, 'modT'], writes=['ln_t%d' % tb])
            o = out_of(j)
            bcol = b_of(j)
            P.op('act', lambda e, o=o, t=t, bcol=bcol: e.activation(o, t, AF.Identity, bias=bcol, scale=1.0),
                 reads=['ln_t%d' % tb, 'modT'], writes=[out_key_of(j)])

    xT_v = xT_d.rearrange("(j p) t -> p j t", p=128)

    for l in range(n_layers):
        mo = l * 96
        def phase_0(l=l, mo=mo):
            P.barrier()
            A.reset()
            hT = A.alloc([128, NCH, S], BF16)
            L = ln_alloc()
            for tb in range(S // TB):
                b = tb % 2
                P.dma('sp', L.x[b], xT_v[:, :, tb * TB:(tb + 1) * TB], reads=['xT_d'], writes=['ln_x%d' % b])
                ln_block(L, L.x[b], 'ln_x%d' % b,
                         lambda j: modT[:, mo + 16 + j:mo + 16 + j + 1],
                         lambda j: modT[:, mo + j:mo + j + 1],
                         lambda j, tb=tb: hT[:, j, tb * TB:(tb + 1) * TB],
                         lambda j, tb=tb: ('hT', tb))
            wt = [A.alloc([128, NCH, 512], BF16) for i in range(2)]
            stg = [A.alloc([128, 512]) for i in range(4)]
            w_in_v = w_in[l].rearrange("(k p) n -> p k n", p=128)
            hkeys = [('hT', tb) for tb in range(S // TB)]
            cnt = 0
            for ng in range((N_IN + 511) // 512):
                n0 = ng * 512
                nw = min(512, N_IN - n0)
                b = ng % 2
                P.dma('pool', wt[b][:, :, :nw], w_in_v[:, :, n0:n0 + nw], writes=['wt%d' % b])
                for nc_ in range((nw + 127) // 128):
                    m = min(128, nw - nc_ * 128)
                    for t4 in range(S // 512):
                        pb = cnt % 4
                        cnt += 1

                        def mm(e, b=b, nc_=nc_, m=m, t4=t4, pb=pb):
                            ins = None
                            for k in range(NCH):
                                ins = e.matmul(psb[pb][:m, :], wt[b][:, k, nc_ * 128:nc_ * 128 + m],
                                               hT[:, k, t4 * 512:(t4 + 1) * 512],
                                               start=(k == 0), stop=(k == NCH - 1))
                            return ins
                        P.op('pe', mm, reads=['wt%d' % b] + hkeys, writes=[PS(pb)])
                        if pb % 2 == 0:
                            P.op('act', lambda e, pb=pb, m=m: e.copy(stg[pb][:m, :], psb[pb][:m, :]),
                                 reads=[PS(pb)], writes=['stg%d' % pb])
                        else:
                            P.op('dve', lambda e, pb=pb, m=m: e.tensor_copy(stg[pb][:m, :], psb[pb][:m, :]),
                                 reads=[PS(pb)], writes=['stg%d' % pb])
                        r0 = n0 + nc_ * 128
                        P.dma('sp', projT_d[r0:r0 + m, t4 * 512:(t4 + 1) * 512], stg[pb][:m, :],
                              reads=['stg%d' % pb], writes=['projT_d'])
            if stop_after == 'B':
                return 'STOP'

            return None
        if phase_0() == 'STOP':
            return finish_debug(C)
        def phase_1(l=l, mo=mo):
            P.barrier()
            A.reset()
            NL = list(range(8)) + [8 * k for k in range(1, 8)] + [64 * k for k in range(1, 8)] + [0, 512, 1024, 1536]
            NP_ = len(NL)
            TWO_PI = 2.0 * math.pi
            MAGIC = 12582912.0
            COL1 = A.alloc([128, 22, 64])
            COL2 = A.alloc([128, 22, 64])
            WcF = A.alloc([128, 4, 64, 16])
            PIp = A.alloc([128, 64])
            identb = A.alloc([128, 128], BF16)
            dsk = A.alloc([128, 8])
            s5_mark = A.mark()
            s5a = A.alloc([128, 3, 64])
            P.dma('sp', s5a, s5A_d[l].rearrange("p (a g) -> p a g", a=3), writes=['s5a'])
            crci = A.alloc([128, 2, 64, 16])
            P.dma('sp', crci, crci_d[l].rearrange("p (a g c) -> p a g c", a=2, g=64), writes=['crci'])
            P.dma('sp', dsk, dskip_d[:, l * 8:(l + 1) * 8], writes=['dsk'])
            NV = A.alloc([128, NP_, 64])
            for i, n in enumerate(NL):
                P.op('pool', lambda e, i=i, n=n: e.memset(NV[:, i, :], float(n)), writes=['NV'])
            stp = A.alloc([128, 64])
            thd = A.alloc([128, 2, 64])
            P.op('act', lambda e: e.activation(stp, s5a[:, 2, :], AF.Exp), reads=['s5a'], writes=['stp'])
            P.op('dve', lambda e: e.tensor_tensor(thd[:, 0, :], s5a[:, 1, :], stp, ALU.mult), reads=['s5a', 'stp'], writes=['thd'])
            P.op('dve', lambda e: e.tensor_tensor(thd[:, 1, :], s5a[:, 0, :], stp, ALU.mult), reads=['s5a', 'stp', 'thd'], writes=['thd'])
            ang = A.alloc([128, NP_, 64])
            tq = A.alloc([128, NP_, 64])
            sinv = A.alloc([128, NP_, 64])
            cosv = A.alloc([128, NP_, 64])
            mag = A.alloc([128, NP_, 64])
            bc = lambda ap2: ap2.unsqueeze(1).to_broadcast([128, NP_, 64])
            P.op('dve', lambda e: e.tensor_tensor(ang, NV, bc(thd[:, 0, :]), ALU.mult), reads=['NV', 'thd'], writes=['ang'])

            def sin_of(dst, key, shift):
                if shift != 0.0:
                    P.op('dve', lambda e: e.tensor_scalar(dst, ang, shift, None, ALU.add), reads=['ang'], writes=[key])
                    srcx = dst
                else:
                    srcx = ang
                P.op('dve', lambda e: e.tensor_scalar(tq, srcx, 1.0 / TWO_PI, MAGIC, ALU.mult, ALU.add),
                     reads=['ang', key], writes=['tq'])
                P.op('dve', lambda e: e.tensor_scalar(tq, tq, MAGIC, None, ALU.subtract), reads=['tq'], writes=['tq'])
                C1 = 6.28125
                C2 = TWO_PI - C1
                P.op('dve', lambda e: e.scalar_tensor_tensor(dst, tq, -C1, srcx, ALU.mult, ALU.add),
                     reads=['tq', 'ang', key], writes=[key])
                P.op('dve', lambda e: e.scalar_tensor_tensor(dst, tq, -C2, dst, ALU.mult, ALU.add),
                     reads=['tq', key], writes=[key])
                P.op('dve', lambda e: e.tensor_scalar(dst, dst, 3.14159, -3.14159, ALU.min, ALU.max), reads=[key], writes=[key])
                P.op('act', lambda e: e.activation(dst, dst, AF.Sin), reads=[key], writes=[key])
            sin_of(sinv, 'sinv', 0.0)
            sin_of(cosv, 'cosv', 0.5 * math.pi)
            P.op('dve', lambda e: e.tensor_tensor(mag, NV, bc(thd[:, 1, :]), ALU.mult), reads=['NV', 'thd'], writes=['mag'])
            P.op('act', lambda e: e.activation(mag, mag, AF.Exp), reads=['mag'], writes=['mag'])
            P.op('dve', lambda e: e.tensor_tensor(cosv, cosv, mag, ALU.mult), reads=['cosv', 'mag'], writes=['cosv'])
            P.op('dve', lambda e: e.tensor_tensor(sinv, sinv, mag, ALU.mult), reads=['sinv', 'mag'], writes=['sinv'])
            ar, ai = cosv, sinv
            cf = A.alloc([128, 6, 64])
            P.op('dve', lambda e: e.tensor_scalar(cf[:, 0, :], ar[:, 1, :], -1.0, None, ALU.add), reads=['cosv'], writes=['cf0'])
            P.op('dve', lambda e: e.tensor_tensor(cf[:, 1, :], s5a[:, 0, :], s5a[:, 0, :], ALU.mult), reads=['s5a'], writes=['cf1'])
            P.op('dve', lambda e: e.tensor_tensor(cf[:, 4, :], s5a[:, 1, :], s5a[:, 1, :], ALU.mult), reads=['s5a'], writes=['cf4'])
            P.op('dve', lambda e: e.tensor_tensor(cf[:, 1, :], cf[:, 1, :], cf[:, 4, :], ALU.add), reads=['cf1', 'cf4'], writes=['cf1'])
            P.op('dve', lambda e: e.reciprocal(cf[:, 5, :], cf[:, 1, :]), reads=['cf1'], writes=['cf5'])
            P.op('dve', lambda e: e.tensor_tensor(cf[:, 2, :], cf[:, 0, :], s5a[:, 0, :], ALU.mult), reads=['cf0', 's5a'], writes=['cf2'])
            P.op('dve', lambda e: e.tensor_tensor(cf[:, 4, :], ai[:, 1, :], s5a[:, 1, :], ALU.mult), reads=['sinv', 's5a', 'cf4'], writes=['cf4'])
            P.op('dve', lambda e: e.tensor_tensor(cf[:, 2, :], cf[:, 2, :], cf[:, 4, :], ALU.add), reads=['cf2', 'cf4'], writes=['cf2'])
            P.op('dve', lambda e: e.tensor_tensor(cf[:, 2, :], cf[:, 2, :], cf[:, 5, :], ALU.mult), reads=['cf2', 'cf5'], writes=['cf2'])
            P.op('dve', lambda e: e.tensor_tensor(cf[:, 3, :], ai[:, 1, :], s5a[:, 0, :], ALU.mult), reads=['sinv', 's5a'], writes=['cf3'])
            P.op('dve', lambda e: e.tensor_tensor(cf[:, 4, :], cf[:, 0, :], s5a[:, 1, :], ALU.mult), reads=['cf0', 's5a', 'cf2'], writes=['cf4'])
            P.op('dve', lambda e: e.tensor_tensor(cf[:, 3, :], cf[:, 3, :], cf[:, 4, :], ALU.subtract), reads=['cf3', 'cf4'], writes=['cf3'])
            P.op('dve', lambda e: e.tensor_tensor(cf[:, 3, :], cf[:, 3, :], cf[:, 5, :], ALU.mult), reads=['cf3', 'cf5'], writes=['cf3'])
            FR = A.alloc([128, 22, 64])
            FI = A.alloc([128, 22, 64])
            ftmp = A.alloc([128, 8, 64])
            bc8 = lambda ap2: ap2.unsqueeze(1).to_broadcast([128, 8, 64])
            P.op('dve', lambda e: e.tensor_tensor(FR[:, 0:8, :], ar[:, 0:8, :], bc8(cf[:, 2, :]), ALU.mult), reads=['cosv', 'cf2'], writes=['FR'])
            P.op('dve', lambda e: e.tensor_tensor(ftmp, ai[:, 0:8, :], bc8(cf[:, 3, :]), ALU.mult), reads=['sinv', 'cf3'], writes=['ftmp'])
            P.op('dve', lambda e: e.tensor_tensor(FR[:, 0:8, :], FR[:, 0:8, :], ftmp, ALU.subtract), reads=['FR', 'ftmp'], writes=['FR'])
            P.op('dve', lambda e: e.tensor_tensor(FI[:, 0:8, :], ar[:, 0:8, :], bc8(cf[:, 3, :]), ALU.mult), reads=['cosv', 'cf3'], writes=['FI'])
            P.op('dve', lambda e: e.tensor_tensor(ftmp, ai[:, 0:8, :], bc8(cf[:, 2, :]), ALU.mult), reads=['sinv', 'cf2', 'FR'], writes=['ftmp'])
            P.op('dve', lambda e: e.tensor_tensor(FI[:, 0:8, :], FI[:, 0:8, :], ftmp, ALU.add), reads=['FI', 'ftmp'], writes=['FI'])
            P.op('dve', lambda e: e.tensor_copy(FR[:, 8:22, :], ar[:, 8:22, :]), reads=['cosv', 'FR'], writes=['FR'])
            P.op('dve', lambda e: e.tensor_copy(FI[:, 8:22, :], ai[:, 8:22, :]), reads=['sinv', 'FI'], writes=['FI'])
            P.op('dve', lambda e: e.tensor_copy(COL1[0:64], FR[0:64]), reads=['FR'], writes=['COL1a'])
            P.op('dve', lambda e: e.tensor_scalar(COL1[64:128], FI[64:128], -1.0, None, ALU.mult), reads=['FI'], writes=['COL1b'])
            P.op('dve', lambda e: e.tensor_copy(COL2[0:64], FI[0:64]), reads=['FI'], writes=['COL2a'])
            P.op('dve', lambda e: e.tensor_copy(COL2[64:128], FR[64:128]), reads=['FR'], writes=['COL2b'])
            colkeys = ['COL1a', 'COL1b', 'COL2a', 'COL2b']
            X1 = A.alloc([128, 4, 64])
            X2 = A.alloc([128, 4, 64])
            P.op('dve', lambda e: e.tensor_copy(X1[0:64], ar[0:64, 22:26, :]), reads=['cosv'], writes=['X1a'])
            P.op('dve', lambda e: e.tensor_scalar(X1[64:128], ai[64:128, 22:26, :], -1.0, None, ALU.mult), reads=['sinv'], writes=['X1b'])
            P.op('dve', lambda e: e.tensor_scalar(X2[0:64], ai[0:64, 22:26, :], -1.0, None, ALU.mult), reads=['sinv'], writes=['X2a'])
            P.op('dve', lambda e: e.tensor_scalar(X2[64:128], ar[64:128, 22:26, :], -1.0, None, ALU.mult), reads=['cosv'], writes=['X2b'])
            wct = A.alloc([128, 64, 16])
            for k in range(4):
                x1b = X1[:, k, :].unsqueeze(2).to_broadcast([128, 64, 16])
                x2b = X2[:, k, :].unsqueeze(2).to_broadcast([128, 64, 16])
                P.op('dve', lambda e, k=k, x1b=x1b: e.tensor_tensor(WcF[:, k], crci[:, 0], x1b, ALU.mult),
                     reads=['crci', 'X1a', 'X1b'], writes=[('WcF', k)])
                P.op('dve', lambda e, k=k, x2b=x2b: e.tensor_tensor(wct, crci[:, 1], x2b, ALU.mult),
                     reads=['crci', 'X2a', 'X2b'], writes=['wct'])
                P.op('dve', lambda e, k=k: e.tensor_tensor(WcF[:, k], WcF[:, k], wct, ALU.add),
                     reads=[('WcF', k), 'wct'], writes=[('WcF', k)])
            P.op('dve', lambda e: e.tensor_copy(PIp[0:64], ident[0:64, 0:64]), reads=['ident'], writes=['PIa'])
            P.op('dve', lambda e: e.tensor_copy(PIp[64:128], ident[64:128, 64:128]), reads=['ident'], writes=['PIb'])
            P.op('dve', lambda e: e.tensor_copy(identb, ident[:]), reads=['ident'], writes=['identb'])

            if debug and 'dbgS5' in debug:
                for nm, t_, shp in (('dbg_col1', COL1, [128, 22 * 64]), ('dbg_col2', COL2, [128, 22 * 64]),
                                    ('dbg_wcf', WcF, [128, 4 * 64 * 16])):
                    dd = nc.dram_tensor(nm, shp, F32, kind="ExternalOutput").ap()
                    flat = t_.rearrange("p a b -> p (a b)") if len(t_.shape) == 3 else t_.rearrange("p a b c -> p (a b c)")
                    P.dma('sp', dd, flat, reads=['COL1a', 'COL1b', 'COL2a', 'COL2b'] + [('WcF', k) for k in range(4)], writes=[nm])
                    C.dram_written.append(nm)
            P.barrier()
            A.release(s5_mark)
            colkeys = []
            PAD0, PAD1, PAD2, PAD3 = 8, 56, 448, 1536
            Wd = A.alloc([128, 22, 8, 128], BF16)
            WcZ = A.alloc([128, 4, 8, 128], BF16)
            Bw = A.alloc([128, 8, 128], BF16)
            u32 = A.alloc([128, S])
            ubf = A.alloc([128, S], BF16)
            z0 = [A.alloc([128, PAD0 + S], BF16) for i in range(2)]
            w1 = [A.alloc([128, PAD1 + S], BF16) for i in range(2)]
            w2 = [A.alloc([128, PAD2 + S], BF16) for i in range(2)]
            w3 = [A.alloc([128, PAD3 + S], BF16) for i in range(8)]
            gyo = [A.alloc([128, 512], BF16) for i in range(2)]
            ytmp = [A.alloc([128, 512]) for i in range(2)]
            C.gelu_tmp = [A.alloc([128, 512]) for i in range(2)]
            C.gelu_i = 0
            P.op('pool', lambda e: e.memset(WcZ, 0.0), writes=['WcZ'])
            for i in range(2):
                P.op('pool', lambda e, i=i: e.memset(z0[i][:, 0:PAD0], 0.0), writes=['z0_%d' % i])
                P.op('pool', lambda e, i=i: e.memset(w1[i][:, 0:PAD1], 0.0), writes=['w1_%d' % i])
                P.op('pool', lambda e, i=i: e.memset(w2[i][:, 0:PAD2], 0.0), writes=['w2_%d' % i])
            for i in range(8):
                P.op('pool', lambda e, i=i: e.memset(w3[i][:, 0:PAD3], 0.0), writes=['w3_%d' % i])
            evc = [0]

            def evac(dst, pb, key, reads_extra=()):
                if evc[0] % 2 == 0:
                    P.op('act', lambda e: e.copy(dst, psb[pb][:]), reads=[PS(pb)], writes=[key])
                else:
                    P.op('dve', lambda e: e.tensor_copy(dst, psb[pb][:]), reads=[PS(pb)], writes=[key])
                evc[0] += 1
            pbc = [0]

            def nextpb():
                pb = pbc[0] % 6
                pbc[0] += 1
                return pb

            import os as _os
            for ch in ([7, 6, 5, 4, 3, 2, 1, 0] if _os.environ.get('S5REV') else range(8)):
                g0 = ch * 8
                P.dma('sp', u32, projT_d[ch * 128:(ch + 1) * 128, :], reads=['projT_d'], writes=['u32'])
                P.op('pool', lambda e: e.tensor_copy(ubf, u32), reads=['u32'], writes=['ubf'])
                P.dma('pool', Bw, braw_d[l, g0:g0 + 8].rearrange("g k m -> k g m"), writes=['Bw'])
                for n in range(22):
                    c1 = COL1[:, n, g0:g0 + 8].unsqueeze(2).to_broadcast([128, 8, 64])
                    c2 = COL2[:, n, g0:g0 + 8].unsqueeze(2).to_broadcast([128, 8, 64])
                    pib = PIp.unsqueeze(1).to_broadcast([128, 8, 64])
                    eng = 'dve' if n % 2 == 0 else 'pool'
                    P.op(eng, lambda e, n=n, c1=c1, pib=pib: e.tensor_tensor(Wd[:, n, :, 0:64], pib, c1, ALU.mult),
                         reads=colkeys + ['PIa', 'PIb'], writes=[('Wd', n, 0)])
                    P.op(eng, lambda e, n=n, c2=c2, pib=pib: e.tensor_tensor(Wd[:, n, :, 64:128], pib, c2, ALU.mult),
                         reads=colkeys + ['PIa', 'PIb'], writes=[('Wd', n, 1)])
                wdkeys = [('Wd', n, h) for n in range(22) for h in range(2)]
                for k in range(4):
                    for i in range(8):
                        P.op("pool", lambda e, k=k, i=i, g0=g0: e.tensor_copy(WcZ[:, k, i, 16 * i:16 * i + 16], WcF[:, k, g0 + i, :]),
                             reads=[('WcF', k)], writes=['WcZ'])
                for gi in range(8):
                    zb = gi % 2
                    for t4 in range(4):
                        pb = nextpb()
                        P.op('pe', lambda e, pb=pb, gi=gi, t4=t4: e.matmul(psb[pb][:], Bw[:, gi, :], ubf[:, t4 * 512:(t4 + 1) * 512],
                                                                          start=True, stop=True),
                             reads=['Bw', 'ubf'], writes=[PS(pb)])
                        evac(z0[zb][:, PAD0 + t4 * 512:PAD0 + (t4 + 1) * 512], pb, 'z0_%d' % zb)
                    for (lev, src, skey, pad_s, dst, dkey, pad_d, stride, nbase) in (
                            (1, z0[zb], 'z0_%d' % zb, PAD0, w1[zb], 'w1_%d' % zb, PAD1, 1, 0),
                            (2, w1[zb], 'w1_%d' % zb, PAD1, w2[zb], 'w2_%d' % zb, PAD2, 8, 7),
                            (3, w2[zb], 'w2_%d' % zb, PAD2, w3[gi], 'w3_%d' % gi, PAD3, 64, 14)):
                        for t4 in range(4):
                            pb = nextpb()

                            def mm(e, pb=pb, lev=lev, src=src, pad_s=pad_s, stride=stride, nbase=nbase, t4=t4, gi=gi):
                                ins = None
                                for k in range(8):
                                    if lev == 1:
                                        w = Wd[:, k, gi, :]
                                    elif k == 0:
                                        w = identb
                                    else:
                                        w = Wd[:, nbase + k, gi, :]
                                    o = pad_s + t4 * 512 - stride * k
                                    ins = e.matmul(psb[pb][:], w, src[:, o:o + 512], start=(k == 0), stop=(k == 7))
                                return ins
                            P.op('pe', mm, reads=wdkeys + ['identb', skey], writes=[PS(pb)])
                            evac(dst[:, pad_d + t4 * 512:pad_d + (t4 + 1) * 512], pb, dkey)
                for t4 in range(4):
                    pb = nextpb()

                    def mm4(e, pb=pb, t4=t4):
                        ins = None
                        for gi in range(8):
                            for k in range(4):
                                o = PAD3 + t4 * 512 - 512 * k
                                ins = e.matmul(psb[pb][:], WcZ[:, k, gi, :], w3[gi][:, o:o + 512],
                                               start=(gi == 0 and k == 0), stop=(gi == 7 and k == 3))
                        return ins
                    P.op('pe', mm4, reads=['WcZ'] + ['w3_%d' % i for i in range(8)], writes=[PS(pb)])
                    yt = ytmp[t4 % 2]
                    P.op('dve', lambda e, yt=yt, pb=pb, t4=t4, ch=ch: e.scalar_tensor_tensor(
                        yt, u32[:, t4 * 512:(t4 + 1) * 512], dsk[:, ch:ch + 1], psb[pb][:], ALU.mult, ALU.add),
                        reads=['u32', 'dsk', PS(pb)], writes=['ytmp%d' % (t4 % 2)])
                    gelu_tanh(C, gyo[t4 % 2], yt, 'ytmp%d' % (t4 % 2), 'gyo%d' % (t4 % 2))
                    P.dma('sp', gyT_d[ch * 128:(ch + 1) * 128, t4 * 512:(t4 + 1) * 512], gyo[t4 % 2],
                          reads=['gyo%d' % (t4 % 2)], writes=['gyT_d'])
            P.barrier()
            A.reset()
            gy = A.alloc([128, 8, S], BF16)
            for ch in range(8):
                P.dma('sp', gy[:, ch, :], gyT_d[ch * 128:(ch + 1) * 128, :], writes=[('gy', ch)])
            wg = A.alloc([128, 8, 2048], BF16)
            for ch in range(8):
                P.dma('pool', wg[:, ch, :], w_glu[l, ch * 128:(ch + 1) * 128, :], writes=[('wg', ch)])
            wgkeys = [('wg', ch) for ch in range(8)]
            gykeys = [('gy', ch) for ch in range(8)]
            sig = [A.alloc([128, 512]) for i in range(2)]
            yo = [A.alloc([128, 512], BF16) for i in range(2)]
            ci = 0
            for j in range(8):
                for t4 in range(4):
                    pa, pbb = (0, 1) if ci % 2 == 0 else (2, 3)
                    sb_i = ci % 2
                    ci += 1
                    for (pbk, col0) in ((pa, j * 128), (pbb, (8 + j) * 128)):
                        def mmg(e, pbk=pbk, col0=col0, t4=t4):
                            ins = None
                            for chh in range(8):
                                ins = e.matmul(psb[pbk][:], wg[:, chh, col0:col0 + 128], gy[:, chh, t4 * 512:(t4 + 1) * 512],
                                               start=(chh == 0), stop=(chh == 7))
                            return ins
                        P.op('pe', mmg, reads=wgkeys + gykeys, writes=[PS(pbk)])
                    P.op('act', lambda e, sb_i=sb_i, pbb=pbb: e.activation(sig[sb_i], psb[pbb][:], AF.Sigmoid),
                         reads=[PS(pbb)], writes=['sig%d' % sb_i])
                    P.op('dve', lambda e, sb_i=sb_i, pa=pa: e.tensor_tensor(yo[sb_i], psb[pa][:], sig[sb_i], ALU.mult),
                         reads=[PS(pa), 'sig%d' % sb_i], writes=['yo%d' % sb_i])
                    P.dma('sp', ycatT_d[j * 128:(j + 1) * 128, t4 * 512:(t4 + 1) * 512], yo[sb_i],
                          reads=['yo%d' % sb_i], writes=['ycatT_d'])
            if stop_after == 'C':
                return 'STOP'

            return None
        if phase_1() == 'STOP':
            return finish_debug(C)
        def phase_2(l=l, mo=mo):
            P.barrier()
            A.reset()
            NEG = -30000.0
            cw = A.alloc([128, 24, 4])
            P.dma('sp', cw, convw_d[l].rearrange("p (j k) -> p j k", k=4), writes=['cw'])
            normw_all = A.alloc([128, DEPTH])
            P.dma('sp', normw_all, normw_d, writes=['normw'])
            normw = normw_all[:, l:l + 1]
            hp = A.alloc([8, 2])
            P.dma('sp', hp, dnhp_d[:, 2 * l:2 * l + 2], writes=['hp'])
            maskL = A.alloc([64, 64])
            maskAT = A.alloc([64, 64])
            selrow = A.alloc([8, 8, 64])
            sel63 = A.alloc([64, 128])
            P.dma('sp', maskL, maskL_d, writes=['maskL'])
            P.dma('sp', maskAT, maskAT_d, writes=['maskAT'])
            P.dma('sp', selrow, selrow_d.rearrange("k (h m) -> k h m", h=8), writes=['selrow'])
            P.dma('sp', sel63, sel63_d, writes=['sel63'])
            ones1 = A.alloc([128, 128])
            P.op('pool', lambda e: e.memset(ones1, 1.0), writes=['ones1'])
            eps6 = A.alloc([128, 1])
            P.op('pool', lambda e: e.memset(eps6, 1e-6), writes=['eps6'])
            one_c = A.alloc([128, 1])
            P.op('pool', lambda e: e.memset(one_c, 1.0), writes=['one_c'])
            gcum = A.alloc([8, S])
            NCK = S // 64
            beta_tm = A.alloc([64, NCK, 8])
            gcum_tm = A.alloc([64, NCK, 8])
            egc_tm = A.alloc([64, NCK, 8])
            nbeta_tm = A.alloc([64, NCK, 8])
            bg_tm = A.alloc([64, NCK, 8])
            etail_tm = A.alloc([64, NCK, 8])
            eglast = A.alloc([128, NCK, 8])
            naexp = A.alloc([8, 1])
            dn_mark = A.mark()
            cmask = A.alloc([8, S])
            P.dma('sp', cmask, cmask_d, writes=['cmask'])
            Bt = A.alloc([8, S])
            At = A.alloc([8, S])
            P.dma('sp', Bt, projT_d[5120:5128, :], reads=['projT_d'], writes=['Bt'])
            P.dma('sp', At, projT_d[5128:5136, :], reads=['projT_d'], writes=['At'])
            P.op('act', lambda e: e.activation(Bt, Bt, AF.Sigmoid), reads=['Bt'], writes=['Bt'])
            P.op('act', lambda e: e.activation(naexp, hp[:, 0:1], AF.Exp), reads=['hp'], writes=['naexp'])
            P.op('dve', lambda e: e.tensor_scalar(naexp, naexp, -1.0, None, ALU.mult), reads=['naexp'], writes=['naexp'])
            P.op('act', lambda e: e.activation(At, At, AF.Exp, bias=hp[:, 1:2], scale=1.0), reads=['At', 'hp'], writes=['At'])
            P.op('act', lambda e: e.activation(At, At, AF.Ln, bias=one_c[0:8, :], scale=1.0), reads=['At', 'one_c'], writes=['At'])
            P.op('dve', lambda e: e.tensor_scalar(At, At, naexp[:, 0:1], None, ALU.mult), reads=['At', 'naexp'], writes=['At'])
            P.op('dve', lambda e: e.tensor_tensor_scan(gcum, cmask, At, 0.0, ALU.mult, ALU.add),
                 reads=['At', 'cmask'], writes=['gcum'])
            for (src_t, skey, dst_t, dkey, pb) in ((Bt, 'Bt', beta_tm, 'beta_tm', 0), (gcum, 'gcum', gcum_tm, 'gcum_tm', 1)):
                def trs(e, src_t=src_t, pb=pb):
                    ins = None
                    for ck in range(NCK):
                        ins = e.transpose(psb[pb][0:64, ck * 8:(ck + 1) * 8], src_t[0:8, ck * 64:(ck + 1) * 64], ident[0:8, 0:8])
                    return ins
                P.op('pe', trs, reads=[skey, 'ident'], writes=[PS(pb)])
                P.op('dve', lambda e, dst_t=dst_t, pb=pb: e.tensor_copy(dst_t.rearrange("p a b -> p (a b)"), psb[pb][0:64, 0:NCK * 8]),
                     reads=[PS(pb)], writes=[dkey])
            P.op('act', lambda e: e.activation(egc_tm, gcum_tm, AF.Exp), reads=['gcum_tm'], writes=['egc_tm'])
            P.op('dve', lambda e: e.tensor_scalar(nbeta_tm, beta_tm, -1.0, None, ALU.mult), reads=['beta_tm'], writes=['nbeta_tm'])
            P.op('dve', lambda e: e.tensor_tensor(bg_tm, beta_tm, egc_tm, ALU.mult), reads=['beta_tm', 'egc_tm'], writes=['bg_tm'])
            P.op('pe', lambda e: e.matmul(psb[2][:, 0:NCK * 8], sel63, gcum_tm.rearrange("p a b -> p (a b)"), start=True, stop=True),
                 reads=['sel63', 'gcum_tm'], writes=[PS(2)])
            P.op('dve', lambda e: e.tensor_tensor(etail_tm.rearrange("p a b -> p (a b)"), psb[2][0:64, 0:NCK * 8],
                                                  gcum_tm.rearrange("p a b -> p (a b)"), ALU.subtract),
                 reads=[PS(2), 'gcum_tm'], writes=['etail_tm'])
            P.op('act', lambda e: e.activation(etail_tm, etail_tm, AF.Exp), reads=['etail_tm'], writes=['etail_tm'])
            P.op('act', lambda e: e.activation(eglast.rearrange("p a b -> p (a b)"), psb[2][:, 0:NCK * 8], AF.Exp),
                 reads=[PS(2)], writes=['eglast'])
            scal_keys = ['beta_tm', 'gcum_tm', 'egc_tm', 'nbeta_tm', 'bg_tm', 'etail_tm', 'eglast']

            import os as _os
            DNS = _os.environ.get('DN_STOP', '')
            if DNS == 'gates':
                return 'STOP'
            for hg in range(2):
                P.barrier()
                A.release(dn_mark)
                HG = 4
                qT = A.alloc([128, HG, S])
                kT = A.alloc([128, HG, S])
                vT = A.alloc([128, HG, S])
                zs = A.alloc([128, HG, S], BF16)
                yT = A.alloc([128, HG, S], BF16)
                Sst = A.alloc([128, HG, 128])
                P.op('pool', lambda e: e.memset(Sst, 0.0), writes=['Sst'])
                conv_mark = A.mark()
                cin = [A.alloc([128, 3 + S]) for i in range(2)]
                cacc = [A.alloc([128, S]) for i in range(2)]
                csq = A.alloc([128, S])
                crs = A.alloc([128, S])
                for i in range(2):
                    P.op('pool', lambda e, i=i: e.memset(cin[i][:, 0:3], 0.0), writes=['cin%d' % i])
                ci = 0
                for hh in range(HG):
                    h = hg * HG + hh
                    for which, dstT in ((0, qT), (1, kT), (2, vT)):
                        jch = which * 8 + h
                        b = ci % 2
                        ci += 1
                        P.dma('sp', cin[b][:, 3:3 + S], projT_d[1024 + jch * 128:1024 + (jch + 1) * 128, :],
                              reads=['projT_d'], writes=['cin%d' % b])
                        acc = cacc[b]
                        ak = 'cacc%d' % b
                        eng = 'dve'
                        P.op(eng, lambda e, acc=acc, b=b, jch=jch: e.tensor_scalar(acc, cin[b][:, 3:3 + S], cw[:, jch, 3:4], None, ALU.mult),
                             reads=['cin%d' % b, 'cw'], writes=[ak])
                        for kk in range(3):
                            P.op(eng, lambda e, acc=acc, b=b, jch=jch, kk=kk: e.scalar_tensor_tensor(
                                acc, cin[b][:, kk:kk + S], cw[:, jch, kk:kk + 1], acc, ALU.mult, ALU.add),
                                reads=['cin%d' % b, 'cw', ak], writes=[ak])
                        if which == 2:
                            P.op('act', lambda e, acc=acc, hh=hh: e.activation(vT[:, hh, :], acc, AF.Silu),
                                 reads=[ak], writes=[('vT', hh)])
                            continue
                        P.op('act', lambda e, acc=acc: e.activation(acc, acc, AF.Silu), reads=[ak], writes=[ak])
                        P.op('act', lambda e, acc=acc: e.activation(csq, acc, AF.Square), reads=[ak], writes=['csq'])
                        for t4 in range(4):
                            pb = t4 % 2
                            P.op('pe', lambda e, pb=pb, t4=t4: e.matmul(psb[pb][:], ones1, csq[:, t4 * 512:(t4 + 1) * 512], start=True, stop=True),
                                 reads=['ones1', 'csq'], writes=[PS(pb)])
                            P.op('act', lambda e, pb=pb, t4=t4: e.activation(crs[:, t4 * 512:(t4 + 1) * 512], psb[pb][:], AF.Sqrt,
                                                                             bias=eps6, scale=1.0),
                                 reads=[PS(pb), 'eps6'], writes=['crs'])
                        P.op('dve', lambda e: e.reciprocal(crs, crs), reads=['crs'], writes=['crs'])
                        sc = (128.0 ** -0.5) if which == 0 else 1.0
                        P.op('dve', lambda e, acc=acc, dstT=dstT, hh=hh, sc=sc: e.scalar_tensor_tensor(
                            dstT[:, hh, :], acc, sc, crs, ALU.mult, ALU.mult),
                            reads=[ak, 'crs'], writes=[('qT' if which == 0 else 'kT', hh)])
                    b = ci % 2
                    ci += 1
                    P.dma('sp', cin[b][:, 3:3 + S], projT_d[4096 + h * 128:4096 + (h + 1) * 128, :],
                          reads=['projT_d'], writes=['cin%d' % b])
                    P.op('act', lambda e, b=b, hh=hh: e.activation(zs[:, hh, :], cin[b][:, 3:3 + S], AF.Silu),
                         reads=['cin%d' % b], writes=[('zs', hh)])
                qkeys = [('qT', hh) for hh in range(HG)]
                kkeys = [('kT', hh) for hh in range(HG)]
                vkeys = [('vT', hh) for hh in range(HG)]
                if DNS == 'conv':
                    return 'STOP'
                P.barrier()
                A.release(conv_mark)
                NB = 2
                ktm = [A.alloc([64, HG, 128]) for i in range(NB)]
                vb = [A.alloc([64, HG, 128]) for i in range(NB)]
                kbg = [A.alloc([64, HG, 128]) for i in range(NB)]
                ktail = [A.alloc([64, HG, 128]) for i in range(NB)]
                dL = [A.alloc([64, HG, 64]) for i in range(NB)]
                dAT = [A.alloc([64, HG, 64]) for i in range(NB)]
                Pm = [A.alloc([64, 2, HG, 64]) for i in range(NB)]
                XT = [A.alloc([64, HG, 64]) for i in range(NB)]
                AinT = [A.alloc([64, HG, 64]) for i in range(NB)]
                Wv = [A.alloc([64, HG, 128]) for i in range(NB)]
                KcT = [A.alloc([128, HG, 64]) for i in range(NB)]
                vnew = A.alloc([64, HG, 128])
                o1 = A.alloc([64, HG, 128])
                osb = A.alloc([64, HG, 128])
                osq = A.alloc([64, HG, 128])
                oss = A.alloc([64, HG])
                identH = A.alloc([64, HG, 64])
                for hh in range(HG):
                    P.op('dve', lambda e, hh=hh: e.tensor_copy(identH[:, hh, :], ident[0:64, 0:64]), reads=['ident'], writes=['identH'])
                ppb = [0]

                def prep_pb():
                    pb = 4 + (ppb[0] % 4)
                    ppb[0] += 1
                    return pb
                pending_rec = []
                for ck in range(NCK):
                    b = ck % NB
                    c0 = ck * 64
                    bk = lambda nm, b=b: '%s%d' % (nm, b)
                    P.rec = []
                    pk = prep_pb()

                    def trk(e, pk=pk, c0=c0):
                        ins = None
                        for hh in range(HG):
                            ins = e.transpose(psb[pk][0:64, hh * 128:(hh + 1) * 128], kT[:, hh, c0:c0 + 64], ident[:])
                        return ins
                    P.op('pe', trk, reads=kkeys + ['ident'], writes=[PS(pk)])
                    pv = prep_pb()

                    def trv(e, pv=pv, c0=c0):
                        ins = None
                        for hh in range(HG):
                            ins = e.transpose(psb[pv][0:64, hh * 128:(hh + 1) * 128], vT[:, hh, c0:c0 + 64], ident[:])
                        return ins
                    P.op('pe', trv, reads=vkeys + ['ident'], writes=[PS(pv)])
                    for hh in range(HG):
                        h = hg * HG + hh
                        P.op('dve', lambda e, b=b, hh=hh, h=h, pk=pk, ck=ck: e.tensor_scalar(
                            kbg[b][:, hh, :], psb[pk][0:64, hh * 128:(hh + 1) * 128], bg_tm[:, ck, h:h + 1], None, ALU.mult),
                            reads=[PS(pk)] + scal_keys, writes=[bk('kbg')])
                        P.op('dve', lambda e, b=b, hh=hh, h=h, pk=pk, ck=ck: e.tensor_scalar(
                            ktail[b][:, hh, :], psb[pk][0:64, hh * 128:(hh + 1) * 128], etail_tm[:, ck, h:h + 1], None, ALU.mult),
                            reads=[PS(pk)] + scal_keys, writes=[bk('ktail')])
                        P.op('act', lambda e, b=b, hh=hh, h=h, pv=pv, ck=ck: e.activation(
                            vb[b][:, hh, :], psb[pv][0:64, hh * 128:(hh + 1) * 128], AF.Copy, scale=beta_tm[:, ck, h:h + 1]),
                            reads=[PS(pv)] + scal_keys, writes=[bk('vb')])
                    pbc = prep_pb()

                    def mbc(e, pbc=pbc, c0=c0, hg=hg):
                        ins = None
                        for hh in range(HG):
                            h = hg * HG + hh
                            ins = e.matmul(psb[pbc][0:64, hh * 64:(hh + 1) * 64], selrow[:, h, :], gcum[:, c0:c0 + 64], start=True, stop=True)
                        return ins
                    P.op('pe', mbc, reads=['selrow', 'gcum'], writes=[PS(pbc)])
                    for hh in range(HG):
                        h = hg * HG + hh
                        P.op('dve', lambda e, b=b, hh=hh, h=h, pbc=pbc, ck=ck: e.tensor_scalar(
                            dL[b][:, hh, :], psb[pbc][0:64, hh * 64:(hh + 1) * 64], -1.0, gcum_tm[:, ck, h:h + 1], ALU.mult, ALU.add),
                            reads=[PS(pbc)] + scal_keys, writes=[bk('dL')])
                        P.op('dve', lambda e, b=b, hh=hh, h=h, pbc=pbc, ck=ck: e.tensor_scalar(
                            dAT[b][:, hh, :], psb[pbc][0:64, hh * 64:(hh + 1) * 64], gcum_tm[:, ck, h:h + 1], None, ALU.subtract),
                            reads=[PS(pbc)] + scal_keys, writes=[bk('dAT')])
                    mLb = maskL.unsqueeze(1).to_broadcast([64, HG, 64])
                    mAb = maskAT.unsqueeze(1).to_broadcast([64, HG, 64])
                    P.op('dve', lambda e, b=b, mLb=mLb: e.scalar_tensor_tensor(dL[b], dL[b], 0.0, mLb, ALU.min, ALU.add),
                         reads=[bk('dL'), 'maskL'], writes=[bk('dL')])
                    P.op('dve', lambda e, b=b, mAb=mAb: e.scalar_tensor_tensor(dAT[b], dAT[b], 0.0, mAb, ALU.min, ALU.add),
                         reads=[bk('dAT'), 'maskAT'], writes=[bk('dAT')])
                    P.op('act', lambda e, b=b: e.activation(dL[b], dL[b], AF.Exp), reads=[bk('dL')], writes=[bk('dL')])
                    P.op('act', lambda e, b=b: e.activation(dAT[b], dAT[b], AF.Exp), reads=[bk('dAT')], writes=[bk('dAT')])
                    pkk = prep_pb()

                    def mkk(e, pkk=pkk, c0=c0):
                        ins = None
                        for hh in range(HG):
                            ins = e.matmul(psb[pkk][0:64, hh * 64:(hh + 1) * 64], kT[:, hh, c0:c0 + 64], kT[:, hh, c0:c0 + 64], start=True, stop=True)
                        for hh in range(HG):
                            ins = e.matmul(psb[pkk][0:64, 256 + hh * 64:256 + (hh + 1) * 64], kT[:, hh, c0:c0 + 64], qT[:, hh, c0:c0 + 64],
                                           start=True, stop=True)
                        return ins
                    P.op('pe', mkk, reads=kkeys + qkeys, writes=[PS(pkk)])
                    for hh in range(HG):
                        h = hg * HG + hh
                        P.op('dve', lambda e, b=b, hh=hh, h=h, pkk=pkk, ck=ck: e.scalar_tensor_tensor(
                            Pm[b][:, 0, hh, :], psb[pkk][0:64, hh * 64:(hh + 1) * 64], nbeta_tm[:, ck, h:h + 1], dL[b][:, hh, :], ALU.mult, ALU.mult),
                            reads=[PS(pkk), bk('dL')] + scal_keys, writes=[bk('Pm')])
                    P.op('dve', lambda e, b=b, pkk=pkk: e.tensor_tensor(AinT[b].rearrange("p a b -> p (a b)"), psb[pkk][0:64, 256:512],
                                                                       dAT[b].rearrange("p a b -> p (a b)"), ALU.mult),
                         reads=[PS(pkk), bk('dAT')], writes=[bk('AinT')])
                    pq = prep_pb()

                    def trq(e, pq=pq, b=b):
                        ins = None
                        for hh in range(HG):
                            ins = e.transpose(psb[pq][0:64, hh * 64:(hh + 1) * 64], Pm[b][:, 0, hh, :], ident[0:64, 0:64])
                        return ins
                    P.op('pe', trq, reads=[bk('Pm'), 'ident'], writes=[PS(pq)])
                    P.op('act', lambda e, b=b, pq=pq: e.copy(Pm[b][:, 1].rearrange("p a b -> p (a b)"), psb[pq][0:64, 0:256]),
                         reads=[PS(pq)], writes=[bk('Pm')])
                    P.op('dve', lambda e, b=b: e.tensor_tensor(XT[b], Pm[b][:, 1], identH, ALU.add),
                         reads=[bk('Pm'), 'identH'], writes=[bk('XT')])
                    for lev in range(1, 6):
                        pp = prep_pb()
                        last = (lev == 5)

                        def msq(e, pp=pp, b=b, last=last):
                            ins = None
                            for hh in range(HG):
                                ins = e.matmul(psb[pp][0:64, hh * 64:(hh + 1) * 64], Pm[b][:, 1, hh, :], Pm[b][:, 0, hh, :], start=True, stop=True)
                            if not last:
                                for hh in range(HG):
                                    ins = e.matmul(psb[pp][0:64, 256 + hh * 64:256 + (hh + 1) * 64], Pm[b][:, 0, hh, :], Pm[b][:, 1, hh, :],
                                                   start=True, stop=True)
                            return ins
                        P.op('pe', msq, reads=[bk('Pm')], writes=[PS(pp)])
                        ncol = 256 if last else 512
                        P.op('act', lambda e, b=b, pp=pp, ncol=ncol: e.copy(Pm[b].rearrange("p a b c -> p (a b c)")[:, 0:ncol], psb[pp][0:64, 0:ncol]),
                             reads=[PS(pp)], writes=[bk('Pm')])
                        px = prep_pb()

                        def mx(e, px=px, b=b):
                            ins = None
                            for hh in range(HG):
                                ins = e.matmul(psb[px][0:64, hh * 64:(hh + 1) * 64], Pm[b][:, 0, hh, :], XT[b][:, hh, :], start=True, stop=True)
                            return ins
                        P.op('pe', mx, reads=[bk('Pm'), bk('XT')], writes=[PS(px)])
                        P.op('dve', lambda e, b=b, px=px: e.tensor_tensor(XT[b].rearrange("p a b -> p (a b)"), XT[b].rearrange("p a b -> p (a b)"),
                                                                         psb[px][0:64, 0:256], ALU.add),
                             reads=[PS(px), bk('XT')], writes=[bk('XT')])
                    pw = prep_pb()

                    def mwv(e, pw=pw, b=b):
                        ins = None
                        for hh in range(HG):
                            ins = e.matmul(psb[pw][0:64, hh * 128:(hh + 1) * 128], XT[b][:, hh, :], vb[b][:, hh, :], start=True, stop=True)
                        return ins
                    P.op('pe', mwv, reads=[bk('XT'), bk('vb')], writes=[PS(pw)])
                    P.op('act', lambda e, b=b, pw=pw: e.copy(Wv[b].rearrange("p a b -> p (a b)"), psb[pw][0:64, :]),
                         reads=[PS(pw)], writes=[bk('Wv')])
                    pc = prep_pb()

                    def mkc(e, pc=pc, b=b):
                        ins = None
                        for hh in range(HG):
                            ins = e.matmul(psb[pc][:, hh * 64:(hh + 1) * 64], kbg[b][:, hh, :], XT[b][:, hh, :], start=True, stop=True)
                        return ins
                    P.op('pe', mkc, reads=[bk('XT'), bk('kbg')], writes=[PS(pc)])
                    P.op('dve', lambda e, b=b, pc=pc: e.tensor_copy(KcT[b].rearrange("p a b -> p (a b)"), psb[pc][:, 0:256]),
                         reads=[PS(pc)], writes=[bk('KcT')])
                    prep_list = P.rec
                    P.rec = []
                    def r1(e, b=b):
                        ins = None
                        for hh in range(HG):
                            ins = e.matmul(psb[0][0:64, hh * 128:(hh + 1) * 128], KcT[b][:, hh, :], Sst[:, hh, :], start=True, stop=True)
                        return ins
                    P.op('pe', r1, reads=[bk('KcT'), 'Sst'], writes=[PS(0)])
                    P.op('dve', lambda e, b=b: e.tensor_tensor(vnew.rearrange("p a b -> p (a b)"), Wv[b].rearrange("p a b -> p (a b)"),
                                                               psb[0][0:64, :], ALU.subtract),
                         reads=[PS(0), bk('Wv')], writes=['vnew'])

                    def r2(e, b=b, c0=c0):
                        ins = None
                        for hh in range(HG):
                            ins = e.matmul(psb[1][0:64, hh * 128:(hh + 1) * 128], qT[:, hh, c0:c0 + 64], Sst[:, hh, :], start=True, stop=True)
                        for hh in range(HG):
                            ins = e.matmul(psb[2][0:64, hh * 128:(hh + 1) * 128], AinT[b][:, hh, :], vnew[:, hh, :], start=True, stop=True)
                        for hh in range(HG):
                            ins = e.matmul(psb[3][:, hh * 128:(hh + 1) * 128], ktail[b][:, hh, :], vnew[:, hh, :], start=True, stop=True)
                        return ins
                    P.op('pe', r2, reads=qkeys + ['Sst', bk('AinT'), 'vnew', bk('ktail')], writes=[PS(1), PS(2), PS(3)])
                    for hh in range(HG):
                        h = hg * HG + hh
                        P.op('act', lambda e, hh=hh, h=h, ck=ck: e.activation(o1[:, hh, :], psb[1][0:64, hh * 128:(hh + 1) * 128], AF.Copy,
                                                                             scale=egc_tm[:, ck, h:h + 1]),
                             reads=[PS(1)] + scal_keys, writes=['o1'])
                        P.op('dve', lambda e, hh=hh, h=h, ck=ck: e.scalar_tensor_tensor(
                            Sst[:, hh, :], Sst[:, hh, :], eglast[:, ck, h:h + 1], psb[3][:, hh * 128:(hh + 1) * 128], ALU.mult, ALU.add),
                            reads=[PS(3), 'Sst'] + scal_keys, writes=['Sst'])
                    P.op('dve', lambda e: e.tensor_tensor(osb.rearrange("p a b -> p (a b)"), o1.rearrange("p a b -> p (a b)"),
                                                          psb[2][0:64, :], ALU.add),
                         reads=[PS(2), 'o1'], writes=['osb'])
                    P.op('pool', lambda e: e.tensor_tensor(osq, osb, osb, ALU.mult), reads=['osb'], writes=['osq'])
                    P.op('dve', lambda e: e.tensor_reduce(oss, osq, mybir.AxisListType.X, ALU.add), reads=['osq'], writes=['oss'])
                    P.op('dve', lambda e: e.tensor_scalar(oss, oss, 1.0 / 128.0, 1e-6, ALU.mult, ALU.add), reads=['oss'], writes=['oss'])
                    P.op('act', lambda e: e.activation(oss, oss, AF.Sqrt), reads=['oss'], writes=['oss'])
                    P.op('dve', lambda e: e.reciprocal(oss, oss), reads=['oss'], writes=['oss'])
                    P.op('dve', lambda e: e.tensor_tensor(osb, osb, oss.unsqueeze(2).to_broadcast([64, HG, 128]), ALU.mult),
                         reads=['osb', 'oss'], writes=['osb'])
                    po = prep_pb()

                    def tro(e, po=po):
                        ins = None
                        for hh in range(HG):
                            ins = e.transpose(psb[po][:, hh * 64:(hh + 1) * 64], osb[:, hh, :], ident[0:64, 0:64])
                        return ins
                    P.op('pe', tro, reads=['osb', 'ident'], writes=[PS(po)])
                    P.op('dve', lambda e, po=po, c0=c0: e.scalar_tensor_tensor(
                        yT[:, :, c0:c0 + 64], psb[po][:, 0:256].rearrange("p (a b) -> p a b", a=HG), normw[:, 0:1], zs[:, :, c0:c0 + 64],
                        ALU.mult, ALU.mult),
                        reads=[PS(po), 'normw'] + [('zs', hh) for hh in range(HG)], writes=['yT'])
                    rec_list = P.rec
                    P.rec = None
                    P.replay_merged(prep_list, pending_rec, ratio=4)
                    pending_rec = rec_list
                P.replay_merged(pending_rec, [])
                for hh in range(HG):
                    h = hg * HG + hh
                    P.dma('sp', ycatT_d[1024 + h * 128:1024 + (h + 1) * 128, :], yT[:, hh, :], reads=['yT'], writes=['ycatT_d'])
            if stop_after == 'D':
                return 'STOP'

            return None
        if phase_2() == 'STOP':
            return finish_debug(C)
        def phase_3(l=l, mo=mo):
            P.barrier()
            A.reset()
            ycat = A.alloc([128, NCH, S], BF16)
            ycat_v = ycatT_d.rearrange("(k p) t -> p k t", p=128)
            for k in range(NCH):
                P.dma('sp', ycat[:, k, :], ycat_v[:, k, :], reads=['ycatT_d'], writes=[('ycat', k)])
            ykeys = [('ycat', k) for k in range(NCH)]
            if stop_after == 'E0':
                return 'STOP'
            wt = [A.alloc([128, NCH, 512], BF16) for i in range(2)]
            xb = [A.alloc([128, 512]) for i in range(3)]
            rb = [A.alloc([128, 512]) for i in range(3)]
            w_out_v = w_out[l].rearrange("(k p) n -> p k n", p=128)
            cnt = 0
            for ng in range(4):
                b = ng % 2
                if not _os.environ.get('E_NOPOOL'):
                    P.dma('pool', wt[b], w_out_v[:, :, ng * 512:(ng + 1) * 512], writes=['wt%d' % b])
                for nc_ in range(4):
                    dch = ng * 4 + nc_
                    for t4 in range(4):
                        pb = cnt % 4
                        xi = cnt % 3
                        cnt += 1

                        def mm(e, b=b, nc_=nc_, t4=t4, pb=pb):
                            ins = None
                            for k in range(NCH):
                                ins = e.matmul(psb[pb][:], wt[b][:, k, nc_ * 128:(nc_ + 1) * 128],
                                               ycat[:, k, t4 * 512:(t4 + 1) * 512], start=(k == 0), stop=(k == NCH - 1))
                            return ins
                        if not _os.environ.get('E_NOPE'):
                            P.op('pe', mm, reads=['wt%d' % b] + ykeys, writes=[PS(pb)])
                        if not _os.environ.get('E_NOX'):
                            P.dma(_os.environ.get('E_XQ', 'sp'), xb[xi], xT_d[dch * 128:(dch + 1) * 128, t4 * 512:(t4 + 1) * 512],
                                  reads=['xT_d'], writes=['xb%d' % xi])
                        if not _os.environ.get('E_NOACT'):
                            P.op('act', lambda e, xi=xi: e.activation(xb[xi], xb[xi], AF.Copy, scale=ALPHA),
                                 reads=['xb%d' % xi], writes=['xb%d' % xi])
                        gt1 = modT[:, mo + 32 + dch:mo + 32 + dch + 1]
                        if not _os.environ.get('E_NODVE'):
                            P.op('dve', lambda e, xi=xi, pb=pb, gt1=gt1: e.scalar_tensor_tensor(rb[xi], psb[pb][:], gt1, xb[xi], ALU.mult, ALU.add),
                                 reads=[PS(pb), 'xb%d' % xi, 'modT'], writes=['rb%d' % xi])
                        if not _os.environ.get('E_NOSTORE'):
                            P.dma('sp', rT_d[dch * 128:(dch + 1) * 128, t4 * 512:(t4 + 1) * 512], rb[xi],
                                  reads=['rb%d' % xi], writes=['rT_d'])
            if stop_after == 'E1':
                return 'STOP'
            P.barrier()
            A.reset()
            L = ln_alloc()
            x1t = [A.alloc([128, NCH, TB]) for i in range(2)]
            hft = [A.alloc([128, NCH, TB], BF16) for i in range(2)]
            rT_v = rT_d.rearrange("(j p) t -> p j t", p=128)
            hfT_v = hfT_d.rearrange("(j p) t -> p j t", p=128)
            lo = l * 64
            for tb in range(S // TB):
                b = tb % 2
                P.dma('sp', L.x[b], rT_v[:, :, tb * TB:(tb + 1) * TB], reads=['rT_d'], writes=['ln_x%d' % b])
                ln_block(L, L.x[b], 'ln_x%d' % b,
                         lambda j: lnp[:, lo + j:lo + j + 1],
                         lambda j: lnp[:, lo + 16 + j:lo + 16 + j + 1],
                         lambda j, b=b: x1t[b][:, j, :],
                         lambda j, b=b: 'x1t%d' % b)
                P.dma('sp', xT_v[:, :, tb * TB:(tb + 1) * TB], x1t[b], reads=['x1t%d' % b], writes=['xT_d'])
                ln_block(L, x1t[b], 'x1t%d' % b,
                         lambda j: modT[:, mo + 64 + j:mo + 64 + j + 1],
                         lambda j: modT[:, mo + 48 + j:mo + 48 + j + 1],
                         lambda j, b=b: hft[b][:, j, :],
                         lambda j, b=b: 'hft%d' % b)
                P.dma('sp', hfT_v[:, :, tb * TB:(tb + 1) * TB], hft[b], reads=['hft%d' % b], writes=['hfT_d'])
            if stop_after == 'E':
                return 'STOP'

            return None
        if phase_3() == 'STOP':
            return finish_debug(C)
        def phase_4(l=l, mo=mo):
            P.barrier()
            A.reset()
            hf = A.alloc([128, NCH, S], BF16)
            hfT_v = hfT_d.rearrange("(k p) t -> p k t", p=128)
            for k in range(NCH):
                P.dma('sp', hf[:, k, :], hfT_v[:, k, :], reads=['hfT_d'], writes=[('hf', k)])
            hkeys = [('hf', k) for k in range(NCH)]
            wt = [A.alloc([128, NCH, 512], BF16) for i in range(2)]
            stg = [A.alloc([128, 512]) for i in range(4)]
            wq_v = wq_d[l].rearrange("(k p) n -> p k n", p=128)
            cnt = 0
            for ng in range(4):
                b = ng % 2
                P.dma('pool', wt[b], wq_v[:, :, ng * 512:(ng + 1) * 512], writes=['wt%d' % b])
                for nc_ in range(4):
                    for t4 in range(4):
                        pb = cnt % 4
                        cnt += 1

                        def mm(e, b=b, nc_=nc_, t4=t4, pb=pb):
                            ins = None
                            for k in range(NCH):
                                ins = e.matmul(psb[pb][:], wt[b][:, k, nc_ * 128:(nc_ + 1) * 128],
                                               hf[:, k, t4 * 512:(t4 + 1) * 512], start=(k == 0), stop=(k == NCH - 1))
                            return ins
                        P.op('pe', mm, reads=['wt%d' % b] + hkeys, writes=[PS(pb)])
                        if pb % 2 == 0:
                            P.op('act', lambda e, pb=pb: e.copy(stg[pb], psb[pb][:]), reads=[PS(pb)], writes=['stg%d' % pb])
                        else:
                            P.op('dve', lambda e, pb=pb: e.tensor_copy(stg[pb], psb[pb][:]), reads=[PS(pb)], writes=['stg%d' % pb])
                        r0 = ng * 512 + nc_ * 128
                        P.dma('sp', qT_d[r0:r0 + 128, t4 * 512:(t4 + 1) * 512], stg[pb], reads=['stg%d' % pb], writes=['qT_d'])
            P.barrier()
            A.reset()
            NEGB = -1.0e30
            U32 = mybir.dt.uint32
            keysT = A.alloc([128, 16, 128])
            P.dma('sp', keysT, keysT_d[l].rearrange("p (c k) -> p c k", c=16), writes=['keysT'])
            iota = A.alloc([128, 128])
            P.dma('sp', iota, iota_d, writes=['iota'])
            qt = [A.alloc([128, 16, 128]) for i in range(2)]
            sc = A.alloc([128, 16, 128])
            sc2 = A.alloc([128, 16, 128])
            tv = A.alloc([128, 16, 16])
            tiu = A.alloc([128, 16, 16]).bitcast(U32)
            tif = A.alloc([128, 16, 16])
            cand = A.alloc([128, 8, 256])
            cand2 = A.alloc([128, 8, 256])
            bv = A.alloc([128, 8, 16])
            bpu = A.alloc([128, 8, 16]).bitcast(U32)
            rcu = A.alloc([128, 2, 8, 16]).bitcast(U32)
            rcf = A.alloc([128, 2, 8, 16])
            eq = A.alloc([128, 8, 16, 16])
            sel = A.alloc([128, 3, 8, 16])
            gz = A.alloc([128, 8])
            selT = A.alloc([128, 3, 128])
            OJ = A.alloc([128, 128, 128], BF16)
            OI = A.alloc([128, 128, 128], BF16)
            X = [A.alloc([128, 128, 128], BF16) for i in range(2)]
            qT_v = qT_d.rearrange("(c p) t -> p c t", p=128)
            Gd_v = Gd.rearrange("i j t -> j i t")
            evn = 0
            for tt in range(S // 128):
                b = tt % 2
                P.dma('sp', qt[b], qT_v[:, :, tt * 128:(tt + 1) * 128], reads=['qT_d'], writes=['qt%d' % b])
                for c4 in range(4):
                    def msc(e, b=b, c4=c4):
                        ins = None
                        for cc in range(4):
                            c = c4 * 4 + cc
                            ins = e.matmul(psb[c4][:, cc * 128:(cc + 1) * 128], qt[b][:, c, :], keysT[:, c, :], start=True, stop=True)
                        return ins
                    P.op('pe', msc, reads=['qt%d' % b, 'keysT'], writes=[PS(c4)])
                    if c4 % 2 == 0:
                        P.op('act', lambda e, c4=c4: e.copy(sc[:, c4 * 4:(c4 + 1) * 4, :].rearrange("p a b -> p (a b)"), psb[c4][:]),
                             reads=[PS(c4)], writes=[('sc', c4)])
                    else:
                        P.op('dve', lambda e, c4=c4: e.tensor_copy(sc[:, c4 * 4:(c4 + 1) * 4, :].rearrange("p a b -> p (a b)"), psb[c4][:]),
                             reads=[PS(c4)], writes=[('sc', c4)])
                chains = []
                for c in range(16):
                    P.rec = []
                    sk = ('sc', c // 4)
                    P.op('dve', lambda e, c=c: e.max(tv[:, c, 0:8], sc[:, c, :]), reads=[sk], writes=[('tv', c)])
                    P.op('dve', lambda e, c=c: e.max_index(tiu[:, c, 0:8], tv[:, c, 0:8], sc[:, c, :]), reads=[sk, ('tv', c)], writes=[('tiu', c)])
                    P.op('dve', lambda e, c=c: e.match_replace(sc2[:, c, :], tv[:, c, 0:8], sc[:, c, :], NEGB), reads=[sk, ('tv', c)], writes=[('sc2', c)])
                    P.op('dve', lambda e, c=c: e.max(tv[:, c, 8:16], sc2[:, c, :]), reads=[('sc2', c)], writes=[('tv', c)])
                    P.op('dve', lambda e, c=c: e.max_index(tiu[:, c, 8:16], tv[:, c, 8:16], sc2[:, c, :]), reads=[('sc2', c), ('tv', c)], writes=[('tiu', c)])
                    chains.append(P.rec)
                    P.rec = None
                for step in range(5):
                    for ch_ in chains:
                        P.replay(ch_[step])
                tvk = [('tv', c) for c in range(16)]
                tik = [('tiu', c) for c in range(16)]
                P.op('dve', lambda e: e.tensor_copy(tif, tiu), reads=tik, writes=['tif'])
                tv4 = tv.rearrange("p (h two) k -> p h two k", two=2)
                P.op('dve', lambda e, tv4=tv4: e.tensor_tensor(
                    cand.rearrange("p h (r c) -> p h r c", r=16),
                    tv4[:, :, 0, :].unsqueeze(3).to_broadcast([128, 8, 16, 16]),
                    tv4[:, :, 1, :].unsqueeze(2).to_broadcast([128, 8, 16, 16]), ALU.add),
                    reads=tvk, writes=['cand'])
                chains = []
                for h in range(8):
                    P.rec = []
                    P.op('dve', lambda e, h=h: e.max(bv[:, h, 0:8], cand[:, h, :]), reads=['cand'], writes=[('bv', h)])
                    P.op('dve', lambda e, h=h: e.max_index(bpu[:, h, 0:8], bv[:, h, 0:8], cand[:, h, :]), reads=['cand', ('bv', h)], writes=[('bpu', h)])
                    P.op('dve', lambda e, h=h: e.match_replace(cand2[:, h, :], bv[:, h, 0:8], cand[:, h, :], NEGB), reads=['cand', ('bv', h)], writes=[('cand2', h)])
                    P.op('dve', lambda e, h=h: e.max(bv[:, h, 8:16], cand2[:, h, :]), reads=[('cand2', h)], writes=[('bv', h)])
                    P.op('dve', lambda e, h=h: e.max_index(bpu[:, h, 8:16], bv[:, h, 8:16], cand2[:, h, :]), reads=[('cand2', h), ('bv', h)], writes=[('bpu', h)])
                    chains.append(P.rec)
                    P.rec = None
                for step in range(5):
                    for ch_ in chains:
                        P.replay(ch_[step])
                bvk = [('bv', h) for h in range(8)]
                bpk = [('bpu', h) for h in range(8)]
                P.op('dve', lambda e: e.tensor_scalar(rcu[:, 0], bpu, 4, None, ALU.logical_shift_right), reads=bpk, writes=['rcu0'])
                P.op('dve', lambda e: e.tensor_scalar(rcu[:, 1], bpu, 15, None, ALU.bitwise_and), reads=bpk, writes=['rcu1'])
                P.op('dve', lambda e: e.tensor_copy(rcf, rcu), reads=['rcu0', 'rcu1'], writes=['rcf'])
                tif4 = tif.rearrange("p (h two) k -> p h two k", two=2)
                io16 = iota[:, 0:16].unsqueeze(1).unsqueeze(1).to_broadcast([128, 8, 16, 16])
                for w in range(2):
                    P.op('dve', lambda e, w=w, io16=io16: e.tensor_tensor(eq, rcf[:, w].unsqueeze(3).to_broadcast([128, 8, 16, 16]), io16, ALU.is_equal),
                         reads=['rcf', 'iota'], writes=['eq'])
                    P.op('dve', lambda e, w=w, tif4=tif4: e.tensor_tensor(eq, eq, tif4[:, :, w, :].unsqueeze(2).to_broadcast([128, 8, 16, 16]), ALU.mult),
                         reads=['eq', 'tif'], writes=['eq'])
                    P.op('dve', lambda e, w=w: e.tensor_reduce(sel[:, w], eq, mybir.AxisListType.X, ALU.add), reads=['eq'], writes=[('sel', w)])
                P.op('dve', lambda e: e.tensor_tensor(sel[:, 2], bv, bv[:, :, 0:1].to_broadcast([128, 8, 16]), ALU.subtract),
                     reads=bvk, writes=[('sel', 2)])
                P.op('act', lambda e: e.activation(sel[:, 2], sel[:, 2], AF.Exp), reads=[('sel', 2)], writes=[('sel', 2)])
                P.op('dve', lambda e: e.tensor_reduce(gz, sel[:, 2], mybir.AxisListType.X, ALU.add), reads=[('sel', 2)], writes=['gz'])
                P.op('dve', lambda e: e.reciprocal(gz, gz), reads=['gz'], writes=['gz'])
                P.op('dve', lambda e: e.tensor_tensor(sel[:, 2], sel[:, 2], gz.unsqueeze(2).to_broadcast([128, 8, 16]), ALU.mult),
                     reads=[('sel', 2), 'gz'], writes=[('sel', 2)])
                def trs(e):
                    ins = None
                    for w in range(3):
                        ins = e.transpose(psb[4][:, w * 128:(w + 1) * 128], sel[:, w].rearrange("p h k -> p (h k)"), ident[:])
                    return ins
                P.op('pe', trs, reads=[('sel', 0), ('sel', 1), ('sel', 2), 'ident'], writes=[PS(4)])
                P.op('act', lambda e: e.copy(selT.rearrange("p a b -> p (a b)"), psb[4][:, 0:384]), reads=[PS(4)], writes=['selT'])
                iob = iota.unsqueeze(1).to_broadcast([128, 128, 128])
                P.op('dve', lambda e, iob=iob: e.tensor_tensor(OJ, iob, selT[:, 1, :].unsqueeze(2).to_broadcast([128, 128, 128]), ALU.is_equal),
                     reads=['selT', 'iota'], writes=['OJ'])
                P.op('dve', lambda e, iob=iob: e.tensor_tensor(OI, iob, selT[:, 0, :].unsqueeze(2).to_broadcast([128, 128, 128]), ALU.is_equal),
                     reads=['selT', 'iota'], writes=['OI'])
                P.op('pool', lambda e: e.tensor_tensor(OI, OI, selT[:, 2, :].unsqueeze(2).to_broadcast([128, 128, 128]), ALU.mult),
                     reads=['selT', 'OI'], writes=['OI'])
                Xv = X[b].rearrange("p i t -> p t i")
                for t4 in range(32):
                    pb = 5 + (t4 % 3)

                    def mg(e, pb=pb, t4=t4):
                        ins = None
                        for q_ in range(4):
                            t = t4 * 4 + q_
                            ins = e.matmul(psb[pb][:, q_ * 128:(q_ + 1) * 128], OJ[:, t, :], OI[:, t, :], start=True, stop=True)
                        return ins
                    P.op('pe', mg, reads=['OJ', 'OI'], writes=[PS(pb)])
                    src_ps = psb[pb][:].rearrange("p (a b) -> p a b", a=4)
                    dst = Xv[:, t4 * 4:(t4 + 1) * 4, :]
                    if evn % 4 != 3:
                        P.op('act', lambda e, dst=dst, src_ps=src_ps: e.copy(dst, src_ps), reads=[PS(pb)], writes=[('X', b, t4)])
                    else:
                        P.op('dve', lambda e, dst=dst, src_ps=src_ps: e.tensor_copy(dst, src_ps), reads=[PS(pb)], writes=[('X', b, t4)])
                    evn += 1
                xkeys = [('X', b, t4) for t4 in range(32)]
                for iq in range(4):
                    P.dma('act', Gd_v[:, iq * 32:(iq + 1) * 32, tt * 128:(tt + 1) * 128], X[b][:, iq * 32:(iq + 1) * 32, :],
                          reads=xkeys, writes=['Gd'])
            if stop_after == 'F3':
                return 'STOP'
            P.barrier()
            A.reset()
            TP = 1024
            hfb = A.alloc([128, NCH, TP], BF16)
            acc = A.alloc([128, NCH, TP])
            ut = [A.alloc([128, NCH, 256], BF16) for i in range(2)]
            vt = [A.alloc([128, 16, 256], BF16) for i in range(2)]
            PT = A.alloc([128, 16, TP], BF16)
            gl = [A.alloc([128, TP], BF16) for i in range(2)]
            gt_ = [A.alloc([128, TP], BF16) for i in range(4)]
            xb = [A.alloc([128, TP]) for i in range(2)]
            rb = [A.alloc([128, TP]) for i in range(2)]
            uT_v = uT_d[l].rearrange("(k p) e -> p k e", p=128)
            v_v = v_d[l].rearrange("(g ic j) d -> g j ic d", ic=16, j=128)
            ui = 0
            vi = 0
            gi_ = 0
            for ps_ in range(S // TP):
                t0 = ps_ * TP
                for k in range(NCH):
                    P.dma('sp', hfb[:, k, :], hfT_v[:, k, t0:t0 + TP], reads=['hfT_d'], writes=[('hfb', k)])
                hbk = [('hfb', k) for k in range(NCH)]
                for eg in range(8):
                    for ib in range(8):
                        ub = ui % 2
                        ui += 1
                        e0 = eg * 2048 + ib * 256
                        P.dma('pool', ut[ub], uT_v[:, :, e0:e0 + 256], writes=['ut%d' % ub])
                        for i2 in range(2):
                            ic = ib * 2 + i2
                            i_abs = eg * 16 + ic
                            gb = gi_ % 4
                            g2 = gi_ % 2
                            gi_ += 1
                            P.dma('act', gt_[gb], Gd[i_abs, :, t0:t0 + TP], reads=['Gd'], writes=['gt%d' % gb])
                            for th in range(2):
                                pb = 2 * g2 + th

                                def ms(e, ub=ub, i2=i2, pb=pb, th=th):
                                    ins = None
                                    for k in range(NCH):
                                        ins = e.matmul(psb[pb][:], ut[ub][:, k, i2 * 128:(i2 + 1) * 128],
                                                       hfb[:, k, th * 512:(th + 1) * 512],
                                                       start=(k == 0), stop=(k == NCH - 1))
                                    return ins
                                P.op('pe', ms, reads=['ut%d' % ub] + hbk, writes=[PS(pb)])
                                P.op('act', lambda e, g2=g2, pb=pb, th=th: e.activation(
                                    gl[g2][:, th * 512:(th + 1) * 512], psb[pb][:], AF.Gelu_apprx_tanh),
                                    reads=[PS(pb)], writes=[('gl', g2, th)])
                                P.op('dve', lambda e, g2=g2, gb=gb, ic=ic, th=th: e.tensor_tensor(
                                    PT[:, ic, th * 512:(th + 1) * 512], gl[g2][:, th * 512:(th + 1) * 512],
                                    gt_[gb][:, th * 512:(th + 1) * 512], ALU.mult),
                                    reads=[('gl', g2, th), 'gt%d' % gb], writes=[('PT', ic, th)])
                    ptk = [('PT', ic, th) for ic in range(16) for th in range(2)]
                    for dq in range(8):
                        vb_ = vi % 2
                        vi += 1
                        P.dma('pool', vt[vb_], v_v[eg][:, :, dq * 256:(dq + 1) * 256], writes=['vt%d' % vb_])
                        for d2 in range(2):
                            dch = dq * 2 + d2
                            for th in range(2):
                                pb = 4 + 2 * (dch % 2) + th

                                def mv(e, vb_=vb_, d2=d2, pb=pb, th=th):
                                    ins = None
                                    for ic in range(16):
                                        ins = e.matmul(psb[pb][:], vt[vb_][:, ic, d2 * 128:(d2 + 1) * 128],
                                                       PT[:, ic, th * 512:(th + 1) * 512],
                                                       start=(ic == 0), stop=(ic == 15))
                                    return ins
                                P.op('pe', mv, reads=['vt%d' % vb_] + ptk, writes=[PS(pb)])
                                dsl = acc[:, dch, th * 512:(th + 1) * 512]
                                if eg == 0:
                                    P.op('act', lambda e, dsl=dsl, pb=pb: e.copy(dsl, psb[pb][:]),
                                         reads=[PS(pb)], writes=[('acc', dch, th)])
                                else:
                                    P.op('dve', lambda e, dsl=dsl, pb=pb: e.tensor_tensor(dsl, dsl, psb[pb][:], ALU.add),
                                         reads=[PS(pb), ('acc', dch, th)], writes=[('acc', dch, th)])
                for dch in range(NCH):
                    xi = dch % 2
                    P.dma('sp', xb[xi], xT_d[dch * 128:(dch + 1) * 128, t0:t0 + TP], reads=['xT_d'], writes=['xb%d' % xi])
                    P.op('act', lambda e, xi=xi: e.activation(xb[xi], xb[xi], AF.Copy, scale=ALPHA), reads=['xb%d' % xi], writes=['xb%d' % xi])
                    gt2 = modT[:, mo + 80 + dch:mo + 80 + dch + 1]
                    P.op('dve', lambda e, xi=xi, dch=dch, gt2=gt2: e.scalar_tensor_tensor(rb[xi], acc[:, dch, :], gt2, xb[xi], ALU.mult, ALU.add),
                         reads=[('acc', dch, 0), ('acc', dch, 1), 'xb%d' % xi, 'modT'], writes=['rb%d' % xi])
                    P.dma('sp', rT_d[dch * 128:(dch + 1) * 128, t0:t0 + TP], rb[xi], reads=['rb%d' % xi], writes=['rT_d'])
            if stop_after == 'F4':
                return 'STOP'
            P.barrier()
            A.reset()
            L = ln_alloc()
            x2t = [A.alloc([128, NCH, TB]) for i in range(2)]
            rT_v = rT_d.rearrange("(j p) t -> p j t", p=128)
            lo = l * 64 + 32
            for tb in range(S // TB):
                b = tb % 2
                P.dma('sp', L.x[b], rT_v[:, :, tb * TB:(tb + 1) * TB], reads=['rT_d'], writes=['ln_x%d' % b])
                ln_block(L, L.x[b], 'ln_x%d' % b,
                         lambda j: lnp[:, lo + j:lo + j + 1],
                         lambda j: lnp[:, lo + 16 + j:lo + 16 + j + 1],
                         lambda j, b=b: x2t[b][:, j, :],
                         lambda j, b=b: 'x2t%d' % b)
                P.dma('sp', xT_v[:, :, tb * TB:(tb + 1) * TB], x2t[b], reads=['x2t%d' % b], writes=['xT_d'])
            if stop_after == 'G':
                return 'STOP'

            return None
        if phase_4() == 'STOP':
            return finish_debug(C)
    def phase_z():
        P.barrier()
        A.reset()
        xf = [A.alloc([128, NCH, 128]) for i in range(2)]
        xo = [A.alloc([128, D]) for i in range(2)]
        ev = 0
        for tt in range(S // 128):
            b = tt % 2
            P.dma('sp', xf[b], xT_v[:, :, tt * 128:(tt + 1) * 128], reads=['xT_d'], writes=['xf%d' % b])
            for g in range(4):
                pb = g % 2

                def tr(e, b=b, g=g, pb=pb):
                    ins = None
                    for jj in range(4):
                        j = g * 4 + jj
                        ins = e.transpose(psb[pb][:, jj * 128:(jj + 1) * 128], xf[b][:, j, :], ident[:])
                    return ins
                P.op('pe', tr, reads=['xf%d' % b, 'ident'], writes=[PS(pb)])
                dst = xo[b][:, g * 512:(g + 1) * 512]
                if ev % 2 == 0:
                    P.op('act', lambda e, dst=dst, pb=pb: e.copy(dst, psb[pb][:]), reads=[PS(pb)], writes=[('xo', b, g)])
                else:
                    P.op('dve', lambda e, dst=dst, pb=pb: e.tensor_copy(dst, psb[pb][:]), reads=[PS(pb)], writes=[('xo', b, g)])
                ev += 1
            P.dma('sp', out_d[tt * 128:(tt + 1) * 128, :], xo[b], reads=[('xo', b, g) for g in range(4)], writes=['out'])
    phase_z()
    return finish_debug(C)


def finish_debug(C):
    C.P.wait_all('sp', C.dram_written)
    C.P.finish()
    return C.nc


def make_in_maps(inputs, n_cores=8):
    f32 = np.float32
    maps = []
    ident = np.eye(128, dtype=f32)
    b_ada_col = np.ascontiguousarray(
        np.asarray(inputs['b_ada'], f32).reshape(DEPTH, 96, 128).transpose(2, 0, 1).reshape(128, DEPTH * 96))
    lam_re = np.asarray(inputs['ssm_lam_re'], f32)
    lam_im = np.asarray(inputs['ssm_lam_im'], f32)
    lstep = np.asarray(inputs['ssm_log_step'], f32)
    s5A = np.zeros((DEPTH, 128, 3, 64), f32)
    for l in range(DEPTH):
        for h in range(2):
            s5A[l, h * 64:(h + 1) * 64, 0, :] = lam_re[l].T
            s5A[l, h * 64:(h + 1) * 64, 1, :] = lam_im[l].T
        s5A[l, :, 2, :] = lstep[l][None, :]
    s5A = s5A.reshape(DEPTH, 128, 192)
    b_re = np.asarray(inputs['ssm_b_re'], f32)
    b_im = np.asarray(inputs['ssm_b_im'], f32)
    braw = np.zeros((DEPTH, 64, 128, 128), f32)
    for g in range(64):
        r0 = 16 * (g % 8)
        braw[:, g, r0:r0 + 16, 0:64] = b_re[:, g].transpose(0, 2, 1)
        braw[:, g, r0:r0 + 16, 64:128] = b_im[:, g].transpose(0, 2, 1)
    c_re = np.asarray(inputs['ssm_c_re'], f32)
    c_im = np.asarray(inputs['ssm_c_im'], f32)
    crci = np.zeros((DEPTH, 128, 2, 64, 16), f32)
    for h in range(2):
        crci[:, h * 64:(h + 1) * 64, 0] = c_re.transpose(0, 3, 1, 2)
        crci[:, h * 64:(h + 1) * 64, 1] = c_im.transpose(0, 3, 1, 2)
    crci = crci.reshape(DEPTH, 128, 2048)
    dskip_col = np.ascontiguousarray(
        np.asarray(inputs['ssm_d'], f32).reshape(DEPTH, 8, 128).transpose(2, 0, 1).reshape(128, DEPTH * 8))
    sk = np.asarray(inputs['peer_sub_keys'], f32)
    keysT = np.ascontiguousarray(sk.reshape(DEPTH, 16, 128, 128).transpose(0, 3, 1, 2).reshape(DEPTH, 128, 2048))
    peer_uT = np.ascontiguousarray(np.asarray(inputs['peer_u'], f32).transpose(0, 2, 1))
    iota128 = np.tile(np.arange(128, dtype=f32)[None, :], (128, 1))
    lnp_col = np.zeros((128, DEPTH, 4, 16), f32)
    for i_, nm in enumerate(['ln1_g', 'ln1_b', 'ln2_g', 'ln2_b']):
        lnp_col[:, :, i_, :] = np.asarray(inputs[nm], f32).reshape(DEPTH, 16, 128).transpose(2, 0, 1)
    lnp_col = lnp_col.reshape(128, DEPTH * 64)
    cwv = np.asarray(inputs['dn_conv_w'], f32)
    convw_col = np.ascontiguousarray(cwv.reshape(DEPTH, 4, 24, 128).transpose(0, 3, 2, 1).reshape(DEPTH, 128, 96))
    dnhp = np.zeros((8, DEPTH * 2), f32)
    dnhp[:, 0::2] = np.asarray(inputs['dn_a_log'], f32).T
    dnhp[:, 1::2] = np.asarray(inputs['dn_dt_bias'], f32).T
    normw_col = np.ascontiguousarray(np.asarray(inputs['dn_norm_w'], f32).T)
    ii = np.arange(64)
    NEG = -30000.0
    maskL = np.where(ii[None, :] < ii[:, None], 0.0, NEG).astype(f32)
    maskAT = np.where(ii[:, None] <= ii[None, :], 0.0, NEG).astype(f32)
    selrow = np.zeros((8, 8, 64), f32)
    for h in range(8):
        selrow[h, h, :] = 1.0
    selrow = selrow.reshape(8, 512)
    sel63 = np.zeros((64, 128), f32)
    sel63[63, :] = 1.0
    cmask = np.ones((8, S), f32)
    cmask[:, 0::64] = 0.0
    for c in range(n_cores):
        b = c % 4
        m = {
            'x': np.ascontiguousarray(np.asarray(inputs['x'][b], f32)),
            'c_col': np.ascontiguousarray(np.asarray(inputs['c'][b], f32).reshape(NCH, 128).T),
            'w_ada': np.asarray(inputs['w_ada'], f32),
            'b_ada_col': b_ada_col,
            'w_in': np.asarray(inputs['w_in'], f32),
            'ident': ident,
            's5A': s5A, 'braw': braw, 'crci': crci, 'dskip_col': dskip_col,
            'ssm_w_glu': np.asarray(inputs['ssm_w_glu'], f32),
            'convw_col': convw_col, 'dnhp': dnhp, 'normw_col': normw_col, 'maskL': maskL, 'maskAT': maskAT,
            'selrow': selrow, 'sel63': sel63, 'cmask': cmask,
            'w_out': np.asarray(inputs['w_out'], f32), 'lnp_col': lnp_col,
            'peer_w_query': np.asarray(inputs['peer_w_query'], f32), 'keysT': keysT,
            'peer_uT': peer_uT, 'peer_v': np.asarray(inputs['peer_v'], f32), 'iota128': iota128,
        }
        maps.append(m)
    return maps


def kernel(**inputs):
    nc = build()
    in_maps = make_in_maps(inputs)
    res = run_bass_kernel_spmd(nc, in_maps, core_ids=list(range(8)))
    out = np.stack([np.asarray(res.results[b]['out']) for b in range(4)], axis=0)
    return out.astype(np.float32)
```
